# Optimizing a Trainium2 kernel written in Bass

```python
import math
import jax, jax.numpy as jnp
from jax import lax
import numpy as np

D_MODEL = 1024
BATCH = 8
SEQ = 4096
DEPTH = 1

D_FF = 2816
DN_HEADS = 8
DN_DK = 128
DN_DV = 128
CONV_K = 4
CHUNK = 64
SA_HEADS = 8
SA_DH = 128
Q_RANK = 256
KV_RANK = 256
IDX_HEADS = 8
IDX_DIM = 64
TOPK_MAX = 256
Q_BLOCK = 128
REL_BUCKETS = 32
REL_MAX_DIST = 128
EPS = 1e-6

DN_QK_W = DN_HEADS * DN_DK
DN_V_W = DN_HEADS * DN_DV
SA_W = SA_HEADS * SA_DH
IN_SPLITS = (2 * DN_QK_W + DN_V_W,
             DN_V_W,
             DN_HEADS,
             DN_HEADS,
             Q_RANK,
             KV_RANK,
             IDX_DIM,
             IDX_HEADS,
             D_MODEL,
             D_MODEL)
IN_WIDTH = sum(IN_SPLITS)

kernel_name = "hybrid_gdn_dsa_macaron_block"


def rmsnorm(x, g):
    xf = x.astype(jnp.float32)
    y = xf * lax.rsqrt(jnp.mean(xf * xf, axis=-1, keepdims=True) + EPS)
    return (y * g.astype(jnp.float32)).astype(x.dtype)


def layernorm(x, g, b):
    xf = x.astype(jnp.float32)
    mu = jnp.mean(xf, axis=-1, keepdims=True)
    var = jnp.mean(jnp.square(xf - mu), axis=-1, keepdims=True)
    y = (xf - mu) * lax.rsqrt(var + EPS)
    return (y * g.astype(jnp.float32) + b.astype(jnp.float32)).astype(x.dtype)


def l2norm(t):
    return t * lax.rsqrt(jnp.sum(t * t, axis=-1, keepdims=True) + EPS)


def swiglu(h, wg, wu, wd):
    return (jax.nn.silu(h @ wg) * (h @ wu)) @ wd


def split_cols(t, sizes):
    out, o = [], 0
    for s in sizes:
        out.append(t[..., o:o + s])
        o += s
    return out


def causal_dwconv(x, w):
    return lax.conv_general_dilated(
        x, w[:, None, :].astype(x.dtype), window_strides=(1,),
        padding=[(CONV_K - 1, 0)], dimension_numbers=('NWC', 'WIO', 'NWC'),
        feature_group_count=x.shape[-1])


def t5_bucket(n):
    max_exact = REL_BUCKETS // 2
    nf = jnp.maximum(n, 1).astype(jnp.float32)
    large = max_exact + (jnp.log(nf / max_exact) / math.log(REL_MAX_DIST / max_exact)
                         * (REL_BUCKETS - max_exact)).astype(jnp.int32)
    large = jnp.minimum(large, REL_BUCKETS - 1)
    return jnp.where(n < max_exact, n, large)


def gated_delta_rule(q, k, v, g, beta):
    B, S, H, dk = q.shape
    dv = v.shape[-1]
    nc = S // CHUNK

    def chunk4(t):
        return t.reshape(B, nc, CHUNK, H, t.shape[-1]).transpose(1, 0, 3, 2, 4)

    def chunk3(t):
        return t.reshape(B, nc, CHUNK, H).transpose(1, 0, 3, 2)

    q_c, k_c, v_c = chunk4(q), chunk4(k), chunk4(v)
    b_c = chunk3(beta)
    g_cum = jnp.cumsum(chunk3(g), axis=-1)
    pos = jnp.arange(CHUNK)
    tril = pos[:, None] >= pos[None, :]
    strict = pos[:, None] > pos[None, :]
    decay = jnp.exp(jnp.where(tril, g_cum[..., :, None] - g_cum[..., None, :], -jnp.inf))
    k_beta = k_c * b_c[..., None]
    v_beta = v_c * b_c[..., None]
    L = jnp.where(strict, jnp.einsum('nbhid,nbhjd->nbhij', k_beta, k_c) * decay, 0.0)
    eye = jnp.eye(CHUNK, dtype=L.dtype)
    T = lax.linalg.triangular_solve(eye + L, jnp.broadcast_to(eye, L.shape),
                                    left_side=True, lower=True)
    u = T @ v_beta
    w = T @ (k_beta * jnp.exp(g_cum)[..., None])
    intra = jnp.einsum('nbhid,nbhjd->nbhij', q_c, k_c) * decay

    def step(state, inp):
        q_i, k_i, u_i, w_i, a_i, gc_i = inp
        v_new = u_i - w_i @ state
        o_i = (q_i * jnp.exp(gc_i)[..., None]) @ state + a_i @ v_new
        g_last = gc_i[..., -1]
        state = state * jnp.exp(g_last)[..., None, None] + jnp.einsum(
            'bhcd,bhce->bhde', k_i * jnp.exp(g_last[..., None] - gc_i)[..., None], v_new)
        return state, o_i

    s0 = jnp.zeros((B, H, dk, dv), jnp.float32)
    _, o = lax.scan(step, s0, (q_c, k_c, u, w, intra, g_cum))
    return o.transpose(1, 0, 3, 2, 4).reshape(B, S, H, dv)


def dsa_attention(cq, ckv, k_idx, w_idx, w_uq, w_uk, w_uv, w_iq, rel_bias):
    B, S, _ = cq.shape
    nb = S // Q_BLOCK
    topk = min(TOPK_MAX, S // 4)
    key_pos = jnp.arange(S, dtype=jnp.int32)

    def blocks(t):
        return t.reshape(B, nb, Q_BLOCK, t.shape[-1]).transpose(1, 0, 2, 3)

    def attend_block(inp):
        cq_b, w_b, t0 = inp
        t = t0 + jnp.arange(Q_BLOCK, dtype=jnp.int32)
        q_idx = (cq_b @ w_iq).reshape(B, Q_BLOCK, IDX_HEADS, IDX_DIM)
        dots = jax.nn.relu(jnp.einsum('bqhd,bsd->bqhs', q_idx, k_idx))
        score = jnp.einsum('bqhs,bqh->bqs', dots, w_b).astype(jnp.float32)
        score = jnp.where(key_pos[None, None, :] <= t[None, :, None], score, -jnp.inf)
        _, sel = lax.top_k(score, topk)
        valid = sel <= t[None, :, None]
        c_sel = jax.vmap(lambda c, i: c[i])(ckv, sel)
        q = (cq_b @ w_uq).reshape(B, Q_BLOCK, SA_HEADS, SA_DH)
        q_lat = jnp.einsum('bqhd,rhd->bqhr', q, w_uk)
        logits = jnp.einsum('bqhr,bqkr->bqhk', q_lat, c_sel).astype(jnp.float32) * (SA_DH ** -0.5)
        bias = rel_bias[t5_bucket(jnp.maximum(t[None, :, None] - sel, 0))]
        logits = logits + jnp.moveaxis(bias, -1, 2).astype(jnp.float32)
        logits = jnp.where(valid[:, :, None, :], logits, -jnp.inf)
        p = jax.nn.softmax(logits, axis=-1).astype(c_sel.dtype)
        o_lat = jnp.einsum('bqhk,bqkr->bqhr', p, c_sel)
        o = jnp.einsum('bqhr,rhd->bqhd', o_lat, w_uv)
        return o.reshape(B, Q_BLOCK, SA_W)

    starts = jnp.arange(nb, dtype=jnp.int32) * Q_BLOCK
    out = lax.map(attend_block, (blocks(cq), blocks(w_idx), starts))
    return out.transpose(1, 0, 2, 3).reshape(B, S, SA_W)


def setup_inputs(seed: int = 0) -> dict:
    key = jax.random.key(seed)
    ks = iter(jax.random.split(key, 32))
    f32 = jnp.float32

    def nrm(shape, scale):
        return jax.random.normal(next(ks), shape, f32) * scale

    def gain(shape):
        return 1.0 + 0.01 * jax.random.normal(next(ks), shape, f32)

    x = jax.random.normal(next(ks), (BATCH, SEQ, D_MODEL), f32)
    a_log = jnp.log(jax.random.uniform(next(ks), (DEPTH, DN_HEADS), f32, 1.0, 16.0))
    dt = jnp.exp(jax.random.uniform(next(ks), (DEPTH, DN_HEADS), f32,
                                    math.log(1e-3), math.log(1e-1)))
    dt_bias = dt + jnp.log(-jnp.expm1(-dt))
    return {
        "x": x,
        "ffn1_norm": gain((DEPTH, D_MODEL)),
        "ffn1_wg": nrm((DEPTH, D_MODEL, D_FF), D_MODEL ** -0.5),
        "ffn1_wu": nrm((DEPTH, D_MODEL, D_FF), D_MODEL ** -0.5),
        "ffn1_wd": nrm((DEPTH, D_FF, D_MODEL), D_FF ** -0.5),
        "mix_norm": gain((DEPTH, D_MODEL)),
        "w_in": nrm((DEPTH, D_MODEL, IN_WIDTH), D_MODEL ** -0.5),
        "conv_w": nrm((DEPTH, CONV_K, 2 * DN_QK_W + DN_V_W), CONV_K ** -0.5),
        "a_log": a_log,
        "dt_bias": dt_bias,
        "dn_out_norm": gain((DEPTH, DN_DV)),
        "q_norm": gain((DEPTH, Q_RANK)),
        "kv_norm": gain((DEPTH, KV_RANK)),
        "w_uq": nrm((DEPTH, Q_RANK, SA_W), Q_RANK ** -0.5),
        "w_uk": nrm((DEPTH, KV_RANK, SA_HEADS, SA_DH), KV_RANK ** -0.5),
        "w_uv": nrm((DEPTH, KV_RANK, SA_HEADS, SA_DH), KV_RANK ** -0.5),
        "w_iq": nrm((DEPTH, Q_RANK, IDX_HEADS * IDX_DIM), Q_RANK ** -0.5),
        "idx_k_g": gain((DEPTH, IDX_DIM)),
        "idx_k_b": nrm((DEPTH, IDX_DIM), 0.01),
        "rel_bias": nrm((REL_BUCKETS, SA_HEADS), 0.2),
        "w_o": nrm((DEPTH, D_MODEL, D_MODEL), D_MODEL ** -0.5),
        "ffn2_norm": gain((DEPTH, D_MODEL)),
        "ffn2_wg": nrm((DEPTH, D_MODEL, D_FF), D_MODEL ** -0.5),
        "ffn2_wu": nrm((DEPTH, D_MODEL, D_FF), D_MODEL ** -0.5),
        "ffn2_wd": nrm((DEPTH, D_FF, D_MODEL), D_FF ** -0.5),
        "final_norm": gain((D_MODEL,)),
    }


def reference(x, ffn1_norm, ffn1_wg, ffn1_wu, ffn1_wd, mix_norm, w_in, conv_w, a_log,
              dt_bias, dn_out_norm, q_norm, kv_norm, w_uq, w_uk, w_uv, w_iq, idx_k_g,
              idx_k_b, rel_bias, w_o, ffn2_norm, ffn2_wg, ffn2_wu, ffn2_wd, final_norm):
    B, S, _ = x.shape
    for l in range(DEPTH):
        x = x + 0.5 * swiglu(rmsnorm(x, ffn1_norm[l]), ffn1_wg[l], ffn1_wu[l], ffn1_wd[l])

        h = rmsnorm(x, mix_norm[l])
        proj = h @ w_in[l]
        (dn_qkv, dn_z, dn_b, dn_a, c_q, c_kv, idx_k_raw, idx_w,
         gate_a, gate_b) = split_cols(proj, IN_SPLITS)

        qkv = jax.nn.silu(causal_dwconv(dn_qkv, conv_w[l])).astype(jnp.float32)
        dq, dk_, dvv = split_cols(qkv, (DN_QK_W, DN_QK_W, DN_V_W))
        dq = l2norm(dq.reshape(B, S, DN_HEADS, DN_DK)) * (DN_DK ** -0.5)
        dk_ = l2norm(dk_.reshape(B, S, DN_HEADS, DN_DK))
        dvv = dvv.reshape(B, S, DN_HEADS, DN_DV)
        beta = jax.nn.sigmoid(dn_b.astype(jnp.float32))
        g = -jnp.exp(a_log[l].astype(jnp.float32)) * jax.nn.softplus(
            dn_a.astype(jnp.float32) + dt_bias[l].astype(jnp.float32))
        o_dn = gated_delta_rule(dq, dk_, dvv, g, beta)
        o_dn = rmsnorm(o_dn, dn_out_norm[l]) * jax.nn.silu(
            dn_z.reshape(B, S, DN_HEADS, DN_DV).astype(jnp.float32))
        o_dn = o_dn.reshape(B, S, DN_V_W).astype(x.dtype)

        cq = rmsnorm(c_q, q_norm[l])
        ckv = rmsnorm(c_kv, kv_norm[l])
        k_idx = layernorm(idx_k_raw, idx_k_g[l], idx_k_b[l])
        w_idx = idx_w * ((IDX_HEADS ** -0.5) * (IDX_DIM ** -0.5))
        o_sa = dsa_attention(cq, ckv, k_idx, w_idx, w_uq[l], w_uk[l], w_uv[l], w_iq[l],
                             rel_bias).astype(x.dtype)

        merged = jax.nn.sigmoid(gate_a) * o_dn + jax.nn.sigmoid(gate_b) * o_sa
        x = x + merged @ w_o[l]

        x = x + 0.5 * swiglu(rmsnorm(x, ffn2_norm[l]), ffn2_wg[l], ffn2_wu[l], ffn2_wd[l])
    return rmsnorm(x, final_norm)
```

```python
import math
import numpy as np
import concourse.bass as bass
import concourse.mybir as mybir
from concourse.bass_utils import run_bass_kernel_spmd

F32 = mybir.dt.float32
BF16 = mybir.dt.bfloat16
AF = mybir.ActivationFunctionType
ALU = mybir.AluOpType
AX = mybir.AxisListType

SAME_ENGINE_SYNC = True
ANNOTATE = False
SEM_LIMIT = 28000

D_MODEL = 1024; SEQ = 4096; D_FF = 2816
NH = 8; DK = 128; DV = 128; CONV_K = 4; CHUNK = 64
Q_RANK = 256; KV_RANK = 256; IDX_HEADS = 8; IDX_DIM = 64; TOPK = 256
REL_BUCKETS = 32; REL_MAX_DIST = 128
EPS = 1e-6
IN_WIDTH = 6744
C_Q0 = 0; C_K0 = 1024; C_V0 = 2048; C_Z0 = 3072; C_B0 = 4096; C_A0 = 4104
C_CQ0 = 4112; C_CKV0 = 4368; C_IK0 = 4624; C_IW0 = 4688; C_GA0 = 4696; C_GB0 = 5720
T = 512; NT = 4
NQ = 4
PTT = "dve"
ARENA_CHUNKS = 26
NIT = 14
NEG = -30000.0
NB_THR = [1, 2, 3, 4, 5, 6, 7, 8, 9, 10, 11, 12, 13, 14, 15, 16, 19, 21, 24, 27, 31, 35, 40, 46, 52, 59, 67, 77, 87, 99, 113]


class Slot:
    __slots__ = ("w", "r")

    def __init__(self):
        self.w = None
        self.r = {}


class View:
    __slots__ = ("buf", "ap", "slots", "aid")

    def __init__(self, buf, ap, slots, aid=None):
        self.buf = buf
        self.ap = ap
        self.slots = slots
        self.aid = aid

    def map(self, fn):
        return View(self.buf, fn(self.ap), self.slots, self.aid)

    def __getitem__(self, idx):
        return View(self.buf, self.ap[idx], self.slots, self.aid)

    def f32(self):
        return self.map(lambda ap: ap.bitcast(F32))

    def r(self, pattern, **kw):
        return self.map(lambda ap: ap.rearrange(pattern, **kw))


class Arena:
    def __init__(self, P, nchunks):
        self.buf = P.buf("arena", [128, nchunks * 1024], BF16, nslots=nchunks, slot_axis=1, slot_size=1024)
        self.n = nchunks
        self.pos = 0
        self.owner = [None] * nchunks
        self.aid = 0
        self.buf.arena = self

    def reserve(self, nch):
        self.lo = nch
        if self.pos < nch:
            self.pos = nch
        self.aid += 1
        for c in range(nch):
            self.owner[c] = self.aid
        v = self.buf[:, 0: nch * 1024]
        v.aid = self.aid
        return v

    def release(self):
        self.lo = 0

    def alloc(self, nelem, dtype=BF16):
        nb = nelem * (4 if dtype == F32 else 2)
        nch = (nb + 2047) // 2048
        lo = getattr(self, "lo", 0)
        assert nch <= self.n - lo
        if self.pos + nch > self.n:
            self.pos = lo
        a = self.pos
        self.pos += nch
        self.aid += 1
        for c in range(a, a + nch):
            self.owner[c] = self.aid
        v = self.buf[:, a * 1024: a * 1024 + nb // 2]
        v.aid = self.aid
        if dtype == F32:
            v = v.f32()
        return v


class Buf:
    def __init__(self, P, name, shape, dtype, space="sbuf", nslots=1, slot_axis=None, kind=None, slot_size=1):
        nc = P.nc
        self.slot_size = slot_size
        self.name = name
        self.shape = list(shape)
        self.dtype = dtype
        self.space = space
        if space == "sbuf":
            self.t = nc.alloc_sbuf_tensor(name, list(shape), dtype)
        elif space == "psum":
            self.t = nc.alloc_psum_tensor(name, list(shape), dtype)
        else:
            self.t = nc.dram_tensor(name, list(shape), dtype, kind=kind or "Internal")
        self.slot_axis = slot_axis
        self.slots = [Slot() for _ in range(nslots)]
        self.dsem = {}

    def _base(self):
        return self.t.ap() if self.space == "dram" else self.t

    def _slots_of(self, idx):
        if self.slot_axis is None:
            return [0]
        if not isinstance(idx, tuple):
            idx = (idx,)
        if len(idx) <= self.slot_axis:
            return list(range(len(self.slots)))
        s = idx[self.slot_axis]
        ss = self.slot_size
        if isinstance(s, int):
            return [s // ss]
        a, b, _ = s.indices(len(self.slots) * ss)
        return list(range(a // ss, (b - 1) // ss + 1))

    def __getitem__(self, idx):
        return View(self, self._base()[idx], self._slots_of(idx))

    def all(self):
        return View(self, self._base()[:], list(range(len(self.slots))))


class EngState:
    def __init__(self, name, sem, is_pe=False):
        self.name = name
        self.sem = sem
        self.count = 0
        self.waited = {}
        self.is_pe = is_pe


class Prog:
    ENG = ("pe", "act", "dve", "pool", "sp")

    def __init__(self, nc):
        self.nc = nc
        self.nsem = 0
        self.eng = {n: EngState(n, self._newsem("s_" + n), n == "pe") for n in self.ENG}
        self.streams = {n: [] for n in self.ENG}
        self.out_tokens = []
        self.tag = "setup"
        self.ninst = {n: 0 for n in self.ENG}

    def _newsem(self, name):
        self.nsem += 1
        return self.nc.alloc_semaphore("%s_%d" % (name, self.nsem))

    def buf(self, name, shape, dtype, **kw):
        return Buf(self, name, shape, dtype, **kw)

    def _deps(self, E, reads, writes):
        need = {}

        def add(tok):
            k = id(tok[0])
            if k not in need or need[k][1] < tok[1]:
                need[k] = tok

        for v in list(reads) + list(writes):
            if v.aid is not None:
                for s in v.slots:
                    assert v.buf.arena.owner[s] == v.aid, "arena buffer reused while live: %s" % v.buf.name
        for v in reads:
            for s in v.slots:
                sl = v.buf.slots[s]
                if sl.w is not None:
                    add(sl.w)
        for v in writes:
            for s in v.slots:
                sl = v.buf.slots[s]
                if sl.w is not None:
                    add(sl.w)
                for tok in sl.r.values():
                    add(tok)
        out = []
        for k, (sem, val) in need.items():
            if sem is E.sem and (E.is_pe or not SAME_ENGINE_SYNC):
                continue
            if E.waited.get(k, 0) >= val:
                continue
            E.waited[k] = val
            out.append((sem, val))
        return out

    def _mark(self, tok, reads, writes):
        k = id(tok[0])
        for v in reads:
            for s in v.slots:
                v.buf.slots[s].r[k] = tok
        for v in writes:
            for s in v.slots:
                sl = v.buf.slots[s]
                sl.w = tok
                sl.r = {}

    def op(self, ename, fn, reads, writes):
        E = self.eng[ename]
        waits = self._deps(E, reads, writes)
        if E.count >= SEM_LIMIT:
            E.sem = self._newsem("s_" + ename)
            E.count = 0
        E.count += 1
        tok = (E.sem, E.count)
        self.streams[ename].append((waits, [fn], E.sem, 1, self.tag))
        self.ninst[ename] += 1 + len(waits)
        self._mark(tok, reads, writes)

    NDSEM = 14

    def dma(self, qname, pairs, reads, writes, fresh=False, **kw):
        E = self.eng[qname]
        waits = self._deps(E, reads, writes)
        if fresh:
            sem = self._newsem("once")
            fns = [(lambda e, o=o, i=i: e.dma_start(out=o, in_=i, **kw)) for (o, i) in pairs]
            tok = (sem, 16 * len(pairs))
            self.streams[qname].append((waits, fns, sem, 16, self.tag))
            self.ninst[qname] += len(pairs) + len(waits)
            self._mark(tok, reads, writes)
            self._need = []
            self._sim(qname, tok, writes[0], reads, dma_bytes=1000000) if hasattr(self, "_sim") else None
            return
        if not hasattr(self, "dpool"):
            self.dpool = [[self._newsem("dma"), 0] for _ in range(self.NDSEM)]
            self.dpool_i = 0
        idx = self.dpool_i % self.NDSEM
        self.dpool_i += 1
        ds = self.dpool[idx]
        if ds[1] + 16 * len(pairs) > SEM_LIMIT:
            ds = [self._newsem("dma"), 0]
            self.dpool[idx] = ds
        if ds[1] > 0 and E.waited.get(id(ds[0]), 0) < ds[1]:
            E.waited[id(ds[0])] = ds[1]
            waits.append((ds[0], ds[1]))
        fns = [(lambda e, o=o, i=i: e.dma_start(out=o, in_=i, **kw)) for (o, i) in pairs]
        ds[1] += 16 * len(pairs)
        tok = (ds[0], ds[1])
        self.streams[qname].append((waits, fns, ds[0], 16, self.tag))
        self.ninst[qname] += len(pairs) + len(waits)
        self._mark(tok, reads, writes)
        if writes[0].buf.space == "dram":
            self.out_tokens.append(tok)

    def raw(self, ename, fns):
        self.streams[ename].append(([], fns, None, None, self.tag))
        self.ninst[ename] += len(fns)

    def finish(self, ename="pool"):
        last = {}
        for sem, val in self.out_tokens:
            if id(sem) not in last or last[id(sem)][1] < val:
                last[id(sem)] = (sem, val)
        self.streams[ename].append((list(last.values()), [], None, 0, "fin"))

    def emit(self):
        nc = self.nc

        def run(e, stream):
            for waits, fns, sem, inc, tag in stream:
                for (s, v) in waits:
                    e.wait_ge(s, v)
                if inc is None:
                    for fn in fns:
                        fn(e)
                    continue
                for fn in fns:
                    ins = fn(e).then_inc(sem, inc)
                    if ANNOTATE:
                        ins.annotate(tag)

        with nc.Block() as block:
            @block.tensor
            def _(e):
                run(e, self.streams["pe"])

            @block.scalar
            def _(e):
                run(e, self.streams["act"])

            @block.vector
            def _(e):
                run(e, self.streams["dve"])

            @block.gpsimd
            def _(e):
                run(e, self.streams["pool"])

            @block.sync
            def _(e):
                run(e, self.streams["sp"])

    def mm(self, out, lhsT, rhs, start=True, stop=True, **kw):
        self.op("pe", lambda e: e.matmul(out.ap, lhsT.ap, rhs.ap, start=start, stop=stop, **kw),
                [lhsT, rhs], [out])

    def tr(self, out, in_, ident):
        self.op("pe", lambda e: e.transpose(out.ap, in_.ap, ident.ap), [in_, ident], [out])

    def act(self, out, in_, func, bias=None, scale=None, accum=None):
        reads = [in_]
        writes = [out]
        kw = {}
        if bias is not None:
            if isinstance(bias, View):
                reads.append(bias)
                kw["bias"] = bias.ap
            else:
                kw["bias"] = bias
        if scale is not None:
            if isinstance(scale, View):
                reads.append(scale)
                kw["scale"] = scale.ap
            else:
                kw["scale"] = scale
        if accum is not None:
            writes.append(accum)
            kw["accum_out"] = accum.ap
        self.op("act", lambda e: e.activation(out=out.ap, in_=in_.ap, func=func, **kw), reads, writes)

    def ts(self, eng, out, in0, s1, op0, s2=None, op1=None, accum=None):
        reads = [in0]
        writes = [out]
        a1 = s1.ap if isinstance(s1, View) else s1
        a2 = s2.ap if isinstance(s2, View) else s2
        if isinstance(s1, View):
            reads.append(s1)
        if isinstance(s2, View):
            reads.append(s2)
        kw = {}
        if op1 is not None:
            kw["op1"] = op1
        if accum is not None:
            writes.append(accum)
            kw["accum_out"] = accum.ap
        self.op(eng, lambda e: e.tensor_scalar(out.ap, in0.ap, a1, a2, op0, **kw), reads, writes)

    def tt(self, eng, out, in0, in1, op):
        self.op(eng, lambda e: e.tensor_tensor(out.ap, in0.ap, in1.ap, op), [in0, in1], [out])

    def stt(self, eng, out, in0, scalar, in1, op0, op1):
        reads = [in0, in1]
        a = scalar.ap if isinstance(scalar, View) else scalar
        if isinstance(scalar, View):
            reads.append(scalar)
        self.op(eng, lambda e: e.scalar_tensor_tensor(out.ap, in0.ap, a, in1.ap, op0, op1), reads, [out])

    def copy(self, eng, out, in_):
        if eng == "act":
            self.op(eng, lambda e: e.copy(out.ap, in_.ap), [in_], [out])
        else:
            self.op(eng, lambda e: e.tensor_copy(out.ap, in_.ap), [in_], [out])

    def memset(self, eng, out, val):
        self.op(eng, lambda e: e.memset(out.ap, val), [], [out])

    def reduce(self, eng, out, in_, op, axis=AX.X):
        self.op(eng, lambda e: e.tensor_reduce(out.ap, in_.ap, axis, op), [in_], [out])

    def recip(self, out, in_):
        self.op("dve", lambda e: e.reciprocal(out.ap, in_.ap), [in_], [out])


class K:
    pass


def bc_last(v, n):
    return v.map(lambda ap: ap.unsqueeze(2).to_broadcast([ap.shape[0], ap.shape[1], n]))


def bc_mid(v, n):
    return v.map(lambda ap: ap.unsqueeze(1).to_broadcast([ap.shape[0], n, ap.shape[1]]))


def build(ngroups=8, stages=("ffn1",), taps=(), glist=None):
    nc = bass.Bass("TRN2", target_bir_lowering=False)
    P = Prog(nc)
    k = K()
    k.P = P
    k.taps = {}
    k.tapset = set(taps)
    k.stages = stages

    def din(name, shape):
        return P.buf(name, shape, F32, space="dram", kind="ExternalInput")

    k.x = din("x", [SEQ, D_MODEL])
    k.ffn1_norm = din("ffn1_norm", [1, D_MODEL])
    k.ffn1_wg = din("ffn1_wg", [D_MODEL, D_FF])
    k.ffn1_wu = din("ffn1_wu", [D_MODEL, D_FF])
    k.ffn1_wd = din("ffn1_wd", [D_FF, D_MODEL])
    k.mix_norm = din("mix_norm", [1, D_MODEL])
    k.w_in = din("w_in", [D_MODEL, IN_WIDTH])
    k.conv_w = din("conv_w", [CONV_K, 3072])
    k.a_log = din("a_log", [1, NH])
    k.dt_bias = din("dt_bias", [1, NH])
    k.dn_out_norm = din("dn_out_norm", [1, DV])
    k.q_norm = din("q_norm", [1, Q_RANK])
    k.kv_norm = din("kv_norm", [1, KV_RANK])
    k.w_uq = din("w_uq", [Q_RANK, 1024])
    k.w_uk = din("w_uk", [KV_RANK, 1024])
    k.w_uv = din("w_uv", [KV_RANK, 1024])
    k.w_iq = din("w_iq", [Q_RANK, 512])
    k.idx_k_g = din("idx_k_g", [1, IDX_DIM])
    k.idx_k_b = din("idx_k_b", [1, IDX_DIM])
    k.rel_bias = din("rel_bias", [1, REL_BUCKETS * NH])
    k.w_o = din("w_o", [D_MODEL, D_MODEL])
    k.ffn2_norm = din("ffn2_norm", [1, D_MODEL])
    k.ffn2_wg = din("ffn2_wg", [D_MODEL, D_FF])
    k.ffn2_wu = din("ffn2_wu", [D_MODEL, D_FF])
    k.ffn2_wd = din("ffn2_wd", [D_FF, D_MODEL])
    k.final_norm = din("final_norm", [1, D_MODEL])
    k.y = P.buf("y", [SEQ, D_MODEL], F32, space="dram", kind="ExternalOutput", nslots=SEQ // 128, slot_axis=0,
                slot_size=128)

    k.xres = P.buf("xres", [128, NT, D_MODEL], F32, nslots=NT, slot_axis=1)
    k.xn = [P.buf("xn0", [128, D_MODEL], BF16)] * 2
    k.xT = P.buf("xT", [128, 8, T], BF16)
    k.psF = [P.buf("psF%d" % i, [128, 512], F32, space="psum") for i in range(4)]
    k.psH = [P.buf("psH%d" % i, [128, 512], F32, space="psum") for i in range(2)]
    k.psF_i = 0
    k.psT = [P.buf("psT%d" % i, [128, 1024], BF16, space="psum") for i in range(2)]
    k.psT_i = 0
    k.stat = [P.buf("stat%d" % i, [128, 8], F32) for i in range(2)]
    k.gcol = {n: P.buf("gcol_" + n, [128, 8], F32) for n in ("ffn1", "mix", "ffn2")}
    k.ident_f = P.buf("ident_f", [128, 128], F32)
    k.dmat = P.buf("dmat", [128, 128], F32)
    k.ident_b = P.buf("ident_b", [128, 128], BF16)
    for n in ("TRIU", "TRIL", "STRICT", "BLOCK", "SELC0", "SELC1", "ONES"):
        setattr(k, n, P.buf("m_" + n, [128, 128], F32))
    k.zg = P.buf("zg", [128, NT, 1024], BF16)
    k.o_dn = P.buf("o_dn", [128, NT, 1024], BF16)
    k.o_sa = P.buf("o_sa", [128, NT, 1024], BF16)
    k.S = P.buf("S", [128, NH, DV], F32, nslots=2, slot_axis=1, slot_size=4)
    k.Sb = P.buf("Sb", [128, NH, DV], BF16, nslots=2, slot_axis=1, slot_size=4)
    k.halo = P.buf("halo", [128, 24, 3], F32, nslots=24, slot_axis=1)
    k.convw = P.buf("convw", [128, 24, 4], F32)
    k.dtb = P.buf("dtb", [128, NH], F32)
    k.nega = P.buf("nega", [128, NH], F32)
    k.gn_dn = P.buf("gn_dn", [128, DV], F32)
    k.beta = P.buf("beta", [128, NT, NH], F32)
    k.negb = P.buf("negb", [128, NT, NH], F32)
    k.gtok = P.buf("gtok", [128, NT, NH], F32)
    k.gst = P.buf("gst", [128, NT, 32], F32)
    k.egc = P.buf("egc", [128, NT, NH], F32)
    k.elast = P.buf("elast", [128, NT, NH], F32)
    k.bge = P.buf("bge", [128, NT, NH], F32)
    k.dec = P.buf("dec", [128, NT, 16], F32)
    k.ph = P.buf("phase", [128, 16384], BF16, nslots=32, slot_axis=1, slot_size=512)

    def phv(off, n):
        return k.ph[:, off:off + n]

    k.q_qT = phv(0, 2048).r("p (h t) -> p h t", h=NQ)
    k.q_kT = phv(2048, 2048).r("p (h t) -> p h t", h=NQ)
    k.q_vtm = phv(4096, 2048).r("p (i h d) -> p i h d", i=NT, h=NQ)
    k.q_ktm = phv(6144, 2048).r("p (i h d) -> p i h d", i=NT, h=NQ)
    k.q_X = [phv(8192 + i * 512, 512).r("p (h t) -> p h t", h=NQ) for i in range(NT)]
    k.q_XT = [phv(10240 + i * 512, 512).r("p (h t) -> p h t", h=NQ) for i in range(NT)]
    k.q_AT = [phv(12288 + i * 512, 512).r("p (h t) -> p h t", h=NQ) for i in range(NT)]
    k.q_IT = [phv(14336 + i * 512, 512).r("p (h t) -> p h t", h=NQ) for i in range(NT)]
    k.s_qidx = phv(0, 2048).r("p (h t) -> p h t", h=4)
    k.s_wuq = phv(2048, 2048).r("p (c f) -> p c f", c=2)
    k.s_wukT = phv(4096, 2048).r("p (h r) -> p h r", h=NH)
    k.s_wuv = phv(6144, 2048).r("p (c f) -> p c f", c=2)
    k.s_wiq = phv(8192, 1024).r("p (c f) -> p c f", c=2)
    k.s_cqT = phv(9216, 1024).r("p (c t) -> p c t", c=2)
    k.s_maskT = phv(10240, 4096)
    k.s_ql = phv(14336, 2048).r("p (h c q) -> p h c q", h=NH, c=2)
    k.rl = [P.buf("rl%d" % i, [128, 512], F32) for i in range(2)]
    k.rl_i = 0
    k.bb_rb = P.buf("bb_rb", [128, REL_BUCKETS * NH], F32)
    k.bb_dl = P.buf("bb_dl", [128, REL_BUCKETS * NH], F32)
    k.bb_dd = P.buf("bb_dd", [128, 128], F32)
    k.bb_acc = P.buf("bb_acc", [128, NH, 128], F32)
    k.bb_ind = P.buf("bb_ind", [128, 128], F32)
    NTT = SEQ // 128
    k.ckv_tm = P.buf("ckv_tm", [128, NTT, KV_RANK + 1], BF16, nslots=NTT, slot_axis=1)
    k.ckvT = P.buf("ckvT", [128, 2, SEQ], BF16, nslots=NTT, slot_axis=2, slot_size=128)
    k.kidxT = P.buf("kidxT", [128, SEQ], BF16, nslots=NTT, slot_axis=1, slot_size=128)
    k.biasD = P.buf("biasD", [128, NH, 128], BF16)
    k.bias1 = P.buf("bias1", [128, NH, 128], BF16)
    k.causneg = P.buf("causneg", [128, 128], F32)
    k.qn_bc = P.buf("qn_bc", [128, Q_RANK], F32)
    k.kvn_bc = P.buf("kvn_bc", [128, KV_RANK], F32)
    k.ikg_bc = P.buf("ikg_bc", [128, IDX_DIM], F32)
    k.ikb_bc = P.buf("ikb_bc", [128, IDX_DIM], F32)
    k.widx = P.buf("widx", [128, NT, IDX_HEADS], F32)
    k.thr = P.buf("thr", [128, 8], F32)
    k.thW = P.buf("thW", [128, NIT + 1], F32)
    k.pw2 = P.buf("pw2", [128, NIT + 1], F32)
    k.arena = Arena(P, ARENA_CHUNKS)

    convert_weights(k)
    setup(k)
    for g in (glist if glist is not None else range(ngroups)):
        group(k, g, stages)
    P.finish()
    P.emit()
    k.nc = nc
    return k


class BankPool:
    def __init__(self, banks):
        self.free = list(banks)

    def get(self):
        assert self.free, "PSUM bank pool exhausted"
        return self.free.pop(0)

    def put(self, b):
        self.free.append(b)


def act_rsqrt(P, out, in_, scale, eps):
    P.act(out, in_, AF.Ln, bias=eps, scale=scale)
    P.act(out, out, AF.Exp, scale=-0.5)


def act_sigmoid(P, out, in_):
    P.act(out, in_, AF.Exp, scale=-1.0)
    P.act(out, out, AF.Ln, bias=1.0)
    P.act(out, out, AF.Exp, scale=-1.0)


def next_psF(k):
    b = k.psF[k.psF_i % len(k.psF)]
    k.psF_i += 1
    return b


def next_psT(k):
    b = k.psT[k.psT_i % len(k.psT)]
    k.psT_i += 1
    return b


def tap(k, name, view, shape, dtype=F32):
    if name not in k.tapset:
        return
    cnt = k.taps.get(name, 0)
    k.taps[name] = cnt + 1
    d = k.P.buf("tap_%s_%d" % (name, cnt), shape, dtype, space="dram", kind="ExternalOutput")
    k.P.dma("sp", [(d.all().ap, view.ap)], [view], [d.all()])


def convert_weights(k):
    P = k.P
    nc = P.nc
    P.tag = "convert"

    W = {}
    order = ("ffn1_wg", "ffn1_wu", "ffn1_wd", "w_in", "w_uq", "w_uv", "w_iq", "w_uk", "w_o", "ffn2_wg", "ffn2_wu", "ffn2_wd")
    for nm in order:
        src = getattr(k, nm)
        W[nm] = P.buf(nm + "_bf", src.shape, BF16, space="dram")
        P.dma("pool", [(W[nm].all().ap, src.all().ap)], [src.all()], [W[nm].all()], fresh=True)
    for nm, b in W.items():
        setattr(k, nm, b)


def setup(k):
    P = k.P
    dm = k.dmat
    P.op("pool", lambda e: e.iota(dm.all().ap, [[1, 128]], base=0, channel_multiplier=-1,
                                  allow_small_or_imprecise_dtypes=True), [], [dm.all()])
    P.memset("dve", k.BLOCK.all(), 0.0)
    P.memset("dve", k.BLOCK[0:64, 0:64], 1.0)
    P.memset("dve", k.BLOCK[64:128, 64:128], 1.0)
    P.memset("dve", k.ONES.all(), 1.0)
    P.memset("dve", k.SELC0.all(), 0.0)
    P.memset("dve", k.SELC0[0:64, :], 1.0)
    P.memset("dve", k.SELC1.all(), 0.0)
    P.memset("dve", k.SELC1[64:128, :], 1.0)
    P.ts("dve", k.TRIU.all(), dm.all(), 0.0, ALU.is_ge)
    P.tt("dve", k.TRIU.all(), k.TRIU.all(), k.BLOCK.all(), ALU.mult)
    P.ts("dve", k.TRIL.all(), dm.all(), 0.0, ALU.is_le)
    P.tt("dve", k.TRIL.all(), k.TRIL.all(), k.BLOCK.all(), ALU.mult)
    P.ts("dve", k.STRICT.all(), dm.all(), 0.0, ALU.is_lt)
    P.tt("dve", k.STRICT.all(), k.STRICT.all(), k.BLOCK.all(), ALU.mult)
    P.ts("dve", k.ident_f.all(), dm.all(), 0.0, ALU.is_equal)
    P.copy("dve", k.ident_b.all(), k.ident_f.all())
    for n, src in (("ffn1", k.ffn1_norm), ("mix", k.mix_norm), ("ffn2", k.ffn2_norm)):
        sv = src.all().r("a (c p) -> p (a c)", p=128)
        P.dma("sp", [(k.gcol[n].all().ap, sv.ap)], [sv], [k.gcol[n].all()], allow_slow_non_contiguous=True)

    def bcast(dst, src):
        sv = src.all().map(lambda ap: ap.rearrange("a d -> (a d)").partition_broadcast(128))
        P.dma("sp", [(dst.all().ap, sv.ap)], [sv], [dst.all()])

    bcast(k.dtb, k.dt_bias)
    bcast(k.nega, k.a_log)
    P.act(k.nega.all(), k.nega.all(), AF.Exp)
    P.ts("dve", k.nega.all(), k.nega.all(), -1.0, ALU.mult)
    bcast(k.gn_dn, k.dn_out_norm)
    for kk in range(CONV_K):
        sv = k.conv_w[kk:kk + 1, :].r("a (c p) -> p (a c)", p=128)
        P.dma("sp", [(k.convw[:, :, kk].ap, sv.ap)], [sv], [k.convw.all()], allow_slow_non_contiguous=True)
    P.memset("dve", k.halo.all(), 0.0)
    for j in range(NIT + 1):
        P.memset("dve", k.pw2[:, j:j + 1], 2.0 ** -(j + 1))
    bcast(k.qn_bc, k.q_norm)
    bcast(k.kvn_bc, k.kv_norm)
    bcast(k.ikg_bc, k.idx_k_g)
    bcast(k.ikb_bc, k.idx_k_b)
    P.memset("dve", k.ckv_tm[:, :, KV_RANK:KV_RANK + 1], 1.0)
    P.ts("dve", k.causneg.all(), dm.all(), 0.0, ALU.is_gt, -1e30, ALU.mult)
    P.memset("dve", k.S.all(), 0.0)
    P.memset("dve", k.Sb.all(), 0.0)


def build_bias_tiles(k):
    P = k.P
    dm = k.dmat
    rb = k.bb_rb.all()
    sv = k.rel_bias.all().map(lambda ap: ap.rearrange("a d -> (a d)").partition_broadcast(128))
    P.dma("sp", [(rb.ap, sv.ap)], [sv], [rb])
    dl = k.bb_dl.all()
    P.tt("dve", dl[:, NH:REL_BUCKETS * NH], rb[:, NH:REL_BUCKETS * NH], rb[:, 0:(REL_BUCKETS - 1) * NH], ALU.subtract)
    P.tt("dve", dl[:, 0:NH], rb[:, 0:NH], rb[:, (REL_BUCKETS - 1) * NH:REL_BUCKETS * NH], ALU.subtract)
    for which, dst in ((0, k.biasD), (1, k.bias1)):
        dd = k.bb_dd.all()
        P.ts("dve", dd, dm.all(), 128.0 * which, ALU.add)
        acc = k.bb_acc.all()
        for h in range(NH):
            P.ts("dve", acc[:, h, :], dd, 0.0, ALU.mult, dl[:, h:h + 1], ALU.add)
        ind = k.bb_ind.all()
        for b in range(1, REL_BUCKETS):
            P.ts("dve", ind, dd, float(NB_THR[b - 1]), ALU.is_ge)
            for h in range(NH):
                P.stt("dve", acc[:, h, :], ind, dl[:, b * NH + h:b * NH + h + 1], acc[:, h, :], ALU.mult, ALU.add)
            yield
        P.copy("dve", dst.all(), acc)


def rms_to_T(k, i, gcol, dstT):
    P = k.P
    xt = k.xres[:, i, :]
    st = k.stat[i % 2]
    xn = k.xn[i % 2]
    P.act(xn.all(), xt, AF.Square, accum=st[:, 0:1])
    act_rsqrt(P, st[:, 2:3], st[:, 0:1], 1.0 / D_MODEL, EPS)
    P.ts("dve", xn.all(), xt, st[:, 2:3], ALU.mult)
    pt = next_psT(k)
    for c in range(8):
        P.tr(pt[:, c * 128:(c + 1) * 128], xn[:, c * 128:(c + 1) * 128], k.ident_b.all())
    P.tt("dve", dstT[:, :, i * 128:(i + 1) * 128], pt.all().r("p (c t) -> p c t", c=8),
         bc_last(gcol.all(), 128), ALU.mult)


def wdma(k, dst_view, src_view):
    k.P.dma("sp", [(dst_view.ap, src_view.ap)], [src_view], [dst_view])


def wload_cols(k, w, c0, ncols):
    wt = k.arena.alloc(8 * ncols).r("p (c f) -> p c f", c=8)
    wdma(k, wt, w[:, c0:c0 + ncols].r("(c p) f -> p c f", p=128))
    return wt


def ffn(k, gcol, wg, wu, wd, bg=None):
    P = k.P
    for i in range(NT):
        rms_to_T(k, i, gcol, k.xT)
    nblk = (D_FF + 511) // 512

    def gate_up(fb):
        f0 = fb * 512
        fw = min(512, D_FF - f0)
        nfc = fw // 128
        wg_v = wload_cols(k, wg, f0, fw)
        wu_v = wload_cols(k, wu, f0, fw)
        hT = k.arena.alloc(nfc * T).r("p (c t) -> p c t", c=nfc)
        for fc in range(nfc):
            pg = next_psF(k)
            pu = next_psF(k)
            for c in range(8):
                P.mm(pg.all(), wg_v[:, c, fc * 128:(fc + 1) * 128], k.xT[:, c, :], start=(c == 0), stop=(c == 7))
            for c in range(8):
                P.mm(pu.all(), wu_v[:, c, fc * 128:(fc + 1) * 128], k.xT[:, c, :], start=(c == 0), stop=(c == 7))
            sg = k.rl[k.rl_i % 2].all()
            k.rl_i += 1
            P.act(sg, pg.all(), AF.Silu)
            P.tt("dve", hT[:, fc, :], sg, pu.all(), ALU.mult)
        return (hT, f0, fw, nfc)

    def down(hT, f0, fw, nfc):
        wd_v = k.arena.alloc(nfc * 1024).r("p (c d) -> p c d", c=nfc)
        wdma(k, wd_v, wd[f0:f0 + fw, :].r("(c p) d -> p c d", p=128))
        for i in range(NT):
            for hh in range(2):
                po = k.psH[(2 * i + hh) % 2]
                for fc in range(nfc):
                    P.mm(po.all(), hT[:, fc, i * 128:(i + 1) * 128], wd_v[:, fc, hh * 512:(hh + 1) * 512],
                         start=(fc == 0), stop=(fc == nfc - 1))
                xs = k.xres[:, i, hh * 512:(hh + 1) * 512]
                P.stt("dve", xs, po.all(), 0.5, xs, ALU.mult, ALU.add)

    def advance(n):
        if bg is not None:
            for _ in range(n):
                if next(bg, "end") == "end":
                    break

    prev = None
    for fb in range(nblk):
        cur = gate_up(fb)
        advance(6)
        if prev is not None:
            down(*prev)
        advance(6)
        prev = cur
    down(*prev)


def mixer_proj(k, g):
    P = k.P
    for i in range(NT):
        rms_to_T(k, i, k.gcol["mix"], k.xT)
    for hh in range(2):
        wz = wload_cols(k, k.w_in, C_Z0 + hh * 512, 512)
        for i in range(NT):
            ps = next_psF(k)
            for c in range(8):
                P.mm(ps.all(), k.xT[:, c, i * 128:(i + 1) * 128], wz[:, c, :], start=(c == 0), stop=(c == 7))
            sg = k.rl[k.rl_i % 2].all()
            k.rl_i += 1
            P.act(sg, ps.all(), AF.Silu)
            P.tt("dve", k.zg[:, i, hh * 512:(hh + 1) * 512].r("p (h d) -> p h d", h=4),
                 sg.r("p (h d) -> p h d", h=4), bc_mid(k.gn_dn.all(), 4), ALU.mult)
    k.smallp = k.arena.alloc(NT * 600, F32).r("p (i c) -> p i c", i=NT)
    w1 = wload_cols(k, k.w_in, 4096, 512)
    w2 = wload_cols(k, k.w_in, 4608, 88)
    for i in range(NT):
        ps = next_psF(k)
        for c in range(8):
            P.mm(ps.all(), k.xT[:, c, i * 128:(i + 1) * 128], w1[:, c, :], start=(c == 0), stop=(c == 7))
        P.copy("act", k.smallp[:, i, 0:512], ps.all())
        ps2 = next_psF(k)
        for c in range(8):
            P.mm(ps2[:, 0:88], k.xT[:, c, i * 128:(i + 1) * 128], w2[:, c, :], start=(c == 0), stop=(c == 7))
        P.copy("act", k.smallp[:, i, 512:600], ps2[:, 0:88])
    act_sigmoid(P, k.beta.all(), k.smallp[:, :, 0:8])
    P.ts("dve", k.negb.all(), k.beta.all(), -1.0, ALU.mult)
    P.tt("dve", k.gtok.all(), k.smallp[:, :, 8:16], bc_mid(k.dtb.all(), NT), ALU.add)
    P.act(k.gtok.all(), k.gtok.all(), AF.Exp)
    P.act(k.gtok.all(), k.gtok.all(), AF.Ln, bias=1.0)
    P.tt("dve", k.gtok.all(), k.gtok.all(), bc_mid(k.nega.all(), NT), ALU.mult)
    ps = next_psF(k)
    for i in range(NT):
        for j, m in enumerate((k.TRIU, k.BLOCK, k.SELC0, k.SELC1)):
            P.mm(ps[:, i * 32 + j * 8: i * 32 + j * 8 + 8], m.all(), k.gtok[:, i, :])
    P.copy("act", k.gst.all(), ps[:, 0:NT * 32].r("p (i c) -> p i c", i=NT))
    P.act(k.egc.all(), k.gst[:, :, 0:8], AF.Exp)
    P.tt("dve", k.elast.all(), k.gst[:, :, 8:16], k.gst[:, :, 0:8], ALU.subtract)
    P.act(k.elast.all(), k.elast.all(), AF.Exp)
    P.act(k.dec.all(), k.gst[:, :, 16:32], AF.Exp)
    P.tt("dve", k.bge.all(), k.beta.all(), k.egc.all(), ALU.mult)
    tap(k, "beta", k.beta.all(), [128, NT, NH])
    tap(k, "gtok", k.gtok.all(), [128, NT, NH])
    dsa_prep(k, g)


def dn_quad(k, g, hq):
    P = k.P
    A = k.arena
    hs = hq * NQ
    P.tag = "dnA"
    qT, kT, v_tm, k_tm = k.q_qT, k.q_kT, k.q_vtm, k.q_ktm
    hsl = slice(hs, hs + NQ)
    r4 = lambda v: v.r("p (h t) -> p h t", h=NQ)
    rd = lambda v: v.r("p (h d) -> p h d", h=NQ)
    FP = BankPool(k.psF + k.psH)
    TP = BankPool(k.psT)

    wvs = {}

    def chainA(kind, hl):
        ch = kind * 8 + hs + hl
        st8 = {}

        def a1():
            if hl == 0:
                wvs[kind] = wload_cols(k, k.w_in, kind * 1024 + hs * 128, NQ * 128)
            wv = wvs[kind]
            ps = FP.get()
            st8["ps"] = ps
            for c in range(8):
                P.mm(ps.all(), wv[:, c, hl * 128:(hl + 1) * 128], k.xT[:, c, :], start=(c == 0), stop=(c == 7))

        def a2a():
            P.tag = "dnA"
            ps = st8["ps"]
            cb = A.alloc(1024, F32)
            st8["cb"] = cb
            P.copy("act", cb[:, 0:3], k.halo[:, ch, :])
            P.copy("act", cb[:, 3:3 + T], ps.all())
            FP.put(ps)
            P.copy("act", k.halo[:, ch, :], cb[:, T:T + 3])

        def a2b():
            P.tag = "dnA"
            cb = st8["cb"]
            yv = A.alloc(T, F32)
            st8["yv"] = yv
            P.ts("dve", yv, cb[:, 3:3 + T], k.convw[:, ch, 3:4], ALU.mult)
            for kk in range(3):
                P.stt("dve", yv, cb[:, kk:kk + T], k.convw[:, ch, kk:kk + 1], yv, ALU.mult, ALU.add)

        def a2c():
            P.tag = "dnA"
            sv = A.alloc(T, F32)
            st8["sv"] = sv
            act_sigmoid(P, sv, st8["yv"])

        def a2d():
            P.tag = "dnA"
            cb, yv, sv = st8["cb"], st8["yv"], st8["sv"]
            if kind == 2:
                sb = cb[:, 0:T // 2].map(lambda ap: ap.bitcast(BF16))
                st8["sb"] = sb
                P.tt("dve", sb, sv, yv, ALU.mult)
            else:
                P.tt("dve", sv, sv, yv, ALU.mult)
                sq = cb[:, 0:T]
                st8["sq"] = sq
                P.act(sq, sv, AF.Square)

        def a3():
            if kind == 2:
                pt = TP.get()
                st8["pt"] = pt
                for i in range(NT):
                    P.tr(pt[:, i * 128:(i + 1) * 128], st8["sb"][:, i * 128:(i + 1) * 128], k.ident_b.all())
            else:
                pss = FP.get()
                st8["pss"] = pss
                P.mm(pss.all(), k.ONES.all(), st8["sq"])

        def a4():
            if kind == 2:
                P.copy("act", v_tm[:, :, hl, :], st8["pt"][:, 0:NT * 128].r("p (i d) -> p i d", i=NT))
                TP.put(st8["pt"])
            else:
                rn = st8["yv"]
                act_rsqrt(P, rn, st8["pss"].all(), 1.0, EPS)
                FP.put(st8["pss"])
                dst = (qT if kind == 0 else kT)[:, hl, :]
                st8["dst"] = dst
                if kind == 0:
                    P.stt("dve", dst, st8["sv"], DK ** -0.5, rn, ALU.mult, ALU.mult)
                else:
                    P.tt("dve", dst, st8["sv"], rn, ALU.mult)

        def a5():
            pt = TP.get()
            st8["pt"] = pt
            for i in range(NT):
                P.tr(pt[:, i * 128:(i + 1) * 128], st8["dst"][:, i * 128:(i + 1) * 128], k.ident_b.all())

        def a6():
            P.copy("act", k_tm[:, :, hl, :], st8["pt"][:, 0:NT * 128].r("p (i d) -> p i d", i=NT))
            TP.put(st8["pt"])

        stages = [a1, a2a, a2b, a2c, a2d, a3, a4]
        if kind == 1:
            stages += [a5, a6]
        return stages

    diagonal([chainA(kind, hl) for kind in (1, 0, 2) for hl in range(NQ)])
    tap(k, "qT", qT, [128, NQ, T], BF16)
    tap(k, "kT", kT, [128, NQ, T], BF16)
    tap(k, "v_tm", v_tm, [128, NT, NQ, 128], BF16)

    P.tag = "dnB"
    X = k.q_X; XT = k.q_XT; AT = k.q_AT; IT = k.q_IT

    def chainBC(i):
        st8 = {}

        def b1():
            lg = r4(A.alloc(NQ * 128, F32))
            st8["lg"] = lg
            P.tt("dve", lg, bc_mid(k.TRIU.all(), NQ), bc_last(k.gtok[:, i, hsl], 128), ALU.mult)

        def b2():
            psG = FP.get()
            st8["psG"] = psG
            for hl in range(NQ):
                P.mm(psG[:, hl * 128:(hl + 1) * 128], st8["lg"][:, hl, :], k.STRICT.all())
            psK = FP.get()
            st8["psK"] = psK
            for hl in range(NQ):
                kt = kT[:, hl, i * 128:(i + 1) * 128]
                P.mm(psK[:, hl * 128:(hl + 1) * 128], kt, kt)

        def b3():
            E = st8["lg"]
            P.act(E, r4(st8["psG"].all()), AF.Exp)
            Ds = r4(A.alloc(NQ * 128, F32))
            P.tt(PTT, Ds, E, bc_mid(k.STRICT.all(), NQ), ALU.mult)
            P.tt(PTT, Ds, Ds, bc_last(k.negb[:, i, hsl], 128), ALU.mult)
            P.tt("dve", X[i], r4(st8["psK"].all()), Ds, ALU.mult)
            P.tt(PTT, E, E, bc_mid(k.TRIL.all(), NQ), ALU.mult)
            FP.put(st8["psG"])
            FP.put(st8["psK"])

        def b4():
            psQ = FP.get()
            st8["psQ"] = psQ
            for hl in range(NQ):
                P.mm(psQ[:, hl * 128:(hl + 1) * 128], qT[:, hl, i * 128:(i + 1) * 128], kT[:, hl, i * 128:(i + 1) * 128])
            pt = TP.get()
            st8["pt"] = pt
            for hl in range(NQ):
                P.tr(pt[:, hl * 128:(hl + 1) * 128], X[i][:, hl, :], k.ident_b.all())
            psA0 = FP.get()
            st8["psA0"] = psA0
            for hl in range(NQ):
                P.mm(psA0[:, hl * 128:(hl + 1) * 128], X[i][:, hl, :], k.ident_b.all(), start=True, stop=False)
                P.mm(psA0[:, hl * 128:(hl + 1) * 128], k.ident_b.all(), k.ident_b.all(), start=False, stop=True)

        def b5():
            intra = r4(A.alloc(NQ * 128))
            st8["intra"] = intra
            P.tt("dve", intra, r4(st8["psQ"].all()), st8["lg"], ALU.mult)
            P.copy("act", XT[i], r4(st8["pt"][:, 0:NQ * 128]))
            P.copy("act", AT[i], r4(st8["psA0"].all()))
            FP.put(st8["psQ"])
            FP.put(st8["psA0"])
            TP.put(st8["pt"])

        def b6():
            pt = TP.get()
            st8["pt2"] = pt
            for hl in range(NQ):
                P.tr(pt[:, hl * 128:(hl + 1) * 128], st8["intra"][:, hl, :], k.ident_b.all())

        def b7():
            P.copy("act", IT[i], r4(st8["pt2"][:, 0:NQ * 128]))
            TP.put(st8["pt2"])

        stages = [b1, b2, b3, b4, b5, b6, b7]
        for m in range(5):
            def c1(m=m):
                P.tag = "dnC"
                psX = FP.get()
                st8["psX"] = psX
                for hl in range(NQ):
                    P.mm(psX[:, hl * 128:(hl + 1) * 128], XT[i][:, hl, :], X[i][:, hl, :])
                if m < 4:
                    psXT = FP.get()
                    st8["psXT"] = psXT
                    for hl in range(NQ):
                        P.mm(psXT[:, hl * 128:(hl + 1) * 128], X[i][:, hl, :], XT[i][:, hl, :])

            def c2(m=m):
                P.copy("act", X[i], r4(st8["psX"].all()))
                FP.put(st8["psX"])
                if m < 4:
                    P.copy("dve", XT[i], r4(st8["psXT"].all()))
                    FP.put(st8["psXT"])

            def c3(m=m):
                psA = FP.get()
                st8["psA"] = psA
                for hl in range(NQ):
                    P.mm(psA[:, hl * 128:(hl + 1) * 128], k.ident_b.all(), AT[i][:, hl, :], start=True, stop=False)
                    P.mm(psA[:, hl * 128:(hl + 1) * 128], X[i][:, hl, :], AT[i][:, hl, :], start=False, stop=True)

            def c4(m=m):
                P.copy("act", AT[i], r4(st8["psA"].all()))
                FP.put(st8["psA"])

            stages += [c1, c2, c3, c4]
        return stages

    diagonal([chainBC(i) for i in range(NT)])

    P.tag = "dnD"
    prep = []
    for i in range(NT):
        vb = rd(A.alloc(NQ * 128))
        P.tt(PTT, vb, v_tm[:, i, :, :], bc_last(k.beta[:, i, hsl], 128), ALU.mult)
        kbg = rd(A.alloc(NQ * 128))
        P.tt(PTT, kbg, k_tm[:, i, :, :], bc_last(k.bge[:, i, hsl], 128), ALU.mult)
        kdec = rd(A.alloc(NQ * 128))
        P.tt(PTT, kdec, k_tm[:, i, :, :], bc_last(k.elast[:, i, hsl], 128), ALU.mult)
        prep.append([vb, kbg, kdec])
    for i in range(NT):
        vb, kbg, kdec = prep[i]
        psU = next_psF(k)
        for hl in range(NQ):
            P.mm(psU[:, hl * 128:(hl + 1) * 128], AT[i][:, hl, :], vb[:, hl, :])
        psW = next_psF(k)
        for hl in range(NQ):
            P.mm(psW[:, hl * 128:(hl + 1) * 128], kbg[:, hl, :], AT[i][:, hl, :])
        prep[i] += [psU, psW]
        if i % 2 == 1 or i == NT - 1:
            for ii in range(i - (i % 2), i + 1):
                u = rd(A.alloc(NQ * 128, F32))
                P.copy("act", u, rd(prep[ii][3].all()))
                wT = X[ii]
                P.copy("dve", wT, r4(prep[ii][4].all()))
                prep[ii] += [u, wT]
    def make_norm(i, o, tmp):
        def norm():
            sq = tmp
            P.act(sq, o, AF.Square)
            st = k.stat[i % 2]
            P.reduce("dve", st[:, 0:NQ], sq, ALU.add)
            act_rsqrt(P, st[:, 0:NQ], st[:, 0:NQ], 1.0 / DV, EPS)
            P.tt("dve", o, o, bc_last(st[:, 0:NQ], 128), ALU.mult)
            P.tt("dve", rd(k.o_dn[:, i, hs * 128:(hs + NQ) * 128]), o,
                 rd(k.zg[:, i, hs * 128:(hs + NQ) * 128]), ALU.mult)
            tap(k, "o_raw", o, [128, NQ, 128])
        return norm

    pending = None
    for i in range(NT):
        vb, kbg, kdec, _, _, u, wT = prep[i]
        o = rd(A.alloc(NQ * 128, F32))
        vn = XT[i]
        tmp = rd(A.alloc(NQ * 128, F32))
        for c in range(2):
            rs = slice(c * 64, c * 64 + 64)
            Sb = k.Sb[:, hs:hs + NQ, :]
            Sf = k.S[:, hs:hs + NQ, :]
            psA = next_psF(k)
            for hl in range(NQ):
                P.mm(psA[:, hl * 128:(hl + 1) * 128], wT[:, hl, :], Sb[:, hl, :])
            psB1 = next_psF(k)
            for hl in range(NQ):
                P.mm(psB1[:, hl * 128:(hl + 1) * 128], qT[:, hl, i * 128:(i + 1) * 128], Sb[:, hl, :])
            P.tt("dve", vn[rs], u[rs], rd(psA[rs, :]), ALU.subtract)
            P.tt(PTT, Sf, Sf, bc_last(k.dec[:, i, c * 8 + hs: c * 8 + hs + NQ], 128), ALU.mult)
            psB2 = next_psF(k)
            for hl in range(NQ):
                P.mm(psB2[:, hl * 128:(hl + 1) * 128], IT[i][rs, hl, :], vn[rs, hl, :])
            psS = next_psF(k)
            for hl in range(NQ):
                P.mm(psS[:, hl * 128:(hl + 1) * 128], kdec[rs, hl, :], vn[rs, hl, :])
            P.tt("dve", Sf, Sf, rd(psS.all()), ALU.add)
            P.copy("act", Sb, Sf)
            P.tt("dve", o[rs], rd(psB1[rs, :]), bc_last(k.egc[rs, i, hsl], 128), ALU.mult)
            P.copy("act", tmp[rs], rd(psB2[rs, :]))
            P.tt("dve", o[rs], o[rs], tmp[rs], ALU.add)
        if pending is not None:
            pending()
        pending = make_norm(i, o, tmp)
    pending()


def dsa_prep(k, g):
    P = k.P
    A = k.arena
    sp = k.smallp
    cqn = A.alloc(NT * 256).r("p (i r) -> p i r", i=NT)
    for (c0, gn, dst) in ((16, k.qn_bc, cqn), (272, k.kvn_bc, None)):
        sq = A.alloc(NT * 256, F32).r("p (i r) -> p i r", i=NT)
        P.tt("dve", sq, sp[:, :, c0:c0 + 256], sp[:, :, c0:c0 + 256], ALU.mult)
        st = k.stat[0]
        P.reduce("dve", st[:, 0:NT], sq, ALU.add)
        act_rsqrt(P, st[:, 0:NT], st[:, 0:NT], 1.0 / 256, EPS)
        P.tt("dve", sq, sp[:, :, c0:c0 + 256], bc_last(st[:, 0:NT], 256), ALU.mult)
        if dst is None:
            dst = k.ckv_tm[:, g * NT:(g + 1) * NT, 0:KV_RANK]
        P.tt("dve", dst, sq, bc_mid(gn.all(), NT), ALU.mult)
    ik = sp[:, :, 528:592]
    st = k.stat[1]
    P.reduce("dve", st[:, 0:NT], ik, ALU.add)
    P.ts("dve", st[:, 0:NT], st[:, 0:NT], -1.0 / IDX_DIM, ALU.mult)
    xc = A.alloc(NT * 64, F32).r("p (i d) -> p i d", i=NT)
    P.tt("dve", xc, ik, bc_last(st[:, 0:NT], 64), ALU.add)
    sq = A.alloc(NT * 64, F32).r("p (i d) -> p i d", i=NT)
    P.tt("dve", sq, xc, xc, ALU.mult)
    P.reduce("dve", st[:, 4:4 + NT], sq, ALU.add)
    act_rsqrt(P, st[:, 4:4 + NT], st[:, 4:4 + NT], 1.0 / IDX_DIM, EPS)
    P.tt("dve", xc, xc, bc_last(st[:, 4:4 + NT], 64), ALU.mult)
    P.tt("dve", xc, xc, bc_mid(k.ikg_bc.all(), NT), ALU.mult)
    kd = A.alloc(NT * 128).r("p (i e d) -> p i e d", i=NT, e=2)
    for e in range(2):
        P.tt("dve", kd[:, :, e, :], xc, bc_mid(k.ikb_bc.all(), NT), ALU.add)
    P.ts("dve", k.widx.all(), sp[:, :, 592:600], (IDX_HEADS ** -0.5) * (IDX_DIM ** -0.5), ALU.mult)
    for i in range(NT):
        gi = g * NT + i
        pt = next_psT(k)
        for c in range(2):
            P.tr(pt[:, c * 128:(c + 1) * 128], cqn[:, i, c * 128:(c + 1) * 128], k.ident_b.all())
        for c in range(2):
            P.tr(pt[:, (2 + c) * 128:(3 + c) * 128], k.ckv_tm[:, gi, c * 128:(c + 1) * 128], k.ident_b.all())
        P.tr(pt[:, 512:640], kd[:, i, :, :].r("p e d -> p (e d)"), k.ident_b.all())
        P.copy("act", k.s_cqT[:, :, i * 128:(i + 1) * 128], pt[:, 0:256].r("p (c t) -> p c t", c=2))
        P.copy("act", k.ckvT[:, :, gi * 128:(gi + 1) * 128], pt[:, 256:512].r("p (c t) -> p c t", c=2))
        P.copy("act", k.kidxT[:, gi * 128:(gi + 1) * 128], pt[:, 512:640])


def dsa(k, g):
    P = k.P
    A = k.arena
    wdma(k, k.s_wuq, k.w_uq.all().r("(c p) f -> p c f", p=128))
    wdma(k, k.s_wuv, k.w_uv.all().r("(c p) f -> p c f", p=128))
    wdma(k, k.s_wiq, k.w_iq.all().r("(c p) f -> p c f", p=128))
    wuk = A.alloc(2 * 1024).r("p (c f) -> p c f", c=2)
    wdma(k, wuk, k.w_uk.all().r("(c p) f -> p c f", p=128))
    for h in range(NH):
        pt = next_psT(k)
        for c in range(2):
            P.tr(pt[:, c * 128:(c + 1) * 128], wuk[:, c, h * 128:(h + 1) * 128], k.ident_b.all())
        P.copy("act", k.s_wukT[:, h, :], pt[:, 0:256])
    for hp in range(4):
        ps = next_psF(k)
        for c in range(2):
            P.mm(ps.all(), k.s_wiq[:, c, hp * 128:(hp + 1) * 128], k.s_cqT[:, c, :], start=(c == 0), stop=(c == 1))
        P.copy("act", k.s_qidx[:, hp, :], ps.all())
    tap(k, "ckv", k.ckv_tm[:, g * NT:(g + 1) * NT, :], [128, NT, KV_RANK + 1], BF16)
    tap(k, "cqT", k.s_cqT, [128, 2, T], BF16)
    tap(k, "kidxT", k.kidxT[:, g * T:(g + 1) * T], [128, T], BF16)
    tap(k, "qidx", k.s_qidx, [128, 4, T], BF16)
    tap(k, "widx", k.widx.all(), [128, NT, 8])
    fx = k.arena.reserve(12)
    score_buf = fx[:, 0:8192].f32()
    junk_buf = fx[:, 8192:12288]
    FP = BankPool(k.psF)
    TP = BankPool(k.psT)
    diagonal(dsa_score_chains(k, g, 0, score_buf, FP))
    diagonal(dsa_thr_chains(k, g, 0, score_buf, junk_buf))
    dsa_mask(k, g, 0, score_buf, junk_buf)
    for i in range(NT):
        att = dsa_att_chains(k, g, i, FP, TP)
        if i + 1 < NT:
            aux = dsa_score_chains(k, g, i + 1, score_buf, FP) + [[lambda: None], [lambda: None]] \
                + dsa_thr_chains(k, g, i + 1, score_buf, junk_buf)
            nsc = len(aux) - (NIT + 1 if (g * NT + i + 2) * 128 > TOPK else 1)
            caux = [0.7 if c < nsc else 5.0 for c in range(len(aux))]
            tot_aux = sum(caux)
            merged = []
            ia = 0
            acc = 0.0
            for c in range(len(att)):
                merged.append(att[c])
                target = tot_aux * (c + 1) / len(att)
                while ia < len(aux) and acc + 0.5 * caux[ia] <= target:
                    merged.append(aux[ia])
                    acc += caux[ia]
                    ia += 1
            merged += aux[ia:]
        else:
            merged = att
        diagonal(merged)
        if i + 1 < NT:
            dsa_mask(k, g, i + 1, score_buf, junk_buf)
    k.arena.release()


def dsa_score_chains(k, g, i, score_buf, FP):
    P = k.P
    gi = g * NT + i
    nk = (gi + 1) * 128
    tq = slice(i * 128, (i + 1) * 128)
    score = score_buf[:, 0:nk]
    chains = []
    for kb in range(0, nk, 512):
        kw = min(512, nk - kb)
        for h in range(IDX_HEADS):
            hp, e = h // 2, h % 2
            st8 = {}

            def s1(st8=st8, hp=hp, e=e, kb=kb, kw=kw):
                P.tag = "dsa_score"
                ps = FP.get()
                st8["ps"] = ps
                P.mm(ps[:, 0:kw], k.s_qidx[e * 64:(e + 1) * 64, hp, tq], k.kidxT[e * 64:(e + 1) * 64, kb:kb + kw])

            def s2(st8=st8, kw=kw):
                P.tag = "dsa_score"
                rl = k.rl[k.rl_i % 2].all()
                k.rl_i += 1
                st8["rl"] = rl
                P.act(rl[:, 0:kw], st8["ps"][:, 0:kw], AF.Relu)
                FP.put(st8["ps"])

            def s3(st8=st8, h=h, kb=kb, kw=kw):
                P.tag = "dsa_score"
                rl = st8["rl"]
                if h == 0:
                    P.ts("dve", score[:, kb:kb + kw], rl[:, 0:kw], k.widx[:, i, h:h + 1], ALU.mult)
                else:
                    P.stt("dve", score[:, kb:kb + kw], rl[:, 0:kw], k.widx[:, i, h:h + 1], score[:, kb:kb + kw],
                          ALU.mult, ALU.add)

            chains.append([s1, s2, s3])

    def causal():
        P.tag = "dsa_score"
        P.tt("dve", score[:, gi * 128:nk], score[:, gi * 128:nk], k.causneg.all(), ALU.add)

    chains.append([lambda: None, lambda: None, causal])
    return chains


def dsa_thr_chains(k, g, i, score_buf, junk_buf):
    P = k.P
    gi = g * NT + i
    nk = (gi + 1) * 128
    score = score_buf[:, 0:nk]
    th = k.thr
    chains = []
    if nk > TOPK:
        W = k.thW

        def init():
            P.tag = "dsa_thr"
            P.reduce("dve", th[:, 0:1], score[:, 0:TOPK], ALU.min)
            P.reduce("dve", th[:, 1:2], score, ALU.max)
            P.tt("dve", th[:, 1:2], th[:, 1:2], th[:, 0:1], ALU.subtract)
            P.ts("dve", W.all(), k.pw2.all(), th[:, 1:2], ALU.mult)
            P.tt("dve", th[:, 2:3], th[:, 0:1], W[:, 0:1], ALU.add)

        chains.append([init])
        junk = junk_buf[:, 0:nk]

        def make_it(j):
            def it():
                P.tag = "dsa_thr"
                P.ts("dve", junk, score, th[:, 2:3], ALU.is_ge, 0.0, ALU.add, accum=th[:, 3:4])
                P.stt("dve", th[:, 4:5], th[:, 3:4], float(TOPK), W[:, j:j + 1], ALU.is_ge, ALU.mult)
                if j < NIT - 1:
                    P.stt("dve", th[:, 2:3], th[:, 4:5], W[:, j + 1:j + 2], th[:, 2:3], ALU.subtract, ALU.add)
                else:
                    P.stt("dve", th[:, 0:1], th[:, 4:5], W[:, j:j + 1], th[:, 2:3], ALU.subtract, ALU.add)
            return it

        for j in range(NIT):
            chains.append([make_it(j)])
    else:
        def init0():
            P.tag = "dsa_thr"
            P.memset("dve", th[:, 0:1], -1e29)

        chains.append([init0])
    return chains


def dsa_mask(k, g, i, score_buf, junk_buf):
    P = k.P
    P.tag = "dsa_thr"
    gi = g * NT + i
    nkt = gi + 1
    nk = nkt * 128
    score = score_buf[:, 0:nk]
    mk = junk_buf[:, 0:nk]
    P.ts("dve", mk, score, k.thr[:, 0:1], ALU.is_lt, NEG, ALU.mult)
    maskT = k.s_maskT[:, 0:nk].r("p (j q) -> p j q", j=nkt)
    for j0 in range(0, nkt, 8):
        nj = min(8, nkt - j0)
        pt = next_psT(k)
        for jj in range(nj):
            P.tr(pt[:, jj * 128:(jj + 1) * 128], mk[:, (j0 + jj) * 128:(j0 + jj + 1) * 128], k.ident_b.all())
        P.copy("act", maskT[:, j0:j0 + nj, :], pt[:, 0:nj * 128].r("p (j q) -> p j q", j=nj))


def dsa_att_chains(k, g, i, FP, TP):
    P = k.P
    A = k.arena
    gi = g * NT + i
    nkt = gi + 1
    nk = nkt * 128
    tq = slice(i * 128, (i + 1) * 128)
    maskT = k.s_maskT[:, 0:nk].r("p (j q) -> p j q", j=nkt)
    P.tag = "dsa_att"
    qhall = A.alloc(NH * 128).r("p (h q) -> p h q", h=NH)
    for h in range(NH):
        ps = FP.get()
        for c in range(2):
            P.mm(ps[:, 0:128], k.s_wuq[:, c, h * 128:(h + 1) * 128], k.s_cqT[:, c, tq], start=(c == 0), stop=(c == 1))
        P.copy("act", qhall[:, h, :], ps[:, 0:128])
        FP.put(ps)
    qlall = k.s_ql
    for h in range(NH):
        ps2 = FP.get()
        for rc in range(2):
            P.mm(ps2[:, rc * 128:(rc + 1) * 128], k.s_wukT[:, h, rc * 128:(rc + 1) * 128], qhall[:, h, :])
        P.act(qlall[:, h, :, :], ps2[:, 0:256].r("p (c q) -> p c q", c=2), AF.Copy, scale=float(DK ** -0.5))
        FP.put(ps2)

    def make_chain(h, j0):
        nj = min(4, nkt - j0)
        ql = qlall[:, h, :, :]
        oa = k.psH[h % 2]
        st8 = {}

        def s1():
            P.tag = "dsa_att"
            pl = FP.get()
            st8["pl"] = pl
            for jj in range(nj):
                j = j0 + jj
                dst = pl[:, jj * 128:(jj + 1) * 128]
                P.mm(dst, k.ckvT[:, 0, j * 128:(j + 1) * 128], ql[:, 0, :], start=True, stop=False)
                P.mm(dst, k.ckvT[:, 1, j * 128:(j + 1) * 128], ql[:, 1, :], start=False, stop=False)
                near = k.biasD if j == gi else (k.bias1 if j == gi - 1 else None)
                P.mm(dst, k.ident_b.all(), maskT[:, j, :], start=False, stop=(near is None))
                if near is not None:
                    P.mm(dst, k.ident_b.all(), near[:, h, :], start=False, stop=True)

        def s3():
            P.tag = "dsa_att"
            pl = st8["pl"]
            pT = A.alloc(512).r("p (j q) -> p j q", j=4)
            st8["pT"] = pT
            P.act(pT[:, 0:nj, :], pl[:, 0:nj * 128].r("p (j q) -> p j q", j=nj), AF.Exp)
            FP.put(pl)

        def s4():
            P.tag = "dsa_att"
            pT = st8["pT"]
            for jj in range(nj):
                j = j0 + jj
                P.mm(oa[:, 0:KV_RANK + 1], pT[:, jj, :], k.ckv_tm[:, j, :], start=(j == 0), stop=(j == nkt - 1))

        stages = [s1, s3, s4]
        if j0 + nj >= nkt:
            def e1():
                P.tag = "dsa_att"
                stt = k.stat[h % 2]
                P.recip(stt[:, 0:1], oa[:, KV_RANK:KV_RANK + 1])
                ol = A.alloc(256)
                st8["ol"] = ol
                P.ts("dve", ol, oa[:, 0:KV_RANK], stt[:, 0:1], ALU.mult)

            def e2():
                P.tag = "dsa_att"
                pt = TP.get()
                st8["pt"] = pt
                for rc in range(2):
                    P.tr(pt[:, rc * 128:(rc + 1) * 128], st8["ol"][:, rc * 128:(rc + 1) * 128], k.ident_b.all())

            def e3():
                P.tag = "dsa_att"
                olT = A.alloc(256).r("p (c q) -> p c q", c=2)
                st8["olT"] = olT
                P.copy("act", olT, st8["pt"][:, 0:256].r("p (c q) -> p c q", c=2))
                TP.put(st8["pt"])

            def e4():
                P.tag = "dsa_att"
                pb = TP.get()
                st8["pb"] = pb
                po = pb[:, 0:256].map(lambda ap: ap.bitcast(F32))
                st8["po"] = po
                for rc in range(2):
                    P.mm(po, st8["olT"][:, rc, :], k.s_wuv[:, rc, h * 128:(h + 1) * 128],
                         start=(rc == 0), stop=(rc == 1))

            def e5():
                P.tag = "dsa_att"
                P.copy("act", k.o_sa[:, i, h * 128:(h + 1) * 128], st8["po"])
                TP.put(st8["pb"])

            stages += [e1, e2, e3, e4, e5]
        return stages

    return [make_chain(h, j0) for h in range(NH) for j0 in range(0, nkt, 4)]


def diagonal(chains):
    active = []
    nxt = 0
    while active or nxt < len(chains):
        for ent in list(active):
            ent[0][ent[1]]()
            ent[1] += 1
            if ent[1] >= len(ent[0]):
                active.remove(ent)
        if nxt < len(chains):
            ch = chains[nxt]
            nxt += 1
            ch[0]()
            if len(ch) > 1:
                active.append([ch, 1])


def merge_wo(k, g):
    P = k.P
    A = k.arena
    for which, src in ((0, k.o_dn), (1, k.o_sa)):
        for hh in range(2):
            wv = wload_cols(k, k.w_in, (C_GA0 if which == 0 else C_GB0) + hh * 512, 512)
            for i in range(NT):
                ps = next_psF(k)
                for c in range(8):
                    P.mm(ps.all(), k.xT[:, c, i * 128:(i + 1) * 128], wv[:, c, :], start=(c == 0), stop=(c == 7))
                sg = A.alloc(512, F32)
                P.act(sg, ps.all(), AF.Sigmoid)
                od = k.o_dn[:, i, hh * 512:(hh + 1) * 512]
                if which == 0:
                    P.tt("dve", od, sg, od, ALU.mult)
                else:
                    P.tt("dve", sg, sg, k.o_sa[:, i, hh * 512:(hh + 1) * 512], ALU.mult)
                    P.tt("dve", od, sg, od, ALU.add)
    tap(k, "merged", k.o_dn.all(), [128, NT, 1024], BF16)
    for i in range(NT):
        pt = next_psT(k)
        for c in range(8):
            P.tr(pt[:, c * 128:(c + 1) * 128], k.o_dn[:, i, c * 128:(c + 1) * 128], k.ident_b.all())
        P.copy("act", k.xT[:, :, i * 128:(i + 1) * 128], pt.all().r("p (c t) -> p c t", c=8))
    for hh in range(2):
        wv = wload_cols(k, k.w_o, hh * 512, 512)
        for i in range(NT):
            ps = next_psF(k)
            for c in range(8):
                P.mm(ps.all(), k.xT[:, c, i * 128:(i + 1) * 128], wv[:, c, :], start=(c == 0), stop=(c == 7))
            xs = k.xres[:, i, hh * 512:(hh + 1) * 512]
            P.tt("dve", xs, xs, ps.all(), ALU.add)


def final_norm_store(k, g, do_norm):
    P = k.P
    A = k.arena
    t0 = g * T
    if do_norm:
        gf = A.alloc(D_MODEL, F32)
        sv = k.final_norm.all().map(lambda ap: ap.rearrange("a d -> (a d)").partition_broadcast(128))
        P.dma("pool", [(gf.ap, sv.ap)], [sv], [gf])
    for i in range(NT):
        dst = k.y[t0 + i * 128: t0 + (i + 1) * 128, :]
        if do_norm:
            xt = k.xres[:, i, :]
            st = k.stat[i % 2]
            yo = A.alloc(D_MODEL, F32)
            P.act(yo, xt, AF.Square, accum=st[:, 0:1])
            act_rsqrt(P, st[:, 2:3], st[:, 0:1], 1.0 / D_MODEL, EPS)
            P.stt("dve", yo, xt, st[:, 2:3], gf, ALU.mult, ALU.mult)
            P.dma("pool", [(dst.ap, yo.ap)], [yo], [dst])
        else:
            P.dma("pool", [(dst.ap, k.xres[:, i, :].ap)], [k.xres[:, i, :]], [dst])


def group(k, g, stages):
    P = k.P
    t0 = g * T
    for i in range(NT):
        src = k.x[t0 + i * 128: t0 + (i + 1) * 128, :]
        P.dma("pool", [(k.xres[:, i, :].ap, src.ap)], [src], [k.xres[:, i, :]])
    P.tag = "ffn1"
    if "ffn1" in stages:
        if "mix" in stages and not getattr(k, "bias_built", False):
            k.bias_gen = build_bias_tiles(k)
            ffn(k, k.gcol["ffn1"], k.ffn1_wg, k.ffn1_wu, k.ffn1_wd, bg=k.bias_gen)
            for _ in k.bias_gen:
                pass
            k.bias_built = True
        else:
            ffn(k, k.gcol["ffn1"], k.ffn1_wg, k.ffn1_wu, k.ffn1_wd)
    if "mix" in stages:
        if not getattr(k, "bias_built", False):
            P.tag = "setup"
            for _ in build_bias_tiles(k):
                pass
            k.bias_built = True
        P.tag = "mixproj"
        mixer_proj(k, g)
        if "dsa" in stages:
            P.tag = "dsa"
            dsa(k, g)
            tap(k, "o_sa", k.o_sa.all(), [128, NT, 1024], BF16)
        P.tag = "dn"
        for hq in range(NH // NQ):
            if "dn0" not in stages:
                dn_quad(k, g, hq)
        tap(k, "o_dn", k.o_dn.all(), [128, NT, 1024], BF16)
        P.tag = "wo"
        if "wo" in stages:
            merge_wo(k, g)
    P.tag = "ffn2"
    if "ffn2" in stages:
        ffn(k, k.gcol["ffn2"], k.ffn2_wg, k.ffn2_wu, k.ffn2_wd)
    P.tag = "final"
    final_norm_store(k, g, "final" in stages)


ALL_STAGES = ("ffn1", "mix", "dsa", "wo", "ffn2", "final")

INPUT_ORDER = ["x", "ffn1_norm", "ffn1_wg", "ffn1_wu", "ffn1_wd", "mix_norm", "w_in", "conv_w", "a_log",
               "dt_bias", "dn_out_norm", "q_norm", "kv_norm", "w_uq", "w_uk", "w_uv", "w_iq", "idx_k_g",
               "idx_k_b", "rel_bias", "w_o", "ffn2_norm", "ffn2_wg", "ffn2_wu", "ffn2_wd", "final_norm"]


def make_in_maps(inputs, ncores=8):
    shared = {}
    for n in INPUT_ORDER:
        if n == "x":
            continue
        a = np.asarray(inputs[n], dtype=np.float32)
        if n == "final_norm":
            a = a.reshape(1, D_MODEL)
        elif n == "rel_bias":
            a = a.reshape(1, REL_BUCKETS * NH)
        elif n in ("w_uk", "w_uv"):
            a = a.reshape(KV_RANK, 1024)
        elif n == "conv_w":
            a = a.reshape(CONV_K, 3072)
        else:
            a = a.reshape(a.shape[1:]) if a.shape[0] == 1 and a.ndim == 3 else a
        shared[n] = np.ascontiguousarray(a)
    x = np.asarray(inputs["x"], dtype=np.float32)
    maps = []
    for c in range(ncores):
        m = dict(shared)
        m["x"] = np.ascontiguousarray(x[c])
        maps.append(m)
    return maps


_CACHE = {}


def kernel(**inputs):
    if "k" not in _CACHE:
        _CACHE["k"] = build(ngroups=8, stages=ALL_STAGES)
    k = _CACHE["k"]
    in_maps = make_in_maps(inputs, 8)
    res = run_bass_kernel_spmd(k.nc, in_maps, core_ids=list(range(8)))
    return np.stack([np.asarray(r["y"]) for r in res.results], axis=0).astype(np.float32)
```

```python
import math
import numpy as np
import concourse.bass as bass
import concourse.mybir as mybir
from concourse.bass_utils import run_bass_kernel_spmd

F32 = mybir.dt.float32
BF16 = mybir.dt.bfloat16
AF = mybir.ActivationFunctionType
ALU = mybir.AluOpType
AX = mybir.AxisListType

SAME_ENGINE_SYNC = True
ANNOTATE = False
SEM_LIMIT = 28000

D_MODEL = 1024; SEQ = 4096; D_FF = 2816
NH = 8; DK = 128; DV = 128; CONV_K = 4; CHUNK = 64
Q_RANK = 256; KV_RANK = 256; IDX_HEADS = 8; IDX_DIM = 64; TOPK = 256
REL_BUCKETS = 32; REL_MAX_DIST = 128
EPS = 1e-6
IN_WIDTH = 6744
C_Q0 = 0; C_K0 = 1024; C_V0 = 2048; C_Z0 = 3072; C_B0 = 4096; C_A0 = 4104
C_CQ0 = 4112; C_CKV0 = 4368; C_IK0 = 4624; C_IW0 = 4688; C_GA0 = 4696; C_GB0 = 5720
T = 512; NT = 4
NQ = 4
PTT = "dve"
ARENA_CHUNKS = 29
NIT = 14
NEG = -30000.0
NB_THR = [1, 2, 3, 4, 5, 6, 7, 8, 9, 10, 11, 12, 13, 14, 15, 16, 19, 21, 24, 27, 31, 35, 40, 46, 52, 59, 67, 77, 87, 99, 113]


class Slot:
    __slots__ = ("w", "r")

    def __init__(self):
        self.w = None
        self.r = {}


class View:
    __slots__ = ("buf", "ap", "slots", "aid")

    def __init__(self, buf, ap, slots, aid=None):
        self.buf = buf
        self.ap = ap
        self.slots = slots
        self.aid = aid

    def map(self, fn):
        return View(self.buf, fn(self.ap), self.slots, self.aid)

    def __getitem__(self, idx):
        return View(self.buf, self.ap[idx], self.slots, self.aid)

    def f32(self):
        return self.map(lambda ap: ap.bitcast(F32))

    def r(self, pattern, **kw):
        return self.map(lambda ap: ap.rearrange(pattern, **kw))


class Arena:
    def __init__(self, P, nchunks):
        self.buf = P.buf("arena", [128, nchunks * 1024], BF16, nslots=nchunks, slot_axis=1, slot_size=1024)
        self.n = nchunks
        self.pos = 0
        self.owner = [None] * nchunks
        self.aid = 0
        self.buf.arena = self

    def reserve(self, nch):
        self.lo = nch
        if self.pos < nch:
            self.pos = nch
        self.aid += 1
        for c in range(nch):
            self.owner[c] = self.aid
        v = self.buf[:, 0: nch * 1024]
        v.aid = self.aid
        return v

    def release(self):
        self.lo = 0

    def alloc(self, nelem, dtype=BF16):
        nb = nelem * (4 if dtype == F32 else 2)
        nch = (nb + 2047) // 2048
        lo = getattr(self, "lo", 0)
        assert nch <= self.n - lo
        if self.pos + nch > self.n:
            self.pos = lo
        a = self.pos
        self.pos += nch
        self.aid += 1
        for c in range(a, a + nch):
            self.owner[c] = self.aid
        v = self.buf[:, a * 1024: a * 1024 + nb // 2]
        v.aid = self.aid
        if dtype == F32:
            v = v.f32()
        return v


class Buf:
    def __init__(self, P, name, shape, dtype, space="sbuf", nslots=1, slot_axis=None, kind=None, slot_size=1):
        nc = P.nc
        self.slot_size = slot_size
        self.name = name
        self.shape = list(shape)
        self.dtype = dtype
        self.space = space
        if space == "sbuf":
            self.t = nc.alloc_sbuf_tensor(name, list(shape), dtype)
        elif space == "psum":
            self.t = nc.alloc_psum_tensor(name, list(shape), dtype)
        else:
            self.t = nc.dram_tensor(name, list(shape), dtype, kind=kind or "Internal")
        self.slot_axis = slot_axis
        self.slots = [Slot() for _ in range(nslots)]
        self.dsem = {}

    def _base(self):
        return self.t.ap() if self.space == "dram" else self.t

    def _slots_of(self, idx):
        if self.slot_axis is None:
            return [0]
        if not isinstance(idx, tuple):
            idx = (idx,)
        if len(idx) <= self.slot_axis:
            return list(range(len(self.slots)))
        s = idx[self.slot_axis]
        ss = self.slot_size
        if isinstance(s, int):
            return [s // ss]
        a, b, _ = s.indices(len(self.slots) * ss)
        return list(range(a // ss, (b - 1) // ss + 1))

    def __getitem__(self, idx):
        return View(self, self._base()[idx], self._slots_of(idx))

    def all(self):
        return View(self, self._base()[:], list(range(len(self.slots))))


class EngState:
    def __init__(self, name, sem, is_pe=False):
        self.name = name
        self.sem = sem
        self.count = 0
        self.waited = {}
        self.is_pe = is_pe


class Prog:
    ENG = ("pe", "act", "dve", "pool", "sp")

    def __init__(self, nc):
        self.nc = nc
        self.nsem = 0
        self.eng = {n: EngState(n, self._newsem("s_" + n), n == "pe") for n in self.ENG}
        self.streams = {n: [] for n in self.ENG}
        self.out_tokens = []
        self.tag = "setup"
        self.ninst = {n: 0 for n in self.ENG}

    def _newsem(self, name):
        self.nsem += 1
        return self.nc.alloc_semaphore("%s_%d" % (name, self.nsem))

    def buf(self, name, shape, dtype, **kw):
        return Buf(self, name, shape, dtype, **kw)

    def _deps(self, E, reads, writes):
        need = {}

        def add(tok):
            k = id(tok[0])
            if k not in need or need[k][1] < tok[1]:
                need[k] = tok

        for v in list(reads) + list(writes):
            if v.aid is not None:
                for s in v.slots:
                    assert v.buf.arena.owner[s] == v.aid, "arena buffer reused while live: %s" % v.buf.name
        for v in reads:
            for s in v.slots:
                sl = v.buf.slots[s]
                if sl.w is not None:
                    add(sl.w)
        for v in writes:
            for s in v.slots:
                sl = v.buf.slots[s]
                if sl.w is not None:
                    add(sl.w)
                for tok in sl.r.values():
                    add(tok)
        out = []
        for k, (sem, val) in need.items():
            if sem is E.sem and (E.is_pe or not SAME_ENGINE_SYNC):
                continue
            if E.waited.get(k, 0) >= val:
                continue
            E.waited[k] = val
            out.append((sem, val))
        return out

    def _mark(self, tok, reads, writes):
        k = id(tok[0])
        for v in reads:
            for s in v.slots:
                v.buf.slots[s].r[k] = tok
        for v in writes:
            for s in v.slots:
                sl = v.buf.slots[s]
                sl.w = tok
                sl.r = {}

    def op(self, ename, fn, reads, writes):
        E = self.eng[ename]
        waits = self._deps(E, reads, writes)
        if E.count >= SEM_LIMIT:
            E.sem = self._newsem("s_" + ename)
            E.count = 0
        E.count += 1
        tok = (E.sem, E.count)
        self.streams[ename].append((waits, [fn], E.sem, 1, self.tag))
        self.ninst[ename] += 1 + len(waits)
        self._mark(tok, reads, writes)

    NDSEM = 14

    def dma(self, qname, pairs, reads, writes, fresh=False, **kw):
        E = self.eng[qname]
        waits = self._deps(E, reads, writes)
        if fresh:
            sem = self._newsem("once")
            fns = [(lambda e, o=o, i=i: e.dma_start(out=o, in_=i, **kw)) for (o, i) in pairs]
            tok = (sem, 16 * len(pairs))
            self.streams[qname].append((waits, fns, sem, 16, self.tag))
            self.ninst[qname] += len(pairs) + len(waits)
            self._mark(tok, reads, writes)
            self._need = []
            self._sim(qname, tok, writes[0], reads, dma_bytes=1000000) if hasattr(self, "_sim") else None
            return
        if not hasattr(self, "dpool"):
            self.dpool = [[self._newsem("dma"), 0] for _ in range(self.NDSEM)]
            self.dpool_i = 0
        idx = self.dpool_i % self.NDSEM
        self.dpool_i += 1
        ds = self.dpool[idx]
        if ds[1] + 16 * len(pairs) > SEM_LIMIT:
            ds = [self._newsem("dma"), 0]
            self.dpool[idx] = ds
        if ds[1] > 0 and E.waited.get(id(ds[0]), 0) < ds[1]:
            E.waited[id(ds[0])] = ds[1]
            waits.append((ds[0], ds[1]))
        fns = [(lambda e, o=o, i=i: e.dma_start(out=o, in_=i, **kw)) for (o, i) in pairs]
        ds[1] += 16 * len(pairs)
        tok = (ds[0], ds[1])
        self.streams[qname].append((waits, fns, ds[0], 16, self.tag))
        self.ninst[qname] += len(pairs) + len(waits)
        self._mark(tok, reads, writes)
        if writes[0].buf.space == "dram":
            self.out_tokens.append(tok)

    def raw(self, ename, fns):
        self.streams[ename].append(([], fns, None, None, self.tag))
        self.ninst[ename] += len(fns)

    def finish(self, ename="pool"):
        last = {}
        for sem, val in self.out_tokens:
            if id(sem) not in last or last[id(sem)][1] < val:
                last[id(sem)] = (sem, val)
        self.streams[ename].append((list(last.values()), [], None, 0, "fin"))

    def emit(self):
        nc = self.nc

        def run(e, stream):
            for waits, fns, sem, inc, tag in stream:
                for (s, v) in waits:
                    e.wait_ge(s, v)
                if inc is None:
                    for fn in fns:
                        fn(e)
                    continue
                for fn in fns:
                    ins = fn(e).then_inc(sem, inc)
                    if ANNOTATE:
                        ins.annotate(tag)

        with nc.Block() as block:
            @block.tensor
            def _(e):
                run(e, self.streams["pe"])

            @block.scalar
            def _(e):
                run(e, self.streams["act"])

            @block.vector
            def _(e):
                run(e, self.streams["dve"])

            @block.gpsimd
            def _(e):
                run(e, self.streams["pool"])

            @block.sync
            def _(e):
                run(e, self.streams["sp"])

    def mm(self, out, lhsT, rhs, start=True, stop=True, **kw):
        self.op("pe", lambda e: e.matmul(out.ap, lhsT.ap, rhs.ap, start=start, stop=stop, **kw),
                [lhsT, rhs], [out])

    def tr(self, out, in_, ident):
        self.op("pe", lambda e: e.transpose(out.ap, in_.ap, ident.ap), [in_, ident], [out])

    def act(self, out, in_, func, bias=None, scale=None, accum=None):
        reads = [in_]
        writes = [out]
        kw = {}
        if bias is not None:
            if isinstance(bias, View):
                reads.append(bias)
                kw["bias"] = bias.ap
            else:
                kw["bias"] = bias
        if scale is not None:
            if isinstance(scale, View):
                reads.append(scale)
                kw["scale"] = scale.ap
            else:
                kw["scale"] = scale
        if accum is not None:
            writes.append(accum)
            kw["accum_out"] = accum.ap
        self.op("act", lambda e: e.activation(out=out.ap, in_=in_.ap, func=func, **kw), reads, writes)

    def ts(self, eng, out, in0, s1, op0, s2=None, op1=None, accum=None):
        reads = [in0]
        writes = [out]
        a1 = s1.ap if isinstance(s1, View) else s1
        a2 = s2.ap if isinstance(s2, View) else s2
        if isinstance(s1, View):
            reads.append(s1)
        if isinstance(s2, View):
            reads.append(s2)
        kw = {}
        if op1 is not None:
            kw["op1"] = op1
        if accum is not None:
            writes.append(accum)
            kw["accum_out"] = accum.ap
        self.op(eng, lambda e: e.tensor_scalar(out.ap, in0.ap, a1, a2, op0, **kw), reads, writes)

    def tt(self, eng, out, in0, in1, op):
        self.op(eng, lambda e: e.tensor_tensor(out.ap, in0.ap, in1.ap, op), [in0, in1], [out])

    def stt(self, eng, out, in0, scalar, in1, op0, op1):
        reads = [in0, in1]
        a = scalar.ap if isinstance(scalar, View) else scalar
        if isinstance(scalar, View):
            reads.append(scalar)
        self.op(eng, lambda e: e.scalar_tensor_tensor(out.ap, in0.ap, a, in1.ap, op0, op1), reads, [out])

    def copy(self, eng, out, in_):
        if eng == "act":
            self.op(eng, lambda e: e.copy(out.ap, in_.ap), [in_], [out])
        else:
            self.op(eng, lambda e: e.tensor_copy(out.ap, in_.ap), [in_], [out])

    def memset(self, eng, out, val):
        self.op(eng, lambda e: e.memset(out.ap, val), [], [out])

    def reduce(self, eng, out, in_, op, axis=AX.X):
        self.op(eng, lambda e: e.tensor_reduce(out.ap, in_.ap, axis, op), [in_], [out])

    def recip(self, out, in_):
        self.op("dve", lambda e: e.reciprocal(out.ap, in_.ap), [in_], [out])


class K:
    pass


def bc_last(v, n):
    return v.map(lambda ap: ap.unsqueeze(2).to_broadcast([ap.shape[0], ap.shape[1], n]))


def bc_mid(v, n):
    return v.map(lambda ap: ap.unsqueeze(1).to_broadcast([ap.shape[0], n, ap.shape[1]]))


def build(ngroups=8, stages=("ffn1",), taps=(), glist=None):
    nc = bass.Bass("TRN2", target_bir_lowering=False)
    P = Prog(nc)
    k = K()
    k.P = P
    k.taps = {}
    k.tapset = set(taps)
    k.stages = stages

    def din(name, shape):
        return P.buf(name, shape, F32, space="dram", kind="ExternalInput")

    k.x = din("x", [SEQ, D_MODEL])
    k.ffn1_norm = din("ffn1_norm", [1, D_MODEL])
    k.ffn1_wg = din("ffn1_wg", [D_MODEL, D_FF])
    k.ffn1_wu = din("ffn1_wu", [D_MODEL, D_FF])
    k.ffn1_wd = din("ffn1_wd", [D_FF, D_MODEL])
    k.mix_norm = din("mix_norm", [1, D_MODEL])
    k.w_in = din("w_in", [D_MODEL, IN_WIDTH])
    k.conv_w = din("conv_w", [CONV_K, 3072])
    k.a_log = din("a_log", [1, NH])
    k.dt_bias = din("dt_bias", [1, NH])
    k.dn_out_norm = din("dn_out_norm", [1, DV])
    k.q_norm = din("q_norm", [1, Q_RANK])
    k.kv_norm = din("kv_norm", [1, KV_RANK])
    k.w_uq = din("w_uq", [Q_RANK, 1024])
    k.w_uk = din("w_uk", [KV_RANK, 1024])
    k.w_uv = din("w_uv", [KV_RANK, 1024])
    k.w_iq = din("w_iq", [Q_RANK, 512])
    k.idx_k_g = din("idx_k_g", [1, IDX_DIM])
    k.idx_k_b = din("idx_k_b", [1, IDX_DIM])
    k.rel_bias = din("rel_bias", [1, REL_BUCKETS * NH])
    k.w_o = din("w_o", [D_MODEL, D_MODEL])
    k.ffn2_norm = din("ffn2_norm", [1, D_MODEL])
    k.ffn2_wg = din("ffn2_wg", [D_MODEL, D_FF])
    k.ffn2_wu = din("ffn2_wu", [D_MODEL, D_FF])
    k.ffn2_wd = din("ffn2_wd", [D_FF, D_MODEL])
    k.final_norm = din("final_norm", [1, D_MODEL])
    k.y = P.buf("y", [SEQ, D_MODEL], F32, space="dram", kind="ExternalOutput", nslots=SEQ // 128, slot_axis=0,
                slot_size=128)

    k.xres = P.buf("xres", [128, NT, D_MODEL], F32, nslots=NT, slot_axis=1)
    k.xn = [P.buf("xn0", [128, D_MODEL], BF16)] * 2
    k.xT = P.buf("xT", [128, 8, T], BF16)
    k.psF = [P.buf("psF%d" % i, [128, 512], F32, space="psum") for i in range(4)]
    k.psH = [P.buf("psH%d" % i, [128, 512], F32, space="psum") for i in range(2)]
    k.psF_i = 0
    k.psT = [P.buf("psT%d" % i, [128, 1024], BF16, space="psum") for i in range(2)]
    k.psT_i = 0
    k.stat = [P.buf("stat%d" % i, [128, 8], F32) for i in range(2)]
    k.gcol = {n: P.buf("gcol_" + n, [128, 8], F32) for n in ("ffn1", "mix", "ffn2")}
    k.ident_f = P.buf("ident_f", [128, 128], F32)
    k.dmat = P.buf("dmat", [128, 128], F32)
    k.ident_b = P.buf("ident_b", [128, 128], BF16)
    for n in ("TRIU", "TRIL", "STRICT", "BLOCK", "SELC0", "SELC1", "ONES"):
        setattr(k, n, P.buf("m_" + n, [128, 128], F32))
    k.zg = P.buf("zg", [128, NT, 1024], BF16)
    k.o_dn = P.buf("o_dn", [128, NT, 1024], BF16)
    k.o_sa = P.buf("o_sa", [128, NT, 1024], BF16)
    k.S = P.buf("S", [128, NH, DV], F32, nslots=2, slot_axis=1, slot_size=4)
    k.Sb = P.buf("Sb", [128, NH, DV], BF16, nslots=2, slot_axis=1, slot_size=4)
    k.halo = P.buf("halo", [128, 24, 3], F32, nslots=24, slot_axis=1)
    k.convw = P.buf("convw", [128, 24, 4], F32)
    k.dtb = P.buf("dtb", [128, NH], F32)
    k.nega = P.buf("nega", [128, NH], F32)
    k.gn_dn = P.buf("gn_dn", [128, DV], F32)
    k.beta = P.buf("beta", [128, NT, NH], F32)
    k.negb = P.buf("negb", [128, NT, NH], F32)
    k.gtok = P.buf("gtok", [128, NT, NH], F32)
    k.gst = P.buf("gst", [128, NT, 32], F32)
    k.egc = P.buf("egc", [128, NT, NH], F32)
    k.elast = P.buf("elast", [128, NT, NH], F32)
    k.bge = P.buf("bge", [128, NT, NH], F32)
    k.dec = P.buf("dec", [128, NT, 16], F32)
    k.ph = P.buf("phase", [128, 16384], BF16, nslots=32, slot_axis=1, slot_size=512)

    def phv(off, n):
        return k.ph[:, off:off + n]

    k.q_qT = phv(0, 2048).r("p (h t) -> p h t", h=NQ)
    k.q_kT = phv(2048, 2048).r("p (h t) -> p h t", h=NQ)
    k.q_vtm = phv(4096, 2048).r("p (i h d) -> p i h d", i=NT, h=NQ)
    k.q_ktm = phv(6144, 2048).r("p (i h d) -> p i h d", i=NT, h=NQ)
    k.q_X = [phv(8192 + i * 512, 512).r("p (h t) -> p h t", h=NQ) for i in range(NT)]
    k.q_XT = [phv(10240 + i * 512, 512).r("p (h t) -> p h t", h=NQ) for i in range(NT)]
    k.q_AT = [phv(12288 + i * 512, 512).r("p (h t) -> p h t", h=NQ) for i in range(NT)]
    k.q_IT = [phv(14336 + i * 512, 512).r("p (h t) -> p h t", h=NQ) for i in range(NT)]
    k.s_qidx = phv(0, 2048).r("p (h t) -> p h t", h=4)
    k.s_wuq = phv(2048, 2048).r("p (c f) -> p c f", c=2)
    k.s_wukT = phv(4096, 2048).r("p (h r) -> p h r", h=NH)
    k.s_wuv = phv(6144, 2048).r("p (c f) -> p c f", c=2)
    k.s_wiq = phv(8192, 1024).r("p (c f) -> p c f", c=2)
    k.s_cqT = phv(9216, 1024).r("p (c t) -> p c t", c=2)
    k.s_maskT = phv(10240, 4096)
    k.s_ql = phv(14336, 2048).r("p (h c q) -> p h c q", h=NH, c=2)
    k.rl = [P.buf("rl%d" % i, [128, 512], F32) for i in range(2)]
    k.rl_i = 0
    k.bb_rb = phv(0, 512).f32()
    k.bb_dl = phv(512, 512).f32()
    k.bb_dd = phv(1024, 256).f32()
    k.bb_ind = phv(1536, 256).f32()
    k.bb_acc = phv(2048, 2048).f32().r("p (h q) -> p h q", h=NH)
    NTT = SEQ // 128
    k.ckv_tm = P.buf("ckv_tm", [128, NTT, KV_RANK + 1], BF16, nslots=NTT, slot_axis=1)
    k.ckvT = P.buf("ckvT", [128, 2, SEQ], BF16, nslots=NTT, slot_axis=2, slot_size=128)
    k.kidxT = P.buf("kidxT", [128, SEQ], BF16, nslots=NTT, slot_axis=1, slot_size=128)
    k.biasD = P.buf("biasD", [128, NH, 128], BF16)
    k.bias1 = P.buf("bias1", [128, NH, 128], BF16)
    k.causneg = P.buf("causneg", [128, 128], F32)
    k.qn_bc = P.buf("qn_bc", [128, Q_RANK], F32)
    k.kvn_bc = P.buf("kvn_bc", [128, KV_RANK], F32)
    k.ikg_bc = P.buf("ikg_bc", [128, IDX_DIM], F32)
    k.ikb_bc = P.buf("ikb_bc", [128, IDX_DIM], F32)
    k.widx = P.buf("widx", [128, NT, IDX_HEADS], F32)
    k.thr = P.buf("thr", [128, 8], F32)
    k.thW = P.buf("thW", [128, NIT + 1], F32)
    k.pw2 = P.buf("pw2", [128, NIT + 1], F32)
    k.arena = Arena(P, ARENA_CHUNKS)

    convert_weights(k)
    setup(k)
    for g in (glist if glist is not None else range(ngroups)):
        group(k, g, stages)
    P.finish()
    P.emit()
    k.nc = nc
    return k


class BankPool:
    def __init__(self, banks):
        self.free = list(banks)

    def get(self):
        assert self.free, "PSUM bank pool exhausted"
        return self.free.pop(0)

    def put(self, b):
        self.free.append(b)


def act_rsqrt(P, out, in_, scale, eps):
    P.act(out, in_, AF.Ln, bias=eps, scale=scale)
    P.act(out, out, AF.Exp, scale=-0.5)


def act_sigmoid(P, out, in_):
    P.act(out, in_, AF.Exp, scale=-1.0)
    P.act(out, out, AF.Ln, bias=1.0)
    P.act(out, out, AF.Exp, scale=-1.0)


def next_psF(k):
    b = k.psF[k.psF_i % len(k.psF)]
    k.psF_i += 1
    return b


def next_psT(k):
    b = k.psT[k.psT_i % len(k.psT)]
    k.psT_i += 1
    return b


def tap(k, name, view, shape, dtype=F32):
    if name not in k.tapset:
        return
    cnt = k.taps.get(name, 0)
    k.taps[name] = cnt + 1
    d = k.P.buf("tap_%s_%d" % (name, cnt), shape, dtype, space="dram", kind="ExternalOutput")
    k.P.dma("sp", [(d.all().ap, view.ap)], [view], [d.all()])


def convert_weights(k):
    P = k.P
    nc = P.nc
    P.tag = "convert"

    W = {}
    order = ("ffn1_wg", "ffn1_wu", "ffn1_wd", "w_in", "w_uq", "w_uv", "w_iq", "w_uk", "w_o", "ffn2_wg", "ffn2_wu", "ffn2_wd")
    for nm in order:
        src = getattr(k, nm)
        W[nm] = P.buf(nm + "_bf", src.shape, BF16, space="dram")
        P.dma("pool", [(W[nm].all().ap, src.all().ap)], [src.all()], [W[nm].all()], fresh=True)
    for nm, b in W.items():
        setattr(k, nm, b)


def setup(k):
    P = k.P
    dm = k.dmat
    P.op("pool", lambda e: e.iota(dm.all().ap, [[1, 128]], base=0, channel_multiplier=-1,
                                  allow_small_or_imprecise_dtypes=True), [], [dm.all()])
    P.memset("dve", k.BLOCK.all(), 0.0)
    P.memset("dve", k.BLOCK[0:64, 0:64], 1.0)
    P.memset("dve", k.BLOCK[64:128, 64:128], 1.0)
    P.memset("dve", k.ONES.all(), 1.0)
    P.memset("dve", k.SELC0.all(), 0.0)
    P.memset("dve", k.SELC0[0:64, :], 1.0)
    P.memset("dve", k.SELC1.all(), 0.0)
    P.memset("dve", k.SELC1[64:128, :], 1.0)
    P.ts("dve", k.TRIU.all(), dm.all(), 0.0, ALU.is_ge)
    P.tt("dve", k.TRIU.all(), k.TRIU.all(), k.BLOCK.all(), ALU.mult)
    P.ts("dve", k.TRIL.all(), dm.all(), 0.0, ALU.is_le)
    P.tt("dve", k.TRIL.all(), k.TRIL.all(), k.BLOCK.all(), ALU.mult)
    P.ts("dve", k.STRICT.all(), dm.all(), 0.0, ALU.is_lt)
    P.tt("dve", k.STRICT.all(), k.STRICT.all(), k.BLOCK.all(), ALU.mult)
    P.ts("dve", k.ident_f.all(), dm.all(), 0.0, ALU.is_equal)
    P.copy("dve", k.ident_b.all(), k.ident_f.all())
    for n, src in (("ffn1", k.ffn1_norm), ("mix", k.mix_norm), ("ffn2", k.ffn2_norm)):
        sv = src.all().r("a (c p) -> p (a c)", p=128)
        P.dma("sp", [(k.gcol[n].all().ap, sv.ap)], [sv], [k.gcol[n].all()], allow_slow_non_contiguous=True)

    def bcast(dst, src):
        sv = src.all().map(lambda ap: ap.rearrange("a d -> (a d)").partition_broadcast(128))
        P.dma("sp", [(dst.all().ap, sv.ap)], [sv], [dst.all()])

    bcast(k.dtb, k.dt_bias)
    bcast(k.nega, k.a_log)
    P.act(k.nega.all(), k.nega.all(), AF.Exp)
    P.ts("dve", k.nega.all(), k.nega.all(), -1.0, ALU.mult)
    bcast(k.gn_dn, k.dn_out_norm)
    for kk in range(CONV_K):
        sv = k.conv_w[kk:kk + 1, :].r("a (c p) -> p (a c)", p=128)
        P.dma("sp", [(k.convw[:, :, kk].ap, sv.ap)], [sv], [k.convw.all()], allow_slow_non_contiguous=True)
    P.memset("dve", k.halo.all(), 0.0)
    for j in range(NIT + 1):
        P.memset("dve", k.pw2[:, j:j + 1], 2.0 ** -(j + 1))
    bcast(k.qn_bc, k.q_norm)
    bcast(k.kvn_bc, k.kv_norm)
    bcast(k.ikg_bc, k.idx_k_g)
    bcast(k.ikb_bc, k.idx_k_b)
    P.memset("dve", k.ckv_tm[:, :, KV_RANK:KV_RANK + 1], 1.0)
    P.ts("dve", k.causneg.all(), dm.all(), 0.0, ALU.is_gt, -1e30, ALU.mult)
    P.memset("dve", k.S.all(), 0.0)
    P.memset("dve", k.Sb.all(), 0.0)


def build_bias_tiles(k):
    P = k.P
    dm = k.dmat
    rb = k.bb_rb
    sv = k.rel_bias.all().map(lambda ap: ap.rearrange("a d -> (a d)").partition_broadcast(128))
    P.dma("sp", [(rb.ap, sv.ap)], [sv], [rb])
    dl = k.bb_dl
    P.tt("dve", dl[:, NH:REL_BUCKETS * NH], rb[:, NH:REL_BUCKETS * NH], rb[:, 0:(REL_BUCKETS - 1) * NH], ALU.subtract)
    P.tt("dve", dl[:, 0:NH], rb[:, 0:NH], rb[:, (REL_BUCKETS - 1) * NH:REL_BUCKETS * NH], ALU.subtract)
    for which, dst in ((0, k.biasD), (1, k.bias1)):
        dd = k.bb_dd
        P.ts("dve", dd, dm.all(), 128.0 * which, ALU.add)
        acc = k.bb_acc
        for h in range(NH):
            P.ts("dve", acc[:, h, :], dd, 0.0, ALU.mult, dl[:, h:h + 1], ALU.add)
        ind = k.bb_ind
        for b in range(1, REL_BUCKETS):
            P.ts("dve", ind, dd, float(NB_THR[b - 1]), ALU.is_ge)
            for h in range(NH):
                P.stt("dve", acc[:, h, :], ind, dl[:, b * NH + h:b * NH + h + 1], acc[:, h, :], ALU.mult, ALU.add)
            yield
        P.copy("dve", dst.all(), acc)


def rms_to_T(k, i, gcol, dstT):
    P = k.P
    xt = k.xres[:, i, :]
    st = k.stat[i % 2]
    xn = k.xn[i % 2]
    P.act(xn.all(), xt, AF.Square, accum=st[:, 0:1])
    act_rsqrt(P, st[:, 2:3], st[:, 0:1], 1.0 / D_MODEL, EPS)
    P.ts("dve", xn.all(), xt, st[:, 2:3], ALU.mult)
    pt = next_psT(k)
    for c in range(8):
        P.tr(pt[:, c * 128:(c + 1) * 128], xn[:, c * 128:(c + 1) * 128], k.ident_b.all())
    P.tt("dve", dstT[:, :, i * 128:(i + 1) * 128], pt.all().r("p (c t) -> p c t", c=8),
         bc_last(gcol.all(), 128), ALU.mult)


def wdma(k, dst_view, src_view):
    k.P.dma("sp", [(dst_view.ap, src_view.ap)], [src_view], [dst_view])


def wload_cols(k, w, c0, ncols):
    wt = k.arena.alloc(8 * ncols).r("p (c f) -> p c f", c=8)
    wdma(k, wt, w[:, c0:c0 + ncols].r("(c p) f -> p c f", p=128))
    return wt


def ffn(k, gcol, wg, wu, wd, bg=None):
    P = k.P
    for i in range(NT):
        rms_to_T(k, i, gcol, k.xT)
    nblk = (D_FF + 511) // 512

    def gate_up(fb):
        f0 = fb * 512
        fw = min(512, D_FF - f0)
        nfc = fw // 128
        wg_v = wload_cols(k, wg, f0, fw)
        wu_v = wload_cols(k, wu, f0, fw)
        hT = k.arena.alloc(nfc * T).r("p (c t) -> p c t", c=nfc)
        for fc in range(nfc):
            pg = next_psF(k)
            pu = next_psF(k)
            for c in range(8):
                P.mm(pg.all(), wg_v[:, c, fc * 128:(fc + 1) * 128], k.xT[:, c, :], start=(c == 0), stop=(c == 7))
            for c in range(8):
                P.mm(pu.all(), wu_v[:, c, fc * 128:(fc + 1) * 128], k.xT[:, c, :], start=(c == 0), stop=(c == 7))
            sg = k.rl[k.rl_i % 2].all()
            k.rl_i += 1
            P.act(sg, pg.all(), AF.Silu)
            P.tt("dve", hT[:, fc, :], sg, pu.all(), ALU.mult)
        return (hT, f0, fw, nfc)

    def down(hT, f0, fw, nfc):
        wd_v = k.arena.alloc(nfc * 1024).r("p (c d) -> p c d", c=nfc)
        wdma(k, wd_v, wd[f0:f0 + fw, :].r("(c p) d -> p c d", p=128))
        for i in range(NT):
            for hh in range(2):
                po = k.psH[(2 * i + hh) % 2]
                for fc in range(nfc):
                    P.mm(po.all(), hT[:, fc, i * 128:(i + 1) * 128], wd_v[:, fc, hh * 512:(hh + 1) * 512],
                         start=(fc == 0), stop=(fc == nfc - 1))
                xs = k.xres[:, i, hh * 512:(hh + 1) * 512]
                P.stt("dve", xs, po.all(), 0.5, xs, ALU.mult, ALU.add)

    def advance(n):
        if bg is not None:
            for _ in range(n):
                if next(bg, "end") == "end":
                    break

    prev = None
    for fb in range(nblk):
        cur = gate_up(fb)
        advance(6)
        if prev is not None:
            down(*prev)
        advance(6)
        prev = cur
    down(*prev)


def mixer_proj(k, g):
    P = k.P
    for i in range(NT):
        rms_to_T(k, i, k.gcol["mix"], k.xT)
    for hh in range(2):
        wz = wload_cols(k, k.w_in, C_Z0 + hh * 512, 512)
        for i in range(NT):
            ps = next_psF(k)
            for c in range(8):
                P.mm(ps.all(), k.xT[:, c, i * 128:(i + 1) * 128], wz[:, c, :], start=(c == 0), stop=(c == 7))
            sg = k.rl[k.rl_i % 2].all()
            k.rl_i += 1
            P.act(sg, ps.all(), AF.Silu)
            P.tt("dve", k.zg[:, i, hh * 512:(hh + 1) * 512].r("p (h d) -> p h d", h=4),
                 sg.r("p (h d) -> p h d", h=4), bc_mid(k.gn_dn.all(), 4), ALU.mult)
    k.smallp = k.arena.alloc(NT * 600, F32).r("p (i c) -> p i c", i=NT)
    w1 = wload_cols(k, k.w_in, 4096, 512)
    w2 = wload_cols(k, k.w_in, 4608, 88)
    for i in range(NT):
        ps = next_psF(k)
        for c in range(8):
            P.mm(ps.all(), k.xT[:, c, i * 128:(i + 1) * 128], w1[:, c, :], start=(c == 0), stop=(c == 7))
        P.copy("act", k.smallp[:, i, 0:512], ps.all())
        ps2 = next_psF(k)
        for c in range(8):
            P.mm(ps2[:, 0:88], k.xT[:, c, i * 128:(i + 1) * 128], w2[:, c, :], start=(c == 0), stop=(c == 7))
        P.copy("act", k.smallp[:, i, 512:600], ps2[:, 0:88])
    act_sigmoid(P, k.beta.all(), k.smallp[:, :, 0:8])
    P.ts("dve", k.negb.all(), k.beta.all(), -1.0, ALU.mult)
    P.tt("dve", k.gtok.all(), k.smallp[:, :, 8:16], bc_mid(k.dtb.all(), NT), ALU.add)
    P.act(k.gtok.all(), k.gtok.all(), AF.Exp)
    P.act(k.gtok.all(), k.gtok.all(), AF.Ln, bias=1.0)
    P.tt("dve", k.gtok.all(), k.gtok.all(), bc_mid(k.nega.all(), NT), ALU.mult)
    ps = next_psF(k)
    for i in range(NT):
        for j, m in enumerate((k.TRIU, k.BLOCK, k.SELC0, k.SELC1)):
            P.mm(ps[:, i * 32 + j * 8: i * 32 + j * 8 + 8], m.all(), k.gtok[:, i, :])
    P.copy("act", k.gst.all(), ps[:, 0:NT * 32].r("p (i c) -> p i c", i=NT))
    P.act(k.egc.all(), k.gst[:, :, 0:8], AF.Exp)
    P.tt("dve", k.elast.all(), k.gst[:, :, 8:16], k.gst[:, :, 0:8], ALU.subtract)
    P.act(k.elast.all(), k.elast.all(), AF.Exp)
    P.act(k.dec.all(), k.gst[:, :, 16:32], AF.Exp)
    P.tt("dve", k.bge.all(), k.beta.all(), k.egc.all(), ALU.mult)
    tap(k, "beta", k.beta.all(), [128, NT, NH])
    tap(k, "gtok", k.gtok.all(), [128, NT, NH])
    dsa_prep(k, g)


def dn_quad(k, g, hq):
    P = k.P
    A = k.arena
    hs = hq * NQ
    P.tag = "dnA"
    qT, kT, v_tm, k_tm = k.q_qT, k.q_kT, k.q_vtm, k.q_ktm
    hsl = slice(hs, hs + NQ)
    r4 = lambda v: v.r("p (h t) -> p h t", h=NQ)
    rd = lambda v: v.r("p (h d) -> p h d", h=NQ)
    FP = BankPool(k.psF + k.psH)
    TP = BankPool(k.psT)

    wvs = {}

    def chainA(kind, hl):
        ch = kind * 8 + hs + hl
        st8 = {}

        def a1():
            if hl == 0:
                wvs[kind] = wload_cols(k, k.w_in, kind * 1024 + hs * 128, NQ * 128)
            wv = wvs[kind]
            ps = FP.get()
            st8["ps"] = ps
            for c in range(8):
                P.mm(ps.all(), wv[:, c, hl * 128:(hl + 1) * 128], k.xT[:, c, :], start=(c == 0), stop=(c == 7))

        def a2a():
            P.tag = "dnA"
            ps = st8["ps"]
            cb = A.alloc(1024, F32)
            st8["cb"] = cb
            P.copy("act", cb[:, 0:3], k.halo[:, ch, :])
            P.copy("act", cb[:, 3:3 + T], ps.all())
            FP.put(ps)
            P.copy("act", k.halo[:, ch, :], cb[:, T:T + 3])

        def a2b():
            P.tag = "dnA"
            cb = st8["cb"]
            yv = A.alloc(T, F32)
            st8["yv"] = yv
            P.ts("dve", yv, cb[:, 3:3 + T], k.convw[:, ch, 3:4], ALU.mult)
            for kk in range(3):
                P.stt("dve", yv, cb[:, kk:kk + T], k.convw[:, ch, kk:kk + 1], yv, ALU.mult, ALU.add)

        def a2c():
            P.tag = "dnA"
            sv = A.alloc(T, F32)
            st8["sv"] = sv
            act_sigmoid(P, sv, st8["yv"])

        def a2d():
            P.tag = "dnA"
            cb, yv, sv = st8["cb"], st8["yv"], st8["sv"]
            if kind == 2:
                sb = cb[:, 0:T // 2].map(lambda ap: ap.bitcast(BF16))
                st8["sb"] = sb
                P.tt("dve", sb, sv, yv, ALU.mult)
            else:
                P.tt("dve", sv, sv, yv, ALU.mult)
                sq = cb[:, 0:T]
                st8["sq"] = sq
                P.act(sq, sv, AF.Square)

        def a3():
            if kind == 2:
                pt = TP.get()
                st8["pt"] = pt
                for i in range(NT):
                    P.tr(pt[:, i * 128:(i + 1) * 128], st8["sb"][:, i * 128:(i + 1) * 128], k.ident_b.all())
            else:
                pss = FP.get()
                st8["pss"] = pss
                P.mm(pss.all(), k.ONES.all(), st8["sq"])

        def a4():
            if kind == 2:
                P.copy("act", v_tm[:, :, hl, :], st8["pt"][:, 0:NT * 128].r("p (i d) -> p i d", i=NT))
                TP.put(st8["pt"])
            else:
                rn = st8["yv"]
                act_rsqrt(P, rn, st8["pss"].all(), 1.0, EPS)
                FP.put(st8["pss"])
                dst = (qT if kind == 0 else kT)[:, hl, :]
                st8["dst"] = dst
                if kind == 0:
                    P.stt("dve", dst, st8["sv"], DK ** -0.5, rn, ALU.mult, ALU.mult)
                else:
                    P.tt("dve", dst, st8["sv"], rn, ALU.mult)

        def a5():
            pt = TP.get()
            st8["pt"] = pt
            for i in range(NT):
                P.tr(pt[:, i * 128:(i + 1) * 128], st8["dst"][:, i * 128:(i + 1) * 128], k.ident_b.all())

        def a6():
            P.copy("act", k_tm[:, :, hl, :], st8["pt"][:, 0:NT * 128].r("p (i d) -> p i d", i=NT))
            TP.put(st8["pt"])

        stages = [a1, a2a, a2b, a2c, a2d, a3, a4]
        if kind == 1:
            stages += [a5, a6]
        return stages

    diagonal([chainA(kind, hl) for kind in (1, 0, 2) for hl in range(NQ)])
    tap(k, "qT", qT, [128, NQ, T], BF16)
    tap(k, "kT", kT, [128, NQ, T], BF16)
    tap(k, "v_tm", v_tm, [128, NT, NQ, 128], BF16)

    P.tag = "dnB"
    X = k.q_X; XT = k.q_XT; AT = k.q_AT; IT = k.q_IT

    def chainBC(i):
        st8 = {}

        def b1():
            lg = r4(A.alloc(NQ * 128, F32))
            st8["lg"] = lg
            P.tt("dve", lg, bc_mid(k.TRIU.all(), NQ), bc_last(k.gtok[:, i, hsl], 128), ALU.mult)

        def b2():
            psG = FP.get()
            st8["psG"] = psG
            for hl in range(NQ):
                P.mm(psG[:, hl * 128:(hl + 1) * 128], st8["lg"][:, hl, :], k.STRICT.all())
            psK = FP.get()
            st8["psK"] = psK
            for hl in range(NQ):
                kt = kT[:, hl, i * 128:(i + 1) * 128]
                P.mm(psK[:, hl * 128:(hl + 1) * 128], kt, kt)

        def b3():
            E = st8["lg"]
            P.act(E, r4(st8["psG"].all()), AF.Exp)
            Ds = r4(A.alloc(NQ * 128, F32))
            P.tt(PTT, Ds, E, bc_mid(k.STRICT.all(), NQ), ALU.mult)
            P.tt(PTT, Ds, Ds, bc_last(k.negb[:, i, hsl], 128), ALU.mult)
            P.tt("dve", X[i], r4(st8["psK"].all()), Ds, ALU.mult)
            P.tt(PTT, E, E, bc_mid(k.TRIL.all(), NQ), ALU.mult)
            FP.put(st8["psG"])
            FP.put(st8["psK"])

        def b4():
            psQ = FP.get()
            st8["psQ"] = psQ
            for hl in range(NQ):
                P.mm(psQ[:, hl * 128:(hl + 1) * 128], qT[:, hl, i * 128:(i + 1) * 128], kT[:, hl, i * 128:(i + 1) * 128])
            pt = TP.get()
            st8["pt"] = pt
            for hl in range(NQ):
                P.tr(pt[:, hl * 128:(hl + 1) * 128], X[i][:, hl, :], k.ident_b.all())
            psA0 = FP.get()
            st8["psA0"] = psA0
            for hl in range(NQ):
                P.mm(psA0[:, hl * 128:(hl + 1) * 128], X[i][:, hl, :], k.ident_b.all(), start=True, stop=False)
                P.mm(psA0[:, hl * 128:(hl + 1) * 128], k.ident_b.all(), k.ident_b.all(), start=False, stop=True)

        def b5():
            intra = r4(A.alloc(NQ * 128))
            st8["intra"] = intra
            P.tt("dve", intra, r4(st8["psQ"].all()), st8["lg"], ALU.mult)
            P.copy("act", XT[i], r4(st8["pt"][:, 0:NQ * 128]))
            P.copy("act", AT[i], r4(st8["psA0"].all()))
            FP.put(st8["psQ"])
            FP.put(st8["psA0"])
            TP.put(st8["pt"])

        def b6():
            pt = TP.get()
            st8["pt2"] = pt
            for hl in range(NQ):
                P.tr(pt[:, hl * 128:(hl + 1) * 128], st8["intra"][:, hl, :], k.ident_b.all())

        def b7():
            P.copy("act", IT[i], r4(st8["pt2"][:, 0:NQ * 128]))
            TP.put(st8["pt2"])

        stages = [b1, b2, b3, b4, b5, b6, b7]
        for m in range(5):
            def c1(m=m):
                P.tag = "dnC"
                psX = FP.get()
                st8["psX"] = psX
                for hl in range(NQ):
                    P.mm(psX[:, hl * 128:(hl + 1) * 128], XT[i][:, hl, :], X[i][:, hl, :])
                if m < 4:
                    psXT = FP.get()
                    st8["psXT"] = psXT
                    for hl in range(NQ):
                        P.mm(psXT[:, hl * 128:(hl + 1) * 128], X[i][:, hl, :], XT[i][:, hl, :])

            def c2(m=m):
                P.copy("act", X[i], r4(st8["psX"].all()))
                FP.put(st8["psX"])
                if m < 4:
                    P.copy("dve", XT[i], r4(st8["psXT"].all()))
                    FP.put(st8["psXT"])

            def c3(m=m):
                psA = FP.get()
                st8["psA"] = psA
                for hl in range(NQ):
                    P.mm(psA[:, hl * 128:(hl + 1) * 128], k.ident_b.all(), AT[i][:, hl, :], start=True, stop=False)
                    P.mm(psA[:, hl * 128:(hl + 1) * 128], X[i][:, hl, :], AT[i][:, hl, :], start=False, stop=True)

            def c4(m=m):
                P.copy("act", AT[i], r4(st8["psA"].all()))
                FP.put(st8["psA"])

            stages += [c1, c2, c3, c4]
        return stages

    diagonal([chainBC(i) for i in range(NT)])

    P.tag = "dnD"
    prep = []
    for i in range(NT):
        vb = rd(A.alloc(NQ * 128))
        P.tt(PTT, vb, v_tm[:, i, :, :], bc_last(k.beta[:, i, hsl], 128), ALU.mult)
        kbg = rd(A.alloc(NQ * 128))
        P.tt(PTT, kbg, k_tm[:, i, :, :], bc_last(k.bge[:, i, hsl], 128), ALU.mult)
        kdec = rd(A.alloc(NQ * 128))
        P.tt(PTT, kdec, k_tm[:, i, :, :], bc_last(k.elast[:, i, hsl], 128), ALU.mult)
        prep.append([vb, kbg, kdec])
    for i in range(NT):
        vb, kbg, kdec = prep[i]
        psU = next_psF(k)
        for hl in range(NQ):
            P.mm(psU[:, hl * 128:(hl + 1) * 128], AT[i][:, hl, :], vb[:, hl, :])
        psW = next_psF(k)
        for hl in range(NQ):
            P.mm(psW[:, hl * 128:(hl + 1) * 128], kbg[:, hl, :], AT[i][:, hl, :])
        prep[i] += [psU, psW]
        if i % 2 == 1 or i == NT - 1:
            for ii in range(i - (i % 2), i + 1):
                u = rd(A.alloc(NQ * 128, F32))
                P.copy("act", u, rd(prep[ii][3].all()))
                wT = X[ii]
                P.copy("dve", wT, r4(prep[ii][4].all()))
                prep[ii] += [u, wT]
    for i in range(NT):
        vb, kbg, kdec, _, _, u, wT = prep[i]
        o = rd(A.alloc(NQ * 128, F32))
        vn = XT[i]
        tmp = rd(A.alloc(NQ * 128, F32))
        for c in range(2):
            rs = slice(c * 64, c * 64 + 64)
            Sb = k.Sb[:, hs:hs + NQ, :]
            Sf = k.S[:, hs:hs + NQ, :]
            psA = next_psF(k)
            for hl in range(NQ):
                P.mm(psA[:, hl * 128:(hl + 1) * 128], wT[:, hl, :], Sb[:, hl, :])
            psB1 = next_psF(k)
            for hl in range(NQ):
                P.mm(psB1[:, hl * 128:(hl + 1) * 128], qT[:, hl, i * 128:(i + 1) * 128], Sb[:, hl, :])
            P.tt("dve", vn[rs], u[rs], rd(psA[rs, :]), ALU.subtract)
            P.tt(PTT, Sf, Sf, bc_last(k.dec[:, i, c * 8 + hs: c * 8 + hs + NQ], 128), ALU.mult)
            psB2 = next_psF(k)
            for hl in range(NQ):
                P.mm(psB2[:, hl * 128:(hl + 1) * 128], IT[i][rs, hl, :], vn[rs, hl, :])
            psS = next_psF(k)
            for hl in range(NQ):
                P.mm(psS[:, hl * 128:(hl + 1) * 128], kdec[rs, hl, :], vn[rs, hl, :])
            P.tt("dve", Sf, Sf, rd(psS.all()), ALU.add)
            P.copy("act", Sb, Sf)
            P.tt("dve", o[rs], rd(psB1[rs, :]), bc_last(k.egc[rs, i, hsl], 128), ALU.mult)
            P.copy("act", tmp[rs], rd(psB2[rs, :]))
            P.tt("dve", o[rs], o[rs], tmp[rs], ALU.add)
        sq = tmp
        P.act(sq, o, AF.Square)
        st = k.stat[i % 2]
        P.reduce("dve", st[:, 0:NQ], sq, ALU.add)
        act_rsqrt(P, st[:, 0:NQ], st[:, 0:NQ], 1.0 / DV, EPS)
        P.tt("dve", o, o, bc_last(st[:, 0:NQ], 128), ALU.mult)
        P.tt("dve", rd(k.o_dn[:, i, hs * 128:(hs + NQ) * 128]), o,
             rd(k.zg[:, i, hs * 128:(hs + NQ) * 128]), ALU.mult)
        tap(k, "o_raw", o, [128, NQ, 128])


def dsa_prep(k, g):
    P = k.P
    A = k.arena
    sp = k.smallp
    cqn = A.alloc(NT * 256).r("p (i r) -> p i r", i=NT)
    for (c0, gn, dst) in ((16, k.qn_bc, cqn), (272, k.kvn_bc, None)):
        sq = A.alloc(NT * 256, F32).r("p (i r) -> p i r", i=NT)
        P.tt("dve", sq, sp[:, :, c0:c0 + 256], sp[:, :, c0:c0 + 256], ALU.mult)
        st = k.stat[0]
        P.reduce("dve", st[:, 0:NT], sq, ALU.add)
        act_rsqrt(P, st[:, 0:NT], st[:, 0:NT], 1.0 / 256, EPS)
        P.tt("dve", sq, sp[:, :, c0:c0 + 256], bc_last(st[:, 0:NT], 256), ALU.mult)
        if dst is None:
            dst = k.ckv_tm[:, g * NT:(g + 1) * NT, 0:KV_RANK]
        P.tt("dve", dst, sq, bc_mid(gn.all(), NT), ALU.mult)
    ik = sp[:, :, 528:592]
    st = k.stat[1]
    P.reduce("dve", st[:, 0:NT], ik, ALU.add)
    P.ts("dve", st[:, 0:NT], st[:, 0:NT], -1.0 / IDX_DIM, ALU.mult)
    xc = A.alloc(NT * 64, F32).r("p (i d) -> p i d", i=NT)
    P.tt("dve", xc, ik, bc_last(st[:, 0:NT], 64), ALU.add)
    sq = A.alloc(NT * 64, F32).r("p (i d) -> p i d", i=NT)
    P.tt("dve", sq, xc, xc, ALU.mult)
    P.reduce("dve", st[:, 4:4 + NT], sq, ALU.add)
    act_rsqrt(P, st[:, 4:4 + NT], st[:, 4:4 + NT], 1.0 / IDX_DIM, EPS)
    P.tt("dve", xc, xc, bc_last(st[:, 4:4 + NT], 64), ALU.mult)
    P.tt("dve", xc, xc, bc_mid(k.ikg_bc.all(), NT), ALU.mult)
    kd = A.alloc(NT * 128).r("p (i e d) -> p i e d", i=NT, e=2)
    for e in range(2):
        P.tt("dve", kd[:, :, e, :], xc, bc_mid(k.ikb_bc.all(), NT), ALU.add)
    P.ts("dve", k.widx.all(), sp[:, :, 592:600], (IDX_HEADS ** -0.5) * (IDX_DIM ** -0.5), ALU.mult)
    for i in range(NT):
        gi = g * NT + i
        pt = next_psT(k)
        for c in range(2):
            P.tr(pt[:, c * 128:(c + 1) * 128], cqn[:, i, c * 128:(c + 1) * 128], k.ident_b.all())
        for c in range(2):
            P.tr(pt[:, (2 + c) * 128:(3 + c) * 128], k.ckv_tm[:, gi, c * 128:(c + 1) * 128], k.ident_b.all())
        P.tr(pt[:, 512:640], kd[:, i, :, :].r("p e d -> p (e d)"), k.ident_b.all())
        P.copy("act", k.s_cqT[:, :, i * 128:(i + 1) * 128], pt[:, 0:256].r("p (c t) -> p c t", c=2))
        P.copy("act", k.ckvT[:, :, gi * 128:(gi + 1) * 128], pt[:, 256:512].r("p (c t) -> p c t", c=2))
        P.copy("act", k.kidxT[:, gi * 128:(gi + 1) * 128], pt[:, 512:640])


def dsa(k, g):
    P = k.P
    A = k.arena
    wdma(k, k.s_wuq, k.w_uq.all().r("(c p) f -> p c f", p=128))
    wdma(k, k.s_wuv, k.w_uv.all().r("(c p) f -> p c f", p=128))
    wdma(k, k.s_wiq, k.w_iq.all().r("(c p) f -> p c f", p=128))
    wuk = A.alloc(2 * 1024).r("p (c f) -> p c f", c=2)
    wdma(k, wuk, k.w_uk.all().r("(c p) f -> p c f", p=128))
    for h in range(NH):
        pt = next_psT(k)
        for c in range(2):
            P.tr(pt[:, c * 128:(c + 1) * 128], wuk[:, c, h * 128:(h + 1) * 128], k.ident_b.all())
        P.copy("act", k.s_wukT[:, h, :], pt[:, 0:256])
    for hp in range(4):
        ps = next_psF(k)
        for c in range(2):
            P.mm(ps.all(), k.s_wiq[:, c, hp * 128:(hp + 1) * 128], k.s_cqT[:, c, :], start=(c == 0), stop=(c == 1))
        P.copy("act", k.s_qidx[:, hp, :], ps.all())
    tap(k, "ckv", k.ckv_tm[:, g * NT:(g + 1) * NT, :], [128, NT, KV_RANK + 1], BF16)
    tap(k, "cqT", k.s_cqT, [128, 2, T], BF16)
    tap(k, "kidxT", k.kidxT[:, g * T:(g + 1) * T], [128, T], BF16)
    tap(k, "qidx", k.s_qidx, [128, 4, T], BF16)
    tap(k, "widx", k.widx.all(), [128, NT, 8])
    fx = k.arena.reserve(12)
    score_buf = fx[:, 0:8192].f32()
    junk_buf = fx[:, 8192:12288]
    FP = BankPool(k.psF)
    TP = BankPool(k.psT)
    diagonal(dsa_score_chains(k, g, 0, score_buf, FP))
    diagonal(dsa_thr_chains(k, g, 0, score_buf, junk_buf))
    dsa_mask(k, g, 0, score_buf, junk_buf)
    for i in range(NT):
        att = dsa_att_chains(k, g, i, FP, TP)
        if i + 1 < NT:
            aux = dsa_score_chains(k, g, i + 1, score_buf, FP) + [[lambda: None], [lambda: None]] \
                + dsa_thr_chains(k, g, i + 1, score_buf, junk_buf)
            nsc = len(aux) - (NIT + 1 if (g * NT + i + 2) * 128 > TOPK else 1)
            caux = [0.7 if c < nsc else 5.0 for c in range(len(aux))]
            tot_aux = sum(caux)
            merged = []
            ia = 0
            acc = 0.0
            for c in range(len(att)):
                merged.append(att[c])
                target = tot_aux * (c + 1) / len(att)
                while ia < len(aux) and acc + 0.5 * caux[ia] <= target:
                    merged.append(aux[ia])
                    acc += caux[ia]
                    ia += 1
            merged += aux[ia:]
        else:
            merged = att
        diagonal(merged)
        if i + 1 < NT:
            dsa_mask(k, g, i + 1, score_buf, junk_buf)
    k.arena.release()


def dsa_score_chains(k, g, i, score_buf, FP):
    P = k.P
    gi = g * NT + i
    nk = (gi + 1) * 128
    tq = slice(i * 128, (i + 1) * 128)
    score = score_buf[:, 0:nk]
    chains = []
    for kb in range(0, nk, 512):
        kw = min(512, nk - kb)
        for h in range(IDX_HEADS):
            hp, e = h // 2, h % 2
            st8 = {}

            def s1(st8=st8, hp=hp, e=e, kb=kb, kw=kw):
                P.tag = "dsa_score"
                ps = FP.get()
                st8["ps"] = ps
                P.mm(ps[:, 0:kw], k.s_qidx[e * 64:(e + 1) * 64, hp, tq], k.kidxT[e * 64:(e + 1) * 64, kb:kb + kw])

            def s2(st8=st8, kw=kw):
                P.tag = "dsa_score"
                rl = k.rl[k.rl_i % 2].all()
                k.rl_i += 1
                st8["rl"] = rl
                P.act(rl[:, 0:kw], st8["ps"][:, 0:kw], AF.Relu)
                FP.put(st8["ps"])

            def s3(st8=st8, h=h, kb=kb, kw=kw):
                P.tag = "dsa_score"
                rl = st8["rl"]
                if h == 0:
                    P.ts("dve", score[:, kb:kb + kw], rl[:, 0:kw], k.widx[:, i, h:h + 1], ALU.mult)
                else:
                    P.stt("dve", score[:, kb:kb + kw], rl[:, 0:kw], k.widx[:, i, h:h + 1], score[:, kb:kb + kw],
                          ALU.mult, ALU.add)

            chains.append([s1, s2, s3])

    def causal():
        P.tag = "dsa_score"
        P.tt("dve", score[:, gi * 128:nk], score[:, gi * 128:nk], k.causneg.all(), ALU.add)

    chains.append([lambda: None, lambda: None, causal])
    return chains


def dsa_thr_chains(k, g, i, score_buf, junk_buf):
    P = k.P
    gi = g * NT + i
    nk = (gi + 1) * 128
    score = score_buf[:, 0:nk]
    th = k.thr
    chains = []
    if nk > TOPK:
        W = k.thW

        def init():
            P.tag = "dsa_thr"
            P.reduce("dve", th[:, 0:1], score[:, 0:gi * 128], ALU.min)
            P.reduce("dve", th[:, 1:2], score, ALU.max)
            P.tt("dve", th[:, 1:2], th[:, 1:2], th[:, 0:1], ALU.subtract)
            P.ts("dve", W.all(), k.pw2.all(), th[:, 1:2], ALU.mult)
            P.tt("dve", th[:, 2:3], th[:, 0:1], W[:, 0:1], ALU.add)

        chains.append([init])
        junk = junk_buf[:, 0:nk]

        def make_it(j):
            def it():
                P.tag = "dsa_thr"
                P.ts("dve", junk, score, th[:, 2:3], ALU.is_ge, 0.0, ALU.add, accum=th[:, 3:4])
                P.stt("dve", th[:, 4:5], th[:, 3:4], float(TOPK), W[:, j:j + 1], ALU.is_ge, ALU.mult)
                if j < NIT - 1:
                    P.stt("dve", th[:, 2:3], th[:, 4:5], W[:, j + 1:j + 2], th[:, 2:3], ALU.subtract, ALU.add)
                else:
                    P.stt("dve", th[:, 0:1], th[:, 4:5], W[:, j:j + 1], th[:, 2:3], ALU.subtract, ALU.add)
            return it

        for j in range(NIT):
            chains.append([make_it(j)])
    else:
        def init0():
            P.tag = "dsa_thr"
            P.memset("dve", th[:, 0:1], -1e29)

        chains.append([init0])
    return chains


def dsa_mask(k, g, i, score_buf, junk_buf):
    P = k.P
    P.tag = "dsa_thr"
    gi = g * NT + i
    nkt = gi + 1
    nk = nkt * 128
    score = score_buf[:, 0:nk]
    mk = junk_buf[:, 0:nk]
    P.ts("dve", mk, score, k.thr[:, 0:1], ALU.is_lt, NEG, ALU.mult)
    maskT = k.s_maskT[:, 0:nk].r("p (j q) -> p j q", j=nkt)
    for j0 in range(0, nkt, 8):
        nj = min(8, nkt - j0)
        pt = next_psT(k)
        for jj in range(nj):
            P.tr(pt[:, jj * 128:(jj + 1) * 128], mk[:, (j0 + jj) * 128:(j0 + jj + 1) * 128], k.ident_b.all())
        P.copy("act", maskT[:, j0:j0 + nj, :], pt[:, 0:nj * 128].r("p (j q) -> p j q", j=nj))


def dsa_att_chains(k, g, i, FP, TP):
    P = k.P
    A = k.arena
    gi = g * NT + i
    nkt = gi + 1
    nk = nkt * 128
    tq = slice(i * 128, (i + 1) * 128)
    maskT = k.s_maskT[:, 0:nk].r("p (j q) -> p j q", j=nkt)
    P.tag = "dsa_att"
    qhall = A.alloc(NH * 128).r("p (h q) -> p h q", h=NH)
    for h in range(NH):
        ps = FP.get()
        for c in range(2):
            P.mm(ps[:, 0:128], k.s_wuq[:, c, h * 128:(h + 1) * 128], k.s_cqT[:, c, tq], start=(c == 0), stop=(c == 1))
        P.copy("act", qhall[:, h, :], ps[:, 0:128])
        FP.put(ps)
    qlall = k.s_ql
    for h in range(NH):
        ps2 = FP.get()
        for rc in range(2):
            P.mm(ps2[:, rc * 128:(rc + 1) * 128], k.s_wukT[:, h, rc * 128:(rc + 1) * 128], qhall[:, h, :])
        P.act(qlall[:, h, :, :], ps2[:, 0:256].r("p (c q) -> p c q", c=2), AF.Copy, scale=float(DK ** -0.5))
        FP.put(ps2)

    def make_chain(h, j0):
        nj = min(4, nkt - j0)
        ql = qlall[:, h, :, :]
        oa = k.psH[h % 2]
        st8 = {}

        def s1():
            P.tag = "dsa_att"
            pl = FP.get()
            st8["pl"] = pl
            for jj in range(nj):
                j = j0 + jj
                dst = pl[:, jj * 128:(jj + 1) * 128]
                P.mm(dst, k.ckvT[:, 0, j * 128:(j + 1) * 128], ql[:, 0, :], start=True, stop=False)
                P.mm(dst, k.ckvT[:, 1, j * 128:(j + 1) * 128], ql[:, 1, :], start=False, stop=False)
                near = k.biasD if j == gi else (k.bias1 if j == gi - 1 else None)
                P.mm(dst, k.ident_b.all(), maskT[:, j, :], start=False, stop=(near is None))
                if near is not None:
                    P.mm(dst, k.ident_b.all(), near[:, h, :], start=False, stop=True)

        def s3():
            P.tag = "dsa_att"
            pl = st8["pl"]
            pT = A.alloc(512).r("p (j q) -> p j q", j=4)
            st8["pT"] = pT
            P.act(pT[:, 0:nj, :], pl[:, 0:nj * 128].r("p (j q) -> p j q", j=nj), AF.Exp)
            FP.put(pl)

        def s4():
            P.tag = "dsa_att"
            pT = st8["pT"]
            for jj in range(nj):
                j = j0 + jj
                P.mm(oa[:, 0:KV_RANK + 1], pT[:, jj, :], k.ckv_tm[:, j, :], start=(j == 0), stop=(j == nkt - 1))

        stages = [s1, s3, s4]
        if j0 + nj >= nkt:
            def e1():
                P.tag = "dsa_att"
                stt = k.stat[h % 2]
                P.recip(stt[:, 0:1], oa[:, KV_RANK:KV_RANK + 1])
                ol = A.alloc(256)
                st8["ol"] = ol
                P.ts("dve", ol, oa[:, 0:KV_RANK], stt[:, 0:1], ALU.mult)

            def e2():
                P.tag = "dsa_att"
                pt = TP.get()
                st8["pt"] = pt
                for rc in range(2):
                    P.tr(pt[:, rc * 128:(rc + 1) * 128], st8["ol"][:, rc * 128:(rc + 1) * 128], k.ident_b.all())

            def e3():
                P.tag = "dsa_att"
                olT = A.alloc(256).r("p (c q) -> p c q", c=2)
                st8["olT"] = olT
                P.copy("act", olT, st8["pt"][:, 0:256].r("p (c q) -> p c q", c=2))
                TP.put(st8["pt"])

            def e4():
                P.tag = "dsa_att"
                pb = TP.get()
                st8["pb"] = pb
                po = pb[:, 0:256].map(lambda ap: ap.bitcast(F32))
                st8["po"] = po
                for rc in range(2):
                    P.mm(po, st8["olT"][:, rc, :], k.s_wuv[:, rc, h * 128:(h + 1) * 128],
                         start=(rc == 0), stop=(rc == 1))

            def e5():
                P.tag = "dsa_att"
                P.copy("act", k.o_sa[:, i, h * 128:(h + 1) * 128], st8["po"])
                TP.put(st8["pb"])

            stages += [e1, e2, e3, e4, e5]
        return stages

    return [make_chain(h, j0) for h in range(NH) for j0 in range(0, nkt, 4)]


def diagonal(chains):
    active = []
    nxt = 0
    while active or nxt < len(chains):
        for ent in list(active):
            ent[0][ent[1]]()
            ent[1] += 1
            if ent[1] >= len(ent[0]):
                active.remove(ent)
        if nxt < len(chains):
            ch = chains[nxt]
            nxt += 1
            ch[0]()
            if len(ch) > 1:
                active.append([ch, 1])


def merge_wo(k, g):
    P = k.P
    A = k.arena
    for which, src in ((0, k.o_dn), (1, k.o_sa)):
        for hh in range(2):
            wv = wload_cols(k, k.w_in, (C_GA0 if which == 0 else C_GB0) + hh * 512, 512)
            for i in range(NT):
                ps = next_psF(k)
                for c in range(8):
                    P.mm(ps.all(), k.xT[:, c, i * 128:(i + 1) * 128], wv[:, c, :], start=(c == 0), stop=(c == 7))
                sg = A.alloc(512, F32)
                P.act(sg, ps.all(), AF.Sigmoid)
                od = k.o_dn[:, i, hh * 512:(hh + 1) * 512]
                if which == 0:
                    P.tt("dve", od, sg, od, ALU.mult)
                else:
                    P.tt("dve", sg, sg, k.o_sa[:, i, hh * 512:(hh + 1) * 512], ALU.mult)
                    P.tt("dve", od, sg, od, ALU.add)
    tap(k, "merged", k.o_dn.all(), [128, NT, 1024], BF16)
    for i in range(NT):
        pt = next_psT(k)
        for c in range(8):
            P.tr(pt[:, c * 128:(c + 1) * 128], k.o_dn[:, i, c * 128:(c + 1) * 128], k.ident_b.all())
        P.copy("act", k.xT[:, :, i * 128:(i + 1) * 128], pt.all().r("p (c t) -> p c t", c=8))
    for hh in range(2):
        wv = wload_cols(k, k.w_o, hh * 512, 512)
        for i in range(NT):
            ps = next_psF(k)
            for c in range(8):
                P.mm(ps.all(), k.xT[:, c, i * 128:(i + 1) * 128], wv[:, c, :], start=(c == 0), stop=(c == 7))
            xs = k.xres[:, i, hh * 512:(hh + 1) * 512]
            P.tt("dve", xs, xs, ps.all(), ALU.add)


def final_norm_store(k, g, do_norm):
    P = k.P
    A = k.arena
    t0 = g * T
    if do_norm:
        gf = A.alloc(D_MODEL, F32)
        sv = k.final_norm.all().map(lambda ap: ap.rearrange("a d -> (a d)").partition_broadcast(128))
        P.dma("pool", [(gf.ap, sv.ap)], [sv], [gf])
    for i in range(NT):
        dst = k.y[t0 + i * 128: t0 + (i + 1) * 128, :]
        if do_norm:
            xt = k.xres[:, i, :]
            st = k.stat[i % 2]
            yo = A.alloc(D_MODEL, F32)
            P.act(yo, xt, AF.Square, accum=st[:, 0:1])
            act_rsqrt(P, st[:, 2:3], st[:, 0:1], 1.0 / D_MODEL, EPS)
            P.stt("dve", yo, xt, st[:, 2:3], gf, ALU.mult, ALU.mult)
            P.dma("pool", [(dst.ap, yo.ap)], [yo], [dst])
        else:
            P.dma("pool", [(dst.ap, k.xres[:, i, :].ap)], [k.xres[:, i, :]], [dst])


def group(k, g, stages):
    P = k.P
    t0 = g * T
    for i in range(NT):
        src = k.x[t0 + i * 128: t0 + (i + 1) * 128, :]
        P.dma("pool", [(k.xres[:, i, :].ap, src.ap)], [src], [k.xres[:, i, :]])
    P.tag = "ffn1"
    if "ffn1" in stages:
        if "mix" in stages and not getattr(k, "bias_built", False):
            k.bias_gen = build_bias_tiles(k)
            ffn(k, k.gcol["ffn1"], k.ffn1_wg, k.ffn1_wu, k.ffn1_wd, bg=k.bias_gen)
            for _ in k.bias_gen:
                pass
            k.bias_built = True
        else:
            ffn(k, k.gcol["ffn1"], k.ffn1_wg, k.ffn1_wu, k.ffn1_wd)
    if "mix" in stages:
        if not getattr(k, "bias_built", False):
            P.tag = "setup"
            for _ in build_bias_tiles(k):
                pass
            k.bias_built = True
        P.tag = "mixproj"
        mixer_proj(k, g)
        if "dsa" in stages:
            P.tag = "dsa"
            dsa(k, g)
            tap(k, "o_sa", k.o_sa.all(), [128, NT, 1024], BF16)
        P.tag = "dn"
        for hq in range(NH // NQ):
            if "dn0" not in stages:
                dn_quad(k, g, hq)
        tap(k, "o_dn", k.o_dn.all(), [128, NT, 1024], BF16)
        P.tag = "wo"
        if "wo" in stages:
            merge_wo(k, g)
    P.tag = "ffn2"
    if "ffn2" in stages:
        ffn(k, k.gcol["ffn2"], k.ffn2_wg, k.ffn2_wu, k.ffn2_wd)
    P.tag = "final"
    final_norm_store(k, g, "final" in stages)


ALL_STAGES = ("ffn1", "mix", "dsa", "wo", "ffn2", "final")

INPUT_ORDER = ["x", "ffn1_norm", "ffn1_wg", "ffn1_wu", "ffn1_wd", "mix_norm", "w_in", "conv_w", "a_log",
               "dt_bias", "dn_out_norm", "q_norm", "kv_norm", "w_uq", "w_uk", "w_uv", "w_iq", "idx_k_g",
               "idx_k_b", "rel_bias", "w_o", "ffn2_norm", "ffn2_wg", "ffn2_wu", "ffn2_wd", "final_norm"]


def make_in_maps(inputs, ncores=8):
    shared = {}
    for n in INPUT_ORDER:
        if n == "x":
            continue
        a = np.asarray(inputs[n], dtype=np.float32)
        if n == "final_norm":
            a = a.reshape(1, D_MODEL)
        elif n == "rel_bias":
            a = a.reshape(1, REL_BUCKETS * NH)
        elif n in ("w_uk", "w_uv"):
            a = a.reshape(KV_RANK, 1024)
        elif n == "conv_w":
            a = a.reshape(CONV_K, 3072)
        else:
            a = a.reshape(a.shape[1:]) if a.shape[0] == 1 and a.ndim == 3 else a
        shared[n] = np.ascontiguousarray(a)
    x = np.asarray(inputs["x"], dtype=np.float32)
    maps = []
    for c in range(ncores):
        m = dict(shared)
        m["x"] = np.ascontiguousarray(x[c])
        maps.append(m)
    return maps


_CACHE = {}


def kernel(**inputs):
    if "k" not in _CACHE:
        _CACHE["k"] = build(ngroups=8, stages=ALL_STAGES)
    k = _CACHE["k"]
    in_maps = make_in_maps(inputs, 8)
    res = run_bass_kernel_spmd(k.nc, in_maps, core_ids=list(range(8)))
    return np.stack([np.asarray(r["y"]) for r in res.results], axis=0).astype(np.float32)
```

```python
import math
import numpy as np
import concourse.bass as bass
import concourse.mybir as mybir
from concourse.bass_utils import run_bass_kernel_spmd

F32 = mybir.dt.float32
BF16 = mybir.dt.bfloat16
AF = mybir.ActivationFunctionType
ALU = mybir.AluOpType
AX = mybir.AxisListType

SAME_ENGINE_SYNC = True
ANNOTATE = False
SEM_LIMIT = 28000

D_MODEL = 1024; SEQ = 4096; D_FF = 2816
NH = 8; DK = 128; DV = 128; CONV_K = 4; CHUNK = 64
Q_RANK = 256; KV_RANK = 256; IDX_HEADS = 8; IDX_DIM = 64; TOPK = 256
REL_BUCKETS = 32; REL_MAX_DIST = 128
EPS = 1e-6
IN_WIDTH = 6744
C_Q0 = 0; C_K0 = 1024; C_V0 = 2048; C_Z0 = 3072; C_B0 = 4096; C_A0 = 4104
C_CQ0 = 4112; C_CKV0 = 4368; C_IK0 = 4624; C_IW0 = 4688; C_GA0 = 4696; C_GB0 = 5720
T = 512; NT = 4
NQ = 4
PTT = "dve"
ARENA_CHUNKS = 30
NIT = 14
NEG = -30000.0
NB_THR = [1, 2, 3, 4, 5, 6, 7, 8, 9, 10, 11, 12, 13, 14, 15, 16, 19, 21, 24, 27, 31, 35, 40, 46, 52, 59, 67, 77, 87, 99, 113]


class Slot:
    __slots__ = ("w", "r")

    def __init__(self):
        self.w = None
        self.r = {}


class View:
    __slots__ = ("buf", "ap", "slots", "aid")

    def __init__(self, buf, ap, slots, aid=None):
        self.buf = buf
        self.ap = ap
        self.slots = slots
        self.aid = aid

    def map(self, fn):
        return View(self.buf, fn(self.ap), self.slots, self.aid)

    def __getitem__(self, idx):
        return View(self.buf, self.ap[idx], self.slots, self.aid)

    def f32(self):
        return self.map(lambda ap: ap.bitcast(F32))

    def r(self, pattern, **kw):
        return self.map(lambda ap: ap.rearrange(pattern, **kw))


class Arena:
    def __init__(self, P, nchunks):
        self.buf = P.buf("arena", [128, nchunks * 1024], BF16, nslots=nchunks, slot_axis=1, slot_size=1024)
        self.n = nchunks
        self.pos = 0
        self.owner = [None] * nchunks
        self.aid = 0
        self.buf.arena = self

    def reserve(self, nch):
        self.lo = nch
        if self.pos < nch:
            self.pos = nch
        self.aid += 1
        for c in range(nch):
            self.owner[c] = self.aid
        v = self.buf[:, 0: nch * 1024]
        v.aid = self.aid
        return v

    def release(self):
        self.lo = 0

    def alloc(self, nelem, dtype=BF16):
        nb = nelem * (4 if dtype == F32 else 2)
        nch = (nb + 2047) // 2048
        lo = getattr(self, "lo", 0)
        assert nch <= self.n - lo
        if self.pos + nch > self.n:
            self.pos = lo
        a = self.pos
        self.pos += nch
        self.aid += 1
        for c in range(a, a + nch):
            self.owner[c] = self.aid
        v = self.buf[:, a * 1024: a * 1024 + nb // 2]
        v.aid = self.aid
        if dtype == F32:
            v = v.f32()
        return v


class Buf:
    def __init__(self, P, name, shape, dtype, space="sbuf", nslots=1, slot_axis=None, kind=None, slot_size=1):
        nc = P.nc
        self.slot_size = slot_size
        self.name = name
        self.shape = list(shape)
        self.dtype = dtype
        self.space = space
        if space == "sbuf":
            self.t = nc.alloc_sbuf_tensor(name, list(shape), dtype)
        elif space == "psum":
            self.t = nc.alloc_psum_tensor(name, list(shape), dtype)
        else:
            self.t = nc.dram_tensor(name, list(shape), dtype, kind=kind or "Internal")
        self.slot_axis = slot_axis
        self.slots = [Slot() for _ in range(nslots)]
        self.dsem = {}

    def _base(self):
        return self.t.ap() if self.space == "dram" else self.t

    def _slots_of(self, idx):
        if self.slot_axis is None:
            return [0]
        if not isinstance(idx, tuple):
            idx = (idx,)
        if len(idx) <= self.slot_axis:
            return list(range(len(self.slots)))
        s = idx[self.slot_axis]
        ss = self.slot_size
        if isinstance(s, int):
            return [s // ss]
        a, b, _ = s.indices(len(self.slots) * ss)
        return list(range(a // ss, (b - 1) // ss + 1))

    def __getitem__(self, idx):
        return View(self, self._base()[idx], self._slots_of(idx))

    def all(self):
        return View(self, self._base()[:], list(range(len(self.slots))))


class EngState:
    def __init__(self, name, sem, is_pe=False):
        self.name = name
        self.sem = sem
        self.count = 0
        self.waited = {}
        self.is_pe = is_pe


class Prog:
    ENG = ("pe", "act", "dve", "pool", "sp")

    def __init__(self, nc):
        self.nc = nc
        self.nsem = 0
        self.eng = {n: EngState(n, self._newsem("s_" + n), n == "pe") for n in self.ENG}
        self.streams = {n: [] for n in self.ENG}
        self.out_tokens = []
        self.tag = "setup"
        self.ninst = {n: 0 for n in self.ENG}

    def _newsem(self, name):
        self.nsem += 1
        return self.nc.alloc_semaphore("%s_%d" % (name, self.nsem))

    def buf(self, name, shape, dtype, **kw):
        return Buf(self, name, shape, dtype, **kw)

    def _deps(self, E, reads, writes):
        need = {}

        def add(tok):
            k = id(tok[0])
            if k not in need or need[k][1] < tok[1]:
                need[k] = tok

        for v in list(reads) + list(writes):
            if v.aid is not None:
                for s in v.slots:
                    assert v.buf.arena.owner[s] == v.aid, "arena buffer reused while live: %s" % v.buf.name
        for v in reads:
            for s in v.slots:
                sl = v.buf.slots[s]
                if sl.w is not None:
                    add(sl.w)
        for v in writes:
            for s in v.slots:
                sl = v.buf.slots[s]
                if sl.w is not None:
                    add(sl.w)
                for tok in sl.r.values():
                    add(tok)
        out = []
        for k, (sem, val) in need.items():
            if sem is E.sem and (E.is_pe or not SAME_ENGINE_SYNC):
                continue
            if E.waited.get(k, 0) >= val:
                continue
            E.waited[k] = val
            out.append((sem, val))
        return out

    def _mark(self, tok, reads, writes):
        k = id(tok[0])
        for v in reads:
            for s in v.slots:
                v.buf.slots[s].r[k] = tok
        for v in writes:
            for s in v.slots:
                sl = v.buf.slots[s]
                sl.w = tok
                sl.r = {}

    def op(self, ename, fn, reads, writes):
        E = self.eng[ename]
        waits = self._deps(E, reads, writes)
        if E.count >= SEM_LIMIT:
            E.sem = self._newsem("s_" + ename)
            E.count = 0
        E.count += 1
        tok = (E.sem, E.count)
        self.streams[ename].append((waits, [fn], E.sem, 1, self.tag))
        self.ninst[ename] += 1 + len(waits)
        self._mark(tok, reads, writes)

    NDSEM = 14

    def dma(self, qname, pairs, reads, writes, fresh=False, **kw):
        E = self.eng[qname]
        waits = self._deps(E, reads, writes)
        if fresh:
            sem = self._newsem("once")
            fns = [(lambda e, o=o, i=i: e.dma_start(out=o, in_=i, **kw)) for (o, i) in pairs]
            tok = (sem, 16 * len(pairs))
            self.streams[qname].append((waits, fns, sem, 16, self.tag))
            self.ninst[qname] += len(pairs) + len(waits)
            self._mark(tok, reads, writes)
            self._need = []
            self._sim(qname, tok, writes[0], reads, dma_bytes=1000000) if hasattr(self, "_sim") else None
            return
        if not hasattr(self, "dpool"):
            self.dpool = [[self._newsem("dma"), 0] for _ in range(self.NDSEM)]
            self.dpool_i = 0
        idx = self.dpool_i % self.NDSEM
        self.dpool_i += 1
        ds = self.dpool[idx]
        if ds[1] + 16 * len(pairs) > SEM_LIMIT:
            ds = [self._newsem("dma"), 0]
            self.dpool[idx] = ds
        if ds[1] > 0 and E.waited.get(id(ds[0]), 0) < ds[1]:
            E.waited[id(ds[0])] = ds[1]
            waits.append((ds[0], ds[1]))
        fns = [(lambda e, o=o, i=i: e.dma_start(out=o, in_=i, **kw)) for (o, i) in pairs]
        ds[1] += 16 * len(pairs)
        tok = (ds[0], ds[1])
        self.streams[qname].append((waits, fns, ds[0], 16, self.tag))
        self.ninst[qname] += len(pairs) + len(waits)
        self._mark(tok, reads, writes)
        if writes[0].buf.space == "dram":
            self.out_tokens.append(tok)

    def raw(self, ename, fns):
        self.streams[ename].append(([], fns, None, None, self.tag))
        self.ninst[ename] += len(fns)

    def finish(self, ename="pool"):
        last = {}
        for sem, val in self.out_tokens:
            if id(sem) not in last or last[id(sem)][1] < val:
                last[id(sem)] = (sem, val)
        self.streams[ename].append((list(last.values()), [], None, 0, "fin"))

    def emit(self):
        nc = self.nc

        def run(e, stream):
            for waits, fns, sem, inc, tag in stream:
                for (s, v) in waits:
                    e.wait_ge(s, v)
                if inc is None:
                    for fn in fns:
                        fn(e)
                    continue
                for fn in fns:
                    ins = fn(e).then_inc(sem, inc)
                    if ANNOTATE:
                        ins.annotate(tag)

        with nc.Block() as block:
            @block.tensor
            def _(e):
                run(e, self.streams["pe"])

            @block.scalar
            def _(e):
                run(e, self.streams["act"])

            @block.vector
            def _(e):
                run(e, self.streams["dve"])

            @block.gpsimd
            def _(e):
                run(e, self.streams["pool"])

            @block.sync
            def _(e):
                run(e, self.streams["sp"])

    def mm(self, out, lhsT, rhs, start=True, stop=True, **kw):
        self.op("pe", lambda e: e.matmul(out.ap, lhsT.ap, rhs.ap, start=start, stop=stop, **kw),
                [lhsT, rhs], [out])

    def tr(self, out, in_, ident):
        self.op("pe", lambda e: e.transpose(out.ap, in_.ap, ident.ap), [in_, ident], [out])

    def act(self, out, in_, func, bias=None, scale=None, accum=None):
        reads = [in_]
        writes = [out]
        kw = {}
        if bias is not None:
            if isinstance(bias, View):
                reads.append(bias)
                kw["bias"] = bias.ap
            else:
                kw["bias"] = bias
        if scale is not None:
            if isinstance(scale, View):
                reads.append(scale)
                kw["scale"] = scale.ap
            else:
                kw["scale"] = scale
        if accum is not None:
            writes.append(accum)
            kw["accum_out"] = accum.ap
        self.op("act", lambda e: e.activation(out=out.ap, in_=in_.ap, func=func, **kw), reads, writes)

    def ts(self, eng, out, in0, s1, op0, s2=None, op1=None, accum=None):
        reads = [in0]
        writes = [out]
        a1 = s1.ap if isinstance(s1, View) else s1
        a2 = s2.ap if isinstance(s2, View) else s2
        if isinstance(s1, View):
            reads.append(s1)
        if isinstance(s2, View):
            reads.append(s2)
        kw = {}
        if op1 is not None:
            kw["op1"] = op1
        if accum is not None:
            writes.append(accum)
            kw["accum_out"] = accum.ap
        self.op(eng, lambda e: e.tensor_scalar(out.ap, in0.ap, a1, a2, op0, **kw), reads, writes)

    def tt(self, eng, out, in0, in1, op):
        self.op(eng, lambda e: e.tensor_tensor(out.ap, in0.ap, in1.ap, op), [in0, in1], [out])

    def stt(self, eng, out, in0, scalar, in1, op0, op1):
        reads = [in0, in1]
        a = scalar.ap if isinstance(scalar, View) else scalar
        if isinstance(scalar, View):
            reads.append(scalar)
        self.op(eng, lambda e: e.scalar_tensor_tensor(out.ap, in0.ap, a, in1.ap, op0, op1), reads, [out])

    def copy(self, eng, out, in_):
        if eng == "act":
            self.op(eng, lambda e: e.copy(out.ap, in_.ap), [in_], [out])
        else:
            self.op(eng, lambda e: e.tensor_copy(out.ap, in_.ap), [in_], [out])

    def memset(self, eng, out, val):
        self.op(eng, lambda e: e.memset(out.ap, val), [], [out])

    def reduce(self, eng, out, in_, op, axis=AX.X):
        self.op(eng, lambda e: e.tensor_reduce(out.ap, in_.ap, axis, op), [in_], [out])

    def recip(self, out, in_):
        self.op("dve", lambda e: e.reciprocal(out.ap, in_.ap), [in_], [out])


class K:
    pass


def bc_last(v, n):
    return v.map(lambda ap: ap.unsqueeze(2).to_broadcast([ap.shape[0], ap.shape[1], n]))


def bc_mid(v, n):
    return v.map(lambda ap: ap.unsqueeze(1).to_broadcast([ap.shape[0], n, ap.shape[1]]))


def build(ngroups=8, stages=("ffn1",), taps=(), glist=None):
    nc = bass.Bass("TRN2", target_bir_lowering=False)
    P = Prog(nc)
    k = K()
    k.P = P
    k.taps = {}
    k.tapset = set(taps)
    k.stages = stages

    def din(name, shape):
        return P.buf(name, shape, F32, space="dram", kind="ExternalInput")

    k.x = din("x", [SEQ, D_MODEL])
    k.ffn1_norm = din("ffn1_norm", [1, D_MODEL])
    k.ffn1_wg = din("ffn1_wg", [D_MODEL, D_FF])
    k.ffn1_wu = din("ffn1_wu", [D_MODEL, D_FF])
    k.ffn1_wd = din("ffn1_wd", [D_FF, D_MODEL])
    k.mix_norm = din("mix_norm", [1, D_MODEL])
    k.w_in = din("w_in", [D_MODEL, IN_WIDTH])
    k.conv_w = din("conv_w", [CONV_K, 3072])
    k.a_log = din("a_log", [1, NH])
    k.dt_bias = din("dt_bias", [1, NH])
    k.dn_out_norm = din("dn_out_norm", [1, DV])
    k.q_norm = din("q_norm", [1, Q_RANK])
    k.kv_norm = din("kv_norm", [1, KV_RANK])
    k.w_uq = din("w_uq", [Q_RANK, 1024])
    k.w_uk = din("w_uk", [KV_RANK, 1024])
    k.w_uv = din("w_uv", [KV_RANK, 1024])
    k.w_iq = din("w_iq", [Q_RANK, 512])
    k.idx_k_g = din("idx_k_g", [1, IDX_DIM])
    k.idx_k_b = din("idx_k_b", [1, IDX_DIM])
    k.rel_bias = din("rel_bias", [1, REL_BUCKETS * NH])
    k.w_o = din("w_o", [D_MODEL, D_MODEL])
    k.ffn2_norm = din("ffn2_norm", [1, D_MODEL])
    k.ffn2_wg = din("ffn2_wg", [D_MODEL, D_FF])
    k.ffn2_wu = din("ffn2_wu", [D_MODEL, D_FF])
    k.ffn2_wd = din("ffn2_wd", [D_FF, D_MODEL])
    k.final_norm = din("final_norm", [1, D_MODEL])
    k.y = P.buf("y", [SEQ, D_MODEL], F32, space="dram", kind="ExternalOutput", nslots=SEQ // 128, slot_axis=0,
                slot_size=128)

    k.xres = P.buf("xres", [128, NT, D_MODEL], F32, nslots=NT, slot_axis=1)
    k.xn = [P.buf("xn0", [128, D_MODEL], BF16)] * 2
    k.xT = P.buf("xT", [128, 8, T], BF16)
    k.psF = [P.buf("psF%d" % i, [128, 512], F32, space="psum") for i in range(4)]
    k.psH = [P.buf("psH%d" % i, [128, 512], F32, space="psum") for i in range(2)]
    k.psF_i = 0
    k.psT = [P.buf("psT%d" % i, [128, 1024], BF16, space="psum") for i in range(2)]
    k.psT_i = 0
    k.stat = [P.buf("stat%d" % i, [128, 8], F32) for i in range(2)]
    k.gcol = {n: P.buf("gcol_" + n, [128, 8], F32) for n in ("ffn1", "mix", "ffn2")}
    k.ident_f = P.buf("ident_f", [128, 128], F32)
    k.dmat = P.buf("dmat", [128, 128], F32)
    k.ident_b = P.buf("ident_b", [128, 128], BF16)
    for n in ("TRIU", "TRIL", "STRICT", "BLOCK", "SELC0", "SELC1", "ONES"):
        setattr(k, n, P.buf("m_" + n, [128, 128], F32))
    k.zg = P.buf("zg", [128, NT, 1024], BF16)
    k.o_dn = P.buf("o_dn", [128, NT, 1024], BF16)
    k.o_sa = P.buf("o_sa", [128, NT, 1024], BF16)
    k.S = P.buf("S", [128, NH, DV], F32, nslots=2, slot_axis=1, slot_size=4)
    k.Sb = P.buf("Sb", [128, NH, DV], BF16, nslots=2, slot_axis=1, slot_size=4)
    k.halo = P.buf("halo", [128, 24, 3], F32, nslots=24, slot_axis=1)
    k.convw = P.buf("convw", [128, 24, 4], F32)
    k.dtb = P.buf("dtb", [128, NH], F32)
    k.nega = P.buf("nega", [128, NH], F32)
    k.gn_dn = P.buf("gn_dn", [128, DV], F32)
    k.beta = P.buf("beta", [128, NT, NH], F32)
    k.negb = P.buf("negb", [128, NT, NH], F32)
    k.gtok = P.buf("gtok", [128, NT, NH], F32)
    k.gst = P.buf("gst", [128, NT, 32], F32)
    k.egc = P.buf("egc", [128, NT, NH], F32)
    k.elast = P.buf("elast", [128, NT, NH], F32)
    k.bge = P.buf("bge", [128, NT, NH], F32)
    k.dec = P.buf("dec", [128, NT, 16], F32)
    k.ph = P.buf("phase", [128, 16384], BF16, nslots=32, slot_axis=1, slot_size=512)

    def phv(off, n):
        return k.ph[:, off:off + n]

    k.q_qT = phv(0, 2048).r("p (h t) -> p h t", h=NQ)
    k.q_kT = phv(2048, 2048).r("p (h t) -> p h t", h=NQ)
    k.q_vtm = phv(4096, 2048).r("p (i h d) -> p i h d", i=NT, h=NQ)
    k.q_ktm = phv(6144, 2048).r("p (i h d) -> p i h d", i=NT, h=NQ)
    k.q_X = [phv(8192 + i * 512, 512).r("p (h t) -> p h t", h=NQ) for i in range(NT)]
    k.q_XT = [phv(10240 + i * 512, 512).r("p (h t) -> p h t", h=NQ) for i in range(NT)]
    k.q_AT = [phv(12288 + i * 512, 512).r("p (h t) -> p h t", h=NQ) for i in range(NT)]
    k.q_IT = [phv(14336 + i * 512, 512).r("p (h t) -> p h t", h=NQ) for i in range(NT)]
    k.s_qidx = phv(0, 2048).r("p (h t) -> p h t", h=4)
    k.s_wuq = phv(2048, 2048).r("p (c f) -> p c f", c=2)
    k.s_wukT = phv(4096, 2048).r("p (h r) -> p h r", h=NH)
    k.s_wuv = phv(6144, 2048).r("p (c f) -> p c f", c=2)
    k.s_wiq = phv(8192, 1024).r("p (c f) -> p c f", c=2)
    k.s_cqT = phv(9216, 1024).r("p (c t) -> p c t", c=2)
    k.s_maskT = phv(10240, 4096)
    k.s_ql = phv(14336, 2048).r("p (h c q) -> p h c q", h=NH, c=2)
    k.rl = [P.buf("rl%d" % i, [128, 512], F32) for i in range(2)]
    k.rl_i = 0
    k.bb_rb = phv(0, 512).f32()
    k.bb_dl = phv(512, 512).f32()
    k.bb_dd = phv(1024, 256).f32()
    k.bb_ind = phv(1536, 256).f32()
    k.bb_acc = phv(2048, 2048).f32().r("p (h q) -> p h q", h=NH)
    NTT = SEQ // 128
    k.ckv_tm = P.buf("ckv_tm", [128, NTT, KV_RANK + 1], BF16, nslots=NTT, slot_axis=1)
    k.ckvT = P.buf("ckvT", [128, 2, SEQ], BF16, nslots=NTT, slot_axis=2, slot_size=128)
    k.kidxT = P.buf("kidxT", [128, SEQ], BF16, nslots=NTT, slot_axis=1, slot_size=128)
    k.biasD = P.buf("biasD", [128, NH, 128], BF16)
    k.bias1 = P.buf("bias1", [128, NH, 128], BF16)
    k.causneg = P.buf("causneg", [128, 128], F32)
    k.qn_bc = P.buf("qn_bc", [128, Q_RANK], F32)
    k.kvn_bc = P.buf("kvn_bc", [128, KV_RANK], F32)
    k.ikg_bc = P.buf("ikg_bc", [128, IDX_DIM], F32)
    k.ikb_bc = P.buf("ikb_bc", [128, IDX_DIM], F32)
    k.widx = P.buf("widx", [128, NT, IDX_HEADS], F32)
    k.thr = P.buf("thr", [128, 8], F32)
    k.thW = P.buf("thW", [128, NIT + 1], F32)
    k.pw2 = P.buf("pw2", [128, NIT + 1], F32)
    k.arena = Arena(P, ARENA_CHUNKS)

    convert_weights(k)
    setup(k)
    for g in (glist if glist is not None else range(ngroups)):
        group(k, g, stages)
    P.finish()
    P.emit()
    k.nc = nc
    return k


class BankPool:
    def __init__(self, banks):
        self.free = list(banks)

    def get(self):
        assert self.free, "PSUM bank pool exhausted"
        return self.free.pop(0)

    def put(self, b):
        self.free.append(b)


def act_rsqrt(P, out, in_, scale, eps):
    P.act(out, in_, AF.Ln, bias=eps, scale=scale)
    P.act(out, out, AF.Exp, scale=-0.5)


def act_sigmoid(P, out, in_):
    P.act(out, in_, AF.Exp, scale=-1.0)
    P.act(out, out, AF.Ln, bias=1.0)
    P.act(out, out, AF.Exp, scale=-1.0)


def next_psF(k):
    b = k.psF[k.psF_i % len(k.psF)]
    k.psF_i += 1
    return b


def next_psT(k):
    b = k.psT[k.psT_i % len(k.psT)]
    k.psT_i += 1
    return b


def tap(k, name, view, shape, dtype=F32):
    if name not in k.tapset:
        return
    cnt = k.taps.get(name, 0)
    k.taps[name] = cnt + 1
    d = k.P.buf("tap_%s_%d" % (name, cnt), shape, dtype, space="dram", kind="ExternalOutput")
    k.P.dma("sp", [(d.all().ap, view.ap)], [view], [d.all()])


def convert_weights(k):
    P = k.P
    nc = P.nc
    P.tag = "convert"

    W = {}
    order = ("ffn1_wg", "ffn1_wu", "ffn1_wd", "w_in", "w_uq", "w_uv", "w_iq", "w_uk", "w_o", "ffn2_wg", "ffn2_wu", "ffn2_wd")
    for nm in order:
        src = getattr(k, nm)
        W[nm] = P.buf(nm + "_bf", src.shape, BF16, space="dram")
        P.dma("pool", [(W[nm].all().ap, src.all().ap)], [src.all()], [W[nm].all()], fresh=True)
    for nm, b in W.items():
        setattr(k, nm, b)


def setup(k):
    P = k.P
    dm = k.dmat
    P.op("pool", lambda e: e.iota(dm.all().ap, [[1, 128]], base=0, channel_multiplier=-1,
                                  allow_small_or_imprecise_dtypes=True), [], [dm.all()])
    P.memset("dve", k.BLOCK.all(), 0.0)
    P.memset("dve", k.BLOCK[0:64, 0:64], 1.0)
    P.memset("dve", k.BLOCK[64:128, 64:128], 1.0)
    P.memset("dve", k.ONES.all(), 1.0)
    P.memset("dve", k.SELC0.all(), 0.0)
    P.memset("dve", k.SELC0[0:64, :], 1.0)
    P.memset("dve", k.SELC1.all(), 0.0)
    P.memset("dve", k.SELC1[64:128, :], 1.0)
    P.ts("dve", k.TRIU.all(), dm.all(), 0.0, ALU.is_ge)
    P.tt("dve", k.TRIU.all(), k.TRIU.all(), k.BLOCK.all(), ALU.mult)
    P.ts("dve", k.TRIL.all(), dm.all(), 0.0, ALU.is_le)
    P.tt("dve", k.TRIL.all(), k.TRIL.all(), k.BLOCK.all(), ALU.mult)
    P.ts("dve", k.STRICT.all(), dm.all(), 0.0, ALU.is_lt)
    P.tt("dve", k.STRICT.all(), k.STRICT.all(), k.BLOCK.all(), ALU.mult)
    P.ts("dve", k.ident_f.all(), dm.all(), 0.0, ALU.is_equal)
    P.copy("dve", k.ident_b.all(), k.ident_f.all())
    for n, src in (("ffn1", k.ffn1_norm), ("mix", k.mix_norm), ("ffn2", k.ffn2_norm)):
        sv = src.all().r("a (c p) -> p (a c)", p=128)
        P.dma("sp", [(k.gcol[n].all().ap, sv.ap)], [sv], [k.gcol[n].all()], allow_slow_non_contiguous=True)

    def bcast(dst, src):
        sv = src.all().map(lambda ap: ap.rearrange("a d -> (a d)").partition_broadcast(128))
        P.dma("sp", [(dst.all().ap, sv.ap)], [sv], [dst.all()])

    bcast(k.dtb, k.dt_bias)
    bcast(k.nega, k.a_log)
    P.act(k.nega.all(), k.nega.all(), AF.Exp)
    P.ts("dve", k.nega.all(), k.nega.all(), -1.0, ALU.mult)
    bcast(k.gn_dn, k.dn_out_norm)
    for kk in range(CONV_K):
        sv = k.conv_w[kk:kk + 1, :].r("a (c p) -> p (a c)", p=128)
        P.dma("sp", [(k.convw[:, :, kk].ap, sv.ap)], [sv], [k.convw.all()], allow_slow_non_contiguous=True)
    P.memset("dve", k.halo.all(), 0.0)
    for j in range(NIT + 1):
        P.memset("dve", k.pw2[:, j:j + 1], 2.0 ** -(j + 1))
    bcast(k.qn_bc, k.q_norm)
    bcast(k.kvn_bc, k.kv_norm)
    bcast(k.ikg_bc, k.idx_k_g)
    bcast(k.ikb_bc, k.idx_k_b)
    P.memset("dve", k.ckv_tm[:, :, KV_RANK:KV_RANK + 1], 1.0)
    P.ts("dve", k.causneg.all(), dm.all(), 0.0, ALU.is_gt, -1e30, ALU.mult)
    P.memset("dve", k.S.all(), 0.0)
    P.memset("dve", k.Sb.all(), 0.0)


def build_bias_tiles(k):
    P = k.P
    dm = k.dmat
    rb = k.bb_rb
    sv = k.rel_bias.all().map(lambda ap: ap.rearrange("a d -> (a d)").partition_broadcast(128))
    P.dma("sp", [(rb.ap, sv.ap)], [sv], [rb])
    dl = k.bb_dl
    P.tt("dve", dl[:, NH:REL_BUCKETS * NH], rb[:, NH:REL_BUCKETS * NH], rb[:, 0:(REL_BUCKETS - 1) * NH], ALU.subtract)
    P.tt("dve", dl[:, 0:NH], rb[:, 0:NH], rb[:, (REL_BUCKETS - 1) * NH:REL_BUCKETS * NH], ALU.subtract)
    for which, dst in ((0, k.biasD), (1, k.bias1)):
        dd = k.bb_dd
        P.ts("dve", dd, dm.all(), 128.0 * which, ALU.add)
        acc = k.bb_acc
        for h in range(NH):
            P.ts("dve", acc[:, h, :], dd, 0.0, ALU.mult, dl[:, h:h + 1], ALU.add)
        ind = k.bb_ind
        for b in range(1, REL_BUCKETS):
            P.ts("dve", ind, dd, float(NB_THR[b - 1]), ALU.is_ge)
            for h in range(NH):
                P.stt("dve", acc[:, h, :], ind, dl[:, b * NH + h:b * NH + h + 1], acc[:, h, :], ALU.mult, ALU.add)
            yield
        P.copy("dve", dst.all(), acc)


def rms_to_T(k, i, gcol, dstT):
    P = k.P
    xt = k.xres[:, i, :]
    st = k.stat[i % 2]
    xn = k.xn[i % 2]
    P.act(xn.all(), xt, AF.Square, accum=st[:, 0:1])
    act_rsqrt(P, st[:, 2:3], st[:, 0:1], 1.0 / D_MODEL, EPS)
    P.ts("dve", xn.all(), xt, st[:, 2:3], ALU.mult)
    pt = next_psT(k)
    for c in range(8):
        P.tr(pt[:, c * 128:(c + 1) * 128], xn[:, c * 128:(c + 1) * 128], k.ident_b.all())
    P.tt("dve", dstT[:, :, i * 128:(i + 1) * 128], pt.all().r("p (c t) -> p c t", c=8),
         bc_last(gcol.all(), 128), ALU.mult)


def wdma(k, dst_view, src_view):
    k.P.dma("sp", [(dst_view.ap, src_view.ap)], [src_view], [dst_view])


def wload_cols(k, w, c0, ncols):
    wt = k.arena.alloc(8 * ncols).r("p (c f) -> p c f", c=8)
    wdma(k, wt, w[:, c0:c0 + ncols].r("(c p) f -> p c f", p=128))
    return wt


def ffn(k, gcol, wg, wu, wd, bg=None):
    P = k.P
    for i in range(NT):
        rms_to_T(k, i, gcol, k.xT)
    nblk = (D_FF + 511) // 512

    def gate_up(fb):
        f0 = fb * 512
        fw = min(512, D_FF - f0)
        nfc = fw // 128
        wg_v = wload_cols(k, wg, f0, fw)
        wu_v = wload_cols(k, wu, f0, fw)
        hT = k.arena.alloc(nfc * T).r("p (c t) -> p c t", c=nfc)
        for fc in range(nfc):
            pg = next_psF(k)
            pu = next_psF(k)
            for c in range(8):
                P.mm(pg.all(), wg_v[:, c, fc * 128:(fc + 1) * 128], k.xT[:, c, :], start=(c == 0), stop=(c == 7))
            for c in range(8):
                P.mm(pu.all(), wu_v[:, c, fc * 128:(fc + 1) * 128], k.xT[:, c, :], start=(c == 0), stop=(c == 7))
            sg = k.rl[k.rl_i % 2].all()
            k.rl_i += 1
            P.act(sg, pg.all(), AF.Silu)
            P.tt("dve", hT[:, fc, :], sg, pu.all(), ALU.mult)
        return (hT, f0, fw, nfc)

    def down(hT, f0, fw, nfc):
        wd_v = k.arena.alloc(nfc * 1024).r("p (c d) -> p c d", c=nfc)
        wdma(k, wd_v, wd[f0:f0 + fw, :].r("(c p) d -> p c d", p=128))
        for i in range(NT):
            for hh in range(2):
                po = k.psH[(2 * i + hh) % 2]
                for fc in range(nfc):
                    P.mm(po.all(), hT[:, fc, i * 128:(i + 1) * 128], wd_v[:, fc, hh * 512:(hh + 1) * 512],
                         start=(fc == 0), stop=(fc == nfc - 1))
                xs = k.xres[:, i, hh * 512:(hh + 1) * 512]
                P.stt("dve", xs, po.all(), 0.5, xs, ALU.mult, ALU.add)

    def advance(n):
        if bg is not None:
            for _ in range(n):
                if next(bg, "end") == "end":
                    break

    prev = None
    for fb in range(nblk):
        cur = gate_up(fb)
        advance(6)
        if prev is not None:
            down(*prev)
        advance(6)
        prev = cur
    down(*prev)


def mixer_proj(k, g):
    P = k.P
    for i in range(NT):
        rms_to_T(k, i, k.gcol["mix"], k.xT)
    for hh in range(2):
        wz = wload_cols(k, k.w_in, C_Z0 + hh * 512, 512)
        for i in range(NT):
            ps = next_psF(k)
            for c in range(8):
                P.mm(ps.all(), k.xT[:, c, i * 128:(i + 1) * 128], wz[:, c, :], start=(c == 0), stop=(c == 7))
            sg = k.rl[k.rl_i % 2].all()
            k.rl_i += 1
            P.act(sg, ps.all(), AF.Silu)
            P.tt("dve", k.zg[:, i, hh * 512:(hh + 1) * 512].r("p (h d) -> p h d", h=4),
                 sg.r("p (h d) -> p h d", h=4), bc_mid(k.gn_dn.all(), 4), ALU.mult)
    k.smallp = k.arena.alloc(NT * 600, F32).r("p (i c) -> p i c", i=NT)
    w1 = wload_cols(k, k.w_in, 4096, 512)
    w2 = wload_cols(k, k.w_in, 4608, 88)
    for i in range(NT):
        ps = next_psF(k)
        for c in range(8):
            P.mm(ps.all(), k.xT[:, c, i * 128:(i + 1) * 128], w1[:, c, :], start=(c == 0), stop=(c == 7))
        P.copy("act", k.smallp[:, i, 0:512], ps.all())
        ps2 = next_psF(k)
        for c in range(8):
            P.mm(ps2[:, 0:88], k.xT[:, c, i * 128:(i + 1) * 128], w2[:, c, :], start=(c == 0), stop=(c == 7))
        P.copy("act", k.smallp[:, i, 512:600], ps2[:, 0:88])
    act_sigmoid(P, k.beta.all(), k.smallp[:, :, 0:8])
    P.ts("dve", k.negb.all(), k.beta.all(), -1.0, ALU.mult)
    P.tt("dve", k.gtok.all(), k.smallp[:, :, 8:16], bc_mid(k.dtb.all(), NT), ALU.add)
    P.act(k.gtok.all(), k.gtok.all(), AF.Exp)
    P.act(k.gtok.all(), k.gtok.all(), AF.Ln, bias=1.0)
    P.tt("dve", k.gtok.all(), k.gtok.all(), bc_mid(k.nega.all(), NT), ALU.mult)
    ps = next_psF(k)
    for i in range(NT):
        for j, m in enumerate((k.TRIU, k.BLOCK, k.SELC0, k.SELC1)):
            P.mm(ps[:, i * 32 + j * 8: i * 32 + j * 8 + 8], m.all(), k.gtok[:, i, :])
    P.copy("act", k.gst.all(), ps[:, 0:NT * 32].r("p (i c) -> p i c", i=NT))
    P.act(k.egc.all(), k.gst[:, :, 0:8], AF.Exp)
    P.tt("dve", k.elast.all(), k.gst[:, :, 8:16], k.gst[:, :, 0:8], ALU.subtract)
    P.act(k.elast.all(), k.elast.all(), AF.Exp)
    P.act(k.dec.all(), k.gst[:, :, 16:32], AF.Exp)
    P.tt("dve", k.bge.all(), k.beta.all(), k.egc.all(), ALU.mult)
    tap(k, "beta", k.beta.all(), [128, NT, NH])
    tap(k, "gtok", k.gtok.all(), [128, NT, NH])
    dsa_prep(k, g)


def dn_quad(k, g, hq):
    P = k.P
    A = k.arena
    hs = hq * NQ
    P.tag = "dnA"
    qT, kT, v_tm, k_tm = k.q_qT, k.q_kT, k.q_vtm, k.q_ktm
    hsl = slice(hs, hs + NQ)
    r4 = lambda v: v.r("p (h t) -> p h t", h=NQ)
    rd = lambda v: v.r("p (h d) -> p h d", h=NQ)
    FP = BankPool(k.psF + k.psH)
    TP = BankPool(k.psT)

    wvs = {}

    def chainA(kind, hl):
        ch = kind * 8 + hs + hl
        st8 = {}

        def a1():
            if hl == 0:
                wvs[kind] = wload_cols(k, k.w_in, kind * 1024 + hs * 128, NQ * 128)
            wv = wvs[kind]
            ps = FP.get()
            st8["ps"] = ps
            for c in range(8):
                P.mm(ps.all(), wv[:, c, hl * 128:(hl + 1) * 128], k.xT[:, c, :], start=(c == 0), stop=(c == 7))

        def a2a():
            P.tag = "dnA"
            ps = st8["ps"]
            cb = A.alloc(1024, F32)
            st8["cb"] = cb
            P.copy("act", cb[:, 0:3], k.halo[:, ch, :])
            P.copy("act", cb[:, 3:3 + T], ps.all())
            FP.put(ps)
            P.copy("act", k.halo[:, ch, :], cb[:, T:T + 3])

        def a2b():
            P.tag = "dnA"
            cb = st8["cb"]
            yv = A.alloc(T, F32)
            st8["yv"] = yv
            P.ts("dve", yv, cb[:, 3:3 + T], k.convw[:, ch, 3:4], ALU.mult)
            for kk in range(3):
                P.stt("dve", yv, cb[:, kk:kk + T], k.convw[:, ch, kk:kk + 1], yv, ALU.mult, ALU.add)

        def a2c():
            P.tag = "dnA"
            sv = A.alloc(T, F32)
            st8["sv"] = sv
            act_sigmoid(P, sv, st8["yv"])

        def a2d():
            P.tag = "dnA"
            cb, yv, sv = st8["cb"], st8["yv"], st8["sv"]
            if kind == 2:
                sb = cb[:, 0:T // 2].map(lambda ap: ap.bitcast(BF16))
                st8["sb"] = sb
                P.tt("dve", sb, sv, yv, ALU.mult)
            else:
                P.tt("dve", sv, sv, yv, ALU.mult)
                sq = cb[:, 0:T]
                st8["sq"] = sq
                P.act(sq, sv, AF.Square)

        def a3():
            if kind == 2:
                pt = TP.get()
                st8["pt"] = pt
                for i in range(NT):
                    P.tr(pt[:, i * 128:(i + 1) * 128], st8["sb"][:, i * 128:(i + 1) * 128], k.ident_b.all())
            else:
                pss = FP.get()
                st8["pss"] = pss
                P.mm(pss.all(), k.ONES.all(), st8["sq"])

        def a4():
            if kind == 2:
                P.copy("act", v_tm[:, :, hl, :], st8["pt"][:, 0:NT * 128].r("p (i d) -> p i d", i=NT))
                TP.put(st8["pt"])
            else:
                rn = st8["yv"]
                act_rsqrt(P, rn, st8["pss"].all(), 1.0, EPS)
                FP.put(st8["pss"])
                dst = (qT if kind == 0 else kT)[:, hl, :]
                st8["dst"] = dst
                if kind == 0:
                    P.stt("dve", dst, st8["sv"], DK ** -0.5, rn, ALU.mult, ALU.mult)
                else:
                    P.tt("dve", dst, st8["sv"], rn, ALU.mult)

        def a5():
            pt = TP.get()
            st8["pt"] = pt
            for i in range(NT):
                P.tr(pt[:, i * 128:(i + 1) * 128], st8["dst"][:, i * 128:(i + 1) * 128], k.ident_b.all())

        def a6():
            P.copy("act", k_tm[:, :, hl, :], st8["pt"][:, 0:NT * 128].r("p (i d) -> p i d", i=NT))
            TP.put(st8["pt"])

        stages = [a1, a2a, a2b, a2c, a2d, a3, a4]
        if kind == 1:
            stages += [a5, a6]
        return stages

    diagonal([chainA(kind, hl) for kind in (1, 0, 2) for hl in range(NQ)])
    tap(k, "qT", qT, [128, NQ, T], BF16)
    tap(k, "kT", kT, [128, NQ, T], BF16)
    tap(k, "v_tm", v_tm, [128, NT, NQ, 128], BF16)

    P.tag = "dnB"
    X = k.q_X; XT = k.q_XT; AT = k.q_AT; IT = k.q_IT

    def chainBC(i):
        st8 = {}

        def b1():
            lg = r4(A.alloc(NQ * 128, F32))
            st8["lg"] = lg
            P.tt("dve", lg, bc_mid(k.TRIU.all(), NQ), bc_last(k.gtok[:, i, hsl], 128), ALU.mult)

        def b2():
            psG = FP.get()
            st8["psG"] = psG
            for hl in range(NQ):
                P.mm(psG[:, hl * 128:(hl + 1) * 128], st8["lg"][:, hl, :], k.STRICT.all())
            psK = FP.get()
            st8["psK"] = psK
            for hl in range(NQ):
                kt = kT[:, hl, i * 128:(i + 1) * 128]
                P.mm(psK[:, hl * 128:(hl + 1) * 128], kt, kt)

        def b3():
            E = st8["lg"]
            P.act(E, r4(st8["psG"].all()), AF.Exp)
            Ds = r4(A.alloc(NQ * 128, F32))
            P.tt(PTT, Ds, E, bc_mid(k.STRICT.all(), NQ), ALU.mult)
            P.tt(PTT, Ds, Ds, bc_last(k.negb[:, i, hsl], 128), ALU.mult)
            P.tt("dve", X[i], r4(st8["psK"].all()), Ds, ALU.mult)
            P.tt(PTT, E, E, bc_mid(k.TRIL.all(), NQ), ALU.mult)
            FP.put(st8["psG"])
            FP.put(st8["psK"])

        def b4():
            psQ = FP.get()
            st8["psQ"] = psQ
            for hl in range(NQ):
                P.mm(psQ[:, hl * 128:(hl + 1) * 128], qT[:, hl, i * 128:(i + 1) * 128], kT[:, hl, i * 128:(i + 1) * 128])
            pt = TP.get()
            st8["pt"] = pt
            for hl in range(NQ):
                P.tr(pt[:, hl * 128:(hl + 1) * 128], X[i][:, hl, :], k.ident_b.all())
            psA0 = FP.get()
            st8["psA0"] = psA0
            for hl in range(NQ):
                P.mm(psA0[:, hl * 128:(hl + 1) * 128], X[i][:, hl, :], k.ident_b.all(), start=True, stop=False)
                P.mm(psA0[:, hl * 128:(hl + 1) * 128], k.ident_b.all(), k.ident_b.all(), start=False, stop=True)

        def b5():
            intra = r4(A.alloc(NQ * 128))
            st8["intra"] = intra
            P.tt("dve", intra, r4(st8["psQ"].all()), st8["lg"], ALU.mult)
            P.copy("act", XT[i], r4(st8["pt"][:, 0:NQ * 128]))
            P.copy("act", AT[i], r4(st8["psA0"].all()))
            FP.put(st8["psQ"])
            FP.put(st8["psA0"])
            TP.put(st8["pt"])

        def b6():
            pt = TP.get()
            st8["pt2"] = pt
            for hl in range(NQ):
                P.tr(pt[:, hl * 128:(hl + 1) * 128], st8["intra"][:, hl, :], k.ident_b.all())

        def b7():
            P.copy("act", IT[i], r4(st8["pt2"][:, 0:NQ * 128]))
            TP.put(st8["pt2"])

        stages = [b1, b2, b3, b4, b5, b6, b7]
        for m in range(5):
            def c1(m=m):
                P.tag = "dnC"
                psX = FP.get()
                st8["psX"] = psX
                for hl in range(NQ):
                    P.mm(psX[:, hl * 128:(hl + 1) * 128], XT[i][:, hl, :], X[i][:, hl, :])
                if m < 4:
                    psXT = FP.get()
                    st8["psXT"] = psXT
                    for hl in range(NQ):
                        P.mm(psXT[:, hl * 128:(hl + 1) * 128], X[i][:, hl, :], XT[i][:, hl, :])

            def c2(m=m):
                P.copy("act", X[i], r4(st8["psX"].all()))
                FP.put(st8["psX"])
                if m < 4:
                    P.copy("dve", XT[i], r4(st8["psXT"].all()))
                    FP.put(st8["psXT"])

            def c3(m=m):
                psA = FP.get()
                st8["psA"] = psA
                for hl in range(NQ):
                    P.mm(psA[:, hl * 128:(hl + 1) * 128], k.ident_b.all(), AT[i][:, hl, :], start=True, stop=False)
                    P.mm(psA[:, hl * 128:(hl + 1) * 128], X[i][:, hl, :], AT[i][:, hl, :], start=False, stop=True)

            def c4(m=m):
                P.copy("act", AT[i], r4(st8["psA"].all()))
                FP.put(st8["psA"])

            stages += [c1, c2, c3, c4]
        return stages

    diagonal([chainBC(i) for i in range(NT)])

    P.tag = "dnD"
    prep = []
    for i in range(NT):
        vb = rd(A.alloc(NQ * 128))
        P.tt(PTT, vb, v_tm[:, i, :, :], bc_last(k.beta[:, i, hsl], 128), ALU.mult)
        kbg = rd(A.alloc(NQ * 128))
        P.tt(PTT, kbg, k_tm[:, i, :, :], bc_last(k.bge[:, i, hsl], 128), ALU.mult)
        kdec = rd(A.alloc(NQ * 128))
        P.tt(PTT, kdec, k_tm[:, i, :, :], bc_last(k.elast[:, i, hsl], 128), ALU.mult)
        prep.append([vb, kbg, kdec])
    for i in range(NT):
        vb, kbg, kdec = prep[i]
        psU = next_psF(k)
        for hl in range(NQ):
            P.mm(psU[:, hl * 128:(hl + 1) * 128], AT[i][:, hl, :], vb[:, hl, :])
        psW = next_psF(k)
        for hl in range(NQ):
            P.mm(psW[:, hl * 128:(hl + 1) * 128], kbg[:, hl, :], AT[i][:, hl, :])
        prep[i] += [psU, psW]
        if i % 2 == 1 or i == NT - 1:
            for ii in range(i - (i % 2), i + 1):
                u = rd(A.alloc(NQ * 128, F32))
                P.copy("act", u, rd(prep[ii][3].all()))
                wT = X[ii]
                P.copy("dve", wT, r4(prep[ii][4].all()))
                prep[ii] += [u, wT]
    for i in range(NT):
        vb, kbg, kdec, _, _, u, wT = prep[i]
        o = rd(A.alloc(NQ * 128, F32))
        vn = XT[i]
        tmp = rd(A.alloc(NQ * 128, F32))
        for c in range(2):
            rs = slice(c * 64, c * 64 + 64)
            Sb = k.Sb[:, hs:hs + NQ, :]
            Sf = k.S[:, hs:hs + NQ, :]
            psA = next_psF(k)
            for hl in range(NQ):
                P.mm(psA[:, hl * 128:(hl + 1) * 128], wT[:, hl, :], Sb[:, hl, :])
            psB1 = next_psF(k)
            for hl in range(NQ):
                P.mm(psB1[:, hl * 128:(hl + 1) * 128], qT[:, hl, i * 128:(i + 1) * 128], Sb[:, hl, :])
            P.tt("dve", vn[rs], u[rs], rd(psA[rs, :]), ALU.subtract)
            P.tt(PTT, Sf, Sf, bc_last(k.dec[:, i, c * 8 + hs: c * 8 + hs + NQ], 128), ALU.mult)
            psB2 = next_psF(k)
            for hl in range(NQ):
                P.mm(psB2[:, hl * 128:(hl + 1) * 128], IT[i][rs, hl, :], vn[rs, hl, :])
            psS = next_psF(k)
            for hl in range(NQ):
                P.mm(psS[:, hl * 128:(hl + 1) * 128], kdec[rs, hl, :], vn[rs, hl, :])
            P.tt("dve", Sf, Sf, rd(psS.all()), ALU.add)
            P.copy("act", Sb, Sf)
            P.tt("dve", o[rs], rd(psB1[rs, :]), bc_last(k.egc[rs, i, hsl], 128), ALU.mult)
            P.copy("act", tmp[rs], rd(psB2[rs, :]))
            P.tt("dve", o[rs], o[rs], tmp[rs], ALU.add)
        sq = tmp
        P.act(sq, o, AF.Square)
        st = k.stat[i % 2]
        P.reduce("dve", st[:, 0:NQ], sq, ALU.add)
        act_rsqrt(P, st[:, 0:NQ], st[:, 0:NQ], 1.0 / DV, EPS)
        P.tt("dve", o, o, bc_last(st[:, 0:NQ], 128), ALU.mult)
        P.tt("dve", rd(k.o_dn[:, i, hs * 128:(hs + NQ) * 128]), o,
             rd(k.zg[:, i, hs * 128:(hs + NQ) * 128]), ALU.mult)
        tap(k, "o_raw", o, [128, NQ, 128])


def dsa_prep(k, g):
    P = k.P
    A = k.arena
    sp = k.smallp
    cqn = A.alloc(NT * 256).r("p (i r) -> p i r", i=NT)
    for (c0, gn, dst) in ((16, k.qn_bc, cqn), (272, k.kvn_bc, None)):
        sq = A.alloc(NT * 256, F32).r("p (i r) -> p i r", i=NT)
        P.tt("dve", sq, sp[:, :, c0:c0 + 256], sp[:, :, c0:c0 + 256], ALU.mult)
        st = k.stat[0]
        P.reduce("dve", st[:, 0:NT], sq, ALU.add)
        act_rsqrt(P, st[:, 0:NT], st[:, 0:NT], 1.0 / 256, EPS)
        P.tt("dve", sq, sp[:, :, c0:c0 + 256], bc_last(st[:, 0:NT], 256), ALU.mult)
        if dst is None:
            dst = k.ckv_tm[:, g * NT:(g + 1) * NT, 0:KV_RANK]
        P.tt("dve", dst, sq, bc_mid(gn.all(), NT), ALU.mult)
    ik = sp[:, :, 528:592]
    st = k.stat[1]
    P.reduce("dve", st[:, 0:NT], ik, ALU.add)
    P.ts("dve", st[:, 0:NT], st[:, 0:NT], -1.0 / IDX_DIM, ALU.mult)
    xc = A.alloc(NT * 64, F32).r("p (i d) -> p i d", i=NT)
    P.tt("dve", xc, ik, bc_last(st[:, 0:NT], 64), ALU.add)
    sq = A.alloc(NT * 64, F32).r("p (i d) -> p i d", i=NT)
    P.tt("dve", sq, xc, xc, ALU.mult)
    P.reduce("dve", st[:, 4:4 + NT], sq, ALU.add)
    act_rsqrt(P, st[:, 4:4 + NT], st[:, 4:4 + NT], 1.0 / IDX_DIM, EPS)
    P.tt("dve", xc, xc, bc_last(st[:, 4:4 + NT], 64), ALU.mult)
    P.tt("dve", xc, xc, bc_mid(k.ikg_bc.all(), NT), ALU.mult)
    kd = A.alloc(NT * 128).r("p (i e d) -> p i e d", i=NT, e=2)
    for e in range(2):
        P.tt("dve", kd[:, :, e, :], xc, bc_mid(k.ikb_bc.all(), NT), ALU.add)
    P.ts("dve", k.widx.all(), sp[:, :, 592:600], (IDX_HEADS ** -0.5) * (IDX_DIM ** -0.5), ALU.mult)
    for i in range(NT):
        gi = g * NT + i
        pt = next_psT(k)
        for c in range(2):
            P.tr(pt[:, c * 128:(c + 1) * 128], cqn[:, i, c * 128:(c + 1) * 128], k.ident_b.all())
        for c in range(2):
            P.tr(pt[:, (2 + c) * 128:(3 + c) * 128], k.ckv_tm[:, gi, c * 128:(c + 1) * 128], k.ident_b.all())
        P.tr(pt[:, 512:640], kd[:, i, :, :].r("p e d -> p (e d)"), k.ident_b.all())
        P.copy("act", k.s_cqT[:, :, i * 128:(i + 1) * 128], pt[:, 0:256].r("p (c t) -> p c t", c=2))
        P.copy("act", k.ckvT[:, :, gi * 128:(gi + 1) * 128], pt[:, 256:512].r("p (c t) -> p c t", c=2))
        P.copy("act", k.kidxT[:, gi * 128:(gi + 1) * 128], pt[:, 512:640])


def dsa(k, g):
    P = k.P
    A = k.arena
    wdma(k, k.s_wuq, k.w_uq.all().r("(c p) f -> p c f", p=128))
    wdma(k, k.s_wuv, k.w_uv.all().r("(c p) f -> p c f", p=128))
    wdma(k, k.s_wiq, k.w_iq.all().r("(c p) f -> p c f", p=128))
    wuk = A.alloc(2 * 1024).r("p (c f) -> p c f", c=2)
    wdma(k, wuk, k.w_uk.all().r("(c p) f -> p c f", p=128))
    for h in range(NH):
        pt = next_psT(k)
        for c in range(2):
            P.tr(pt[:, c * 128:(c + 1) * 128], wuk[:, c, h * 128:(h + 1) * 128], k.ident_b.all())
        P.copy("act", k.s_wukT[:, h, :], pt[:, 0:256])
    for hp in range(4):
        ps = next_psF(k)
        for c in range(2):
            P.mm(ps.all(), k.s_wiq[:, c, hp * 128:(hp + 1) * 128], k.s_cqT[:, c, :], start=(c == 0), stop=(c == 1))
        P.copy("act", k.s_qidx[:, hp, :], ps.all())
    tap(k, "ckv", k.ckv_tm[:, g * NT:(g + 1) * NT, :], [128, NT, KV_RANK + 1], BF16)
    tap(k, "cqT", k.s_cqT, [128, 2, T], BF16)
    tap(k, "kidxT", k.kidxT[:, g * T:(g + 1) * T], [128, T], BF16)
    tap(k, "qidx", k.s_qidx, [128, 4, T], BF16)
    tap(k, "widx", k.widx.all(), [128, NT, 8])
    fx = k.arena.reserve(12)
    score_buf = fx[:, 0:8192].f32()
    junk_buf = fx[:, 8192:12288]
    FP = BankPool(k.psF)
    TP = BankPool(k.psT)
    diagonal(dsa_score_chains(k, g, 0, score_buf, FP))
    diagonal(dsa_thr_chains(k, g, 0, score_buf, junk_buf))
    dsa_mask(k, g, 0, score_buf, junk_buf)
    for i in range(NT):
        att = dsa_att_chains(k, g, i, FP, TP)
        if i + 1 < NT:
            aux = dsa_score_chains(k, g, i + 1, score_buf, FP) + [[lambda: None], [lambda: None]] \
                + dsa_thr_chains(k, g, i + 1, score_buf, junk_buf)
            nsc = len(aux) - (NIT + 1 if (g * NT + i + 2) * 128 > TOPK else 1)
            caux = [0.7 if c < nsc else 5.0 for c in range(len(aux))]
            tot_aux = sum(caux)
            merged = []
            ia = 0
            acc = 0.0
            for c in range(len(att)):
                merged.append(att[c])
                target = tot_aux * (c + 1) / len(att)
                while ia < len(aux) and acc + 0.5 * caux[ia] <= target:
                    merged.append(aux[ia])
                    acc += caux[ia]
                    ia += 1
            merged += aux[ia:]
        else:
            merged = att
        diagonal(merged)
        if i + 1 < NT:
            dsa_mask(k, g, i + 1, score_buf, junk_buf)
    k.arena.release()


def dsa_score_chains(k, g, i, score_buf, FP):
    P = k.P
    gi = g * NT + i
    nk = (gi + 1) * 128
    tq = slice(i * 128, (i + 1) * 128)
    score = score_buf[:, 0:nk]
    chains = []
    for kb in range(0, nk, 512):
        kw = min(512, nk - kb)
        for h in range(IDX_HEADS):
            hp, e = h // 2, h % 2
            st8 = {}

            def s1(st8=st8, hp=hp, e=e, kb=kb, kw=kw):
                P.tag = "dsa_score"
                ps = FP.get()
                st8["ps"] = ps
                P.mm(ps[:, 0:kw], k.s_qidx[e * 64:(e + 1) * 64, hp, tq], k.kidxT[e * 64:(e + 1) * 64, kb:kb + kw])

            def s2(st8=st8, kw=kw):
                P.tag = "dsa_score"
                rl = k.rl[k.rl_i % 2].all()
                k.rl_i += 1
                st8["rl"] = rl
                P.act(rl[:, 0:kw], st8["ps"][:, 0:kw], AF.Relu)
                FP.put(st8["ps"])

            def s3(st8=st8, h=h, kb=kb, kw=kw):
                P.tag = "dsa_score"
                rl = st8["rl"]
                if h == 0:
                    P.ts("dve", score[:, kb:kb + kw], rl[:, 0:kw], k.widx[:, i, h:h + 1], ALU.mult)
                else:
                    P.stt("dve", score[:, kb:kb + kw], rl[:, 0:kw], k.widx[:, i, h:h + 1], score[:, kb:kb + kw],
                          ALU.mult, ALU.add)

            chains.append([s1, s2, s3])

    def causal():
        P.tag = "dsa_score"
        P.tt("dve", score[:, gi * 128:nk], score[:, gi * 128:nk], k.causneg.all(), ALU.add)

    chains.append([lambda: None, lambda: None, causal])
    return chains


def dsa_thr_chains(k, g, i, score_buf, junk_buf):
    P = k.P
    gi = g * NT + i
    nk = (gi + 1) * 128
    score = score_buf[:, 0:nk]
    th = k.thr
    chains = []
    if nk > TOPK:
        W = k.thW

        def init():
            P.tag = "dsa_thr"
            P.reduce("dve", th[:, 0:1], score[:, 0:gi * 128], ALU.min)
            P.reduce("dve", th[:, 1:2], score, ALU.max)
            P.tt("dve", th[:, 1:2], th[:, 1:2], th[:, 0:1], ALU.subtract)
            P.ts("dve", W.all(), k.pw2.all(), th[:, 1:2], ALU.mult)
            P.tt("dve", th[:, 2:3], th[:, 0:1], W[:, 0:1], ALU.add)

        chains.append([init])
        junk = junk_buf[:, 0:nk]

        def make_it(j):
            def it():
                P.tag = "dsa_thr"
                P.ts("dve", junk, score, th[:, 2:3], ALU.is_ge, 0.0, ALU.add, accum=th[:, 3:4])
                P.stt("dve", th[:, 4:5], th[:, 3:4], float(TOPK), W[:, j:j + 1], ALU.is_ge, ALU.mult)
                if j < NIT - 1:
                    P.stt("dve", th[:, 2:3], th[:, 4:5], W[:, j + 1:j + 2], th[:, 2:3], ALU.subtract, ALU.add)
                else:
                    P.stt("dve", th[:, 0:1], th[:, 4:5], W[:, j:j + 1], th[:, 2:3], ALU.subtract, ALU.add)
            return it

        for j in range(NIT):
            chains.append([make_it(j)])
    else:
        def init0():
            P.tag = "dsa_thr"
            P.memset("dve", th[:, 0:1], -1e29)

        chains.append([init0])
    return chains


def dsa_mask(k, g, i, score_buf, junk_buf):
    P = k.P
    P.tag = "dsa_thr"
    gi = g * NT + i
    nkt = gi + 1
    nk = nkt * 128
    score = score_buf[:, 0:nk]
    mk = junk_buf[:, 0:nk]
    P.ts("dve", mk, score, k.thr[:, 0:1], ALU.is_lt, NEG, ALU.mult)
    maskT = k.s_maskT[:, 0:nk].r("p (j q) -> p j q", j=nkt)
    for j0 in range(0, nkt, 8):
        nj = min(8, nkt - j0)
        pt = next_psT(k)
        for jj in range(nj):
            P.tr(pt[:, jj * 128:(jj + 1) * 128], mk[:, (j0 + jj) * 128:(j0 + jj + 1) * 128], k.ident_b.all())
        P.copy("act", maskT[:, j0:j0 + nj, :], pt[:, 0:nj * 128].r("p (j q) -> p j q", j=nj))


def dsa_att_chains(k, g, i, FP, TP):
    P = k.P
    A = k.arena
    gi = g * NT + i
    nkt = gi + 1
    nk = nkt * 128
    tq = slice(i * 128, (i + 1) * 128)
    maskT = k.s_maskT[:, 0:nk].r("p (j q) -> p j q", j=nkt)
    P.tag = "dsa_att"
    qhall = A.alloc(NH * 128).r("p (h q) -> p h q", h=NH)
    for h in range(NH):
        ps = FP.get()
        for c in range(2):
            P.mm(ps[:, 0:128], k.s_wuq[:, c, h * 128:(h + 1) * 128], k.s_cqT[:, c, tq], start=(c == 0), stop=(c == 1))
        P.copy("act", qhall[:, h, :], ps[:, 0:128])
        FP.put(ps)
    qlall = k.s_ql
    for h in range(NH):
        ps2 = FP.get()
        for rc in range(2):
            P.mm(ps2[:, rc * 128:(rc + 1) * 128], k.s_wukT[:, h, rc * 128:(rc + 1) * 128], qhall[:, h, :])
        P.act(qlall[:, h, :, :], ps2[:, 0:256].r("p (c q) -> p c q", c=2), AF.Copy, scale=float(DK ** -0.5))
        FP.put(ps2)

    def make_chain(h, j0):
        nj = min(4, nkt - j0)
        ql = qlall[:, h, :, :]
        oa = k.psH[h % 2]
        st8 = {}

        def s1():
            P.tag = "dsa_att"
            pl = FP.get()
            st8["pl"] = pl
            for jj in range(nj):
                j = j0 + jj
                dst = pl[:, jj * 128:(jj + 1) * 128]
                P.mm(dst, k.ckvT[:, 0, j * 128:(j + 1) * 128], ql[:, 0, :], start=True, stop=False)
                P.mm(dst, k.ckvT[:, 1, j * 128:(j + 1) * 128], ql[:, 1, :], start=False, stop=False)
                near = k.biasD if j == gi else (k.bias1 if j == gi - 1 else None)
                P.mm(dst, k.ident_b.all(), maskT[:, j, :], start=False, stop=(near is None))
                if near is not None:
                    P.mm(dst, k.ident_b.all(), near[:, h, :], start=False, stop=True)

        def s3():
            P.tag = "dsa_att"
            pl = st8["pl"]
            pT = A.alloc(512).r("p (j q) -> p j q", j=4)
            st8["pT"] = pT
            P.act(pT[:, 0:nj, :], pl[:, 0:nj * 128].r("p (j q) -> p j q", j=nj), AF.Exp)
            FP.put(pl)

        def s4():
            P.tag = "dsa_att"
            pT = st8["pT"]
            for jj in range(nj):
                j = j0 + jj
                P.mm(oa[:, 0:KV_RANK + 1], pT[:, jj, :], k.ckv_tm[:, j, :], start=(j == 0), stop=(j == nkt - 1))

        stages = [s1, s3, s4]
        if j0 + nj >= nkt:
            def e1():
                P.tag = "dsa_att"
                stt = k.stat[h % 2]
                P.recip(stt[:, 0:1], oa[:, KV_RANK:KV_RANK + 1])
                ol = A.alloc(256)
                st8["ol"] = ol
                P.ts("dve", ol, oa[:, 0:KV_RANK], stt[:, 0:1], ALU.mult)

            def e2():
                P.tag = "dsa_att"
                pt = TP.get()
                st8["pt"] = pt
                for rc in range(2):
                    P.tr(pt[:, rc * 128:(rc + 1) * 128], st8["ol"][:, rc * 128:(rc + 1) * 128], k.ident_b.all())

            def e3():
                P.tag = "dsa_att"
                olT = A.alloc(256).r("p (c q) -> p c q", c=2)
                st8["olT"] = olT
                P.copy("act", olT, st8["pt"][:, 0:256].r("p (c q) -> p c q", c=2))
                TP.put(st8["pt"])

            def e4():
                P.tag = "dsa_att"
                pb = TP.get()
                st8["pb"] = pb
                po = pb[:, 0:256].map(lambda ap: ap.bitcast(F32))
                st8["po"] = po
                for rc in range(2):
                    P.mm(po, st8["olT"][:, rc, :], k.s_wuv[:, rc, h * 128:(h + 1) * 128],
                         start=(rc == 0), stop=(rc == 1))

            def e5():
                P.tag = "dsa_att"
                P.copy("act", k.o_sa[:, i, h * 128:(h + 1) * 128], st8["po"])
                TP.put(st8["pb"])

            stages += [e1, e2, e3, e4, e5]
        return stages

    return [make_chain(h, j0) for h in range(NH) for j0 in range(0, nkt, 4)]


def diagonal(chains):
    active = []
    nxt = 0
    while active or nxt < len(chains):
        for ent in list(active):
            ent[0][ent[1]]()
            ent[1] += 1
            if ent[1] >= len(ent[0]):
                active.remove(ent)
        if nxt < len(chains):
            ch = chains[nxt]
            nxt += 1
            ch[0]()
            if len(ch) > 1:
                active.append([ch, 1])


def merge_wo(k, g):
    P = k.P
    A = k.arena
    for which, src in ((0, k.o_dn), (1, k.o_sa)):
        for hh in range(2):
            wv = wload_cols(k, k.w_in, (C_GA0 if which == 0 else C_GB0) + hh * 512, 512)
            for i in range(NT):
                ps = next_psF(k)
                for c in range(8):
                    P.mm(ps.all(), k.xT[:, c, i * 128:(i + 1) * 128], wv[:, c, :], start=(c == 0), stop=(c == 7))
                sg = A.alloc(512, F32)
                P.act(sg, ps.all(), AF.Sigmoid)
                od = k.o_dn[:, i, hh * 512:(hh + 1) * 512]
                if which == 0:
                    P.tt("dve", od, sg, od, ALU.mult)
                else:
                    P.tt("dve", sg, sg, k.o_sa[:, i, hh * 512:(hh + 1) * 512], ALU.mult)
                    P.tt("dve", od, sg, od, ALU.add)
    tap(k, "merged", k.o_dn.all(), [128, NT, 1024], BF16)
    for i in range(NT):
        pt = next_psT(k)
        for c in range(8):
            P.tr(pt[:, c * 128:(c + 1) * 128], k.o_dn[:, i, c * 128:(c + 1) * 128], k.ident_b.all())
        P.copy("act", k.xT[:, :, i * 128:(i + 1) * 128], pt.all().r("p (c t) -> p c t", c=8))
    for hh in range(2):
        wv = wload_cols(k, k.w_o, hh * 512, 512)
        for i in range(NT):
            ps = next_psF(k)
            for c in range(8):
                P.mm(ps.all(), k.xT[:, c, i * 128:(i + 1) * 128], wv[:, c, :], start=(c == 0), stop=(c == 7))
            xs = k.xres[:, i, hh * 512:(hh + 1) * 512]
            P.tt("dve", xs, xs, ps.all(), ALU.add)


def final_norm_store(k, g, do_norm):
    P = k.P
    A = k.arena
    t0 = g * T
    if do_norm:
        gf = A.alloc(D_MODEL, F32)
        sv = k.final_norm.all().map(lambda ap: ap.rearrange("a d -> (a d)").partition_broadcast(128))
        P.dma("pool", [(gf.ap, sv.ap)], [sv], [gf])
    for i in range(NT):
        dst = k.y[t0 + i * 128: t0 + (i + 1) * 128, :]
        if do_norm:
            xt = k.xres[:, i, :]
            st = k.stat[i % 2]
            yo = A.alloc(D_MODEL, F32)
            P.act(yo, xt, AF.Square, accum=st[:, 0:1])
            act_rsqrt(P, st[:, 2:3], st[:, 0:1], 1.0 / D_MODEL, EPS)
            P.stt("dve", yo, xt, st[:, 2:3], gf, ALU.mult, ALU.mult)
            P.dma("pool", [(dst.ap, yo.ap)], [yo], [dst])
        else:
            P.dma("pool", [(dst.ap, k.xres[:, i, :].ap)], [k.xres[:, i, :]], [dst])


def group(k, g, stages):
    P = k.P
    t0 = g * T
    for i in range(NT):
        src = k.x[t0 + i * 128: t0 + (i + 1) * 128, :]
        P.dma("pool", [(k.xres[:, i, :].ap, src.ap)], [src], [k.xres[:, i, :]])
    P.tag = "ffn1"
    if "ffn1" in stages:
        if "mix" in stages and not getattr(k, "bias_built", False):
            k.bias_gen = build_bias_tiles(k)
            ffn(k, k.gcol["ffn1"], k.ffn1_wg, k.ffn1_wu, k.ffn1_wd, bg=k.bias_gen)
            for _ in k.bias_gen:
                pass
            k.bias_built = True
        else:
            ffn(k, k.gcol["ffn1"], k.ffn1_wg, k.ffn1_wu, k.ffn1_wd)
    if "mix" in stages:
        if not getattr(k, "bias_built", False):
            P.tag = "setup"
            for _ in build_bias_tiles(k):
                pass
            k.bias_built = True
        P.tag = "mixproj"
        mixer_proj(k, g)
        if "dsa" in stages:
            P.tag = "dsa"
            dsa(k, g)
            tap(k, "o_sa", k.o_sa.all(), [128, NT, 1024], BF16)
        P.tag = "dn"
        for hq in range(NH // NQ):
            if "dn0" not in stages:
                dn_quad(k, g, hq)
        tap(k, "o_dn", k.o_dn.all(), [128, NT, 1024], BF16)
        P.tag = "wo"
        if "wo" in stages:
            merge_wo(k, g)
    P.tag = "ffn2"
    if "ffn2" in stages:
        ffn(k, k.gcol["ffn2"], k.ffn2_wg, k.ffn2_wu, k.ffn2_wd)
    P.tag = "final"
    final_norm_store(k, g, "final" in stages)


ALL_STAGES = ("ffn1", "mix", "dsa", "wo", "ffn2", "final")

INPUT_ORDER = ["x", "ffn1_norm", "ffn1_wg", "ffn1_wu", "ffn1_wd", "mix_norm", "w_in", "conv_w", "a_log",
               "dt_bias", "dn_out_norm", "q_norm", "kv_norm", "w_uq", "w_uk", "w_uv", "w_iq", "idx_k_g",
               "idx_k_b", "rel_bias", "w_o", "ffn2_norm", "ffn2_wg", "ffn2_wu", "ffn2_wd", "final_norm"]


def make_in_maps(inputs, ncores=8):
    shared = {}
    for n in INPUT_ORDER:
        if n == "x":
            continue
        a = np.asarray(inputs[n], dtype=np.float32)
        if n == "final_norm":
            a = a.reshape(1, D_MODEL)
        elif n == "rel_bias":
            a = a.reshape(1, REL_BUCKETS * NH)
        elif n in ("w_uk", "w_uv"):
            a = a.reshape(KV_RANK, 1024)
        elif n == "conv_w":
            a = a.reshape(CONV_K, 3072)
        else:
            a = a.reshape(a.shape[1:]) if a.shape[0] == 1 and a.ndim == 3 else a
        shared[n] = np.ascontiguousarray(a)
    x = np.asarray(inputs["x"], dtype=np.float32)
    maps = []
    for c in range(ncores):
        m = dict(shared)
        m["x"] = np.ascontiguousarray(x[c])
        maps.append(m)
    return maps


_CACHE = {}


def kernel(**inputs):
    if "k" not in _CACHE:
        _CACHE["k"] = build(ngroups=8, stages=ALL_STAGES)
    k = _CACHE["k"]
    in_maps = make_in_maps(inputs, 8)
    res = run_bass_kernel_spmd(k.nc, in_maps, core_ids=list(range(8)))
    return np.stack([np.asarray(r["y"]) for r in res.results], axis=0).astype(np.float32)
```

```python
import math
import numpy as np
import concourse.bass as bass
import concourse.mybir as mybir
from concourse.bass_utils import run_bass_kernel_spmd

F32 = mybir.dt.float32
BF16 = mybir.dt.bfloat16
AF = mybir.ActivationFunctionType
ALU = mybir.AluOpType
AX = mybir.AxisListType

SAME_ENGINE_SYNC = True
ANNOTATE = False
SEM_LIMIT = 28000

D_MODEL = 1024; SEQ = 4096; D_FF = 2816
NH = 8; DK = 128; DV = 128; CONV_K = 4; CHUNK = 64
Q_RANK = 256; KV_RANK = 256; IDX_HEADS = 8; IDX_DIM = 64; TOPK = 256
REL_BUCKETS = 32; REL_MAX_DIST = 128
EPS = 1e-6
IN_WIDTH = 6744
C_Q0 = 0; C_K0 = 1024; C_V0 = 2048; C_Z0 = 3072; C_B0 = 4096; C_A0 = 4104
C_CQ0 = 4112; C_CKV0 = 4368; C_IK0 = 4624; C_IW0 = 4688; C_GA0 = 4696; C_GB0 = 5720
T = 512; NT = 4
NQ = 4
PTT = "dve"
ARENA_CHUNKS = 30
NIT = 14
NEG = -30000.0
NB_THR = [1, 2, 3, 4, 5, 6, 7, 8, 9, 10, 11, 12, 13, 14, 15, 16, 19, 21, 24, 27, 31, 35, 40, 46, 52, 59, 67, 77, 87, 99, 113]


class Slot:
    __slots__ = ("w", "r")

    def __init__(self):
        self.w = None
        self.r = {}


class View:
    __slots__ = ("buf", "ap", "slots", "aid")

    def __init__(self, buf, ap, slots, aid=None):
        self.buf = buf
        self.ap = ap
        self.slots = slots
        self.aid = aid

    def map(self, fn):
        return View(self.buf, fn(self.ap), self.slots, self.aid)

    def __getitem__(self, idx):
        return View(self.buf, self.ap[idx], self.slots, self.aid)

    def f32(self):
        return self.map(lambda ap: ap.bitcast(F32))

    def r(self, pattern, **kw):
        return self.map(lambda ap: ap.rearrange(pattern, **kw))


class Arena:
    def __init__(self, P, nchunks):
        self.buf = P.buf("arena", [128, nchunks * 1024], BF16, nslots=nchunks, slot_axis=1, slot_size=1024)
        self.n = nchunks
        self.pos = 0
        self.owner = [None] * nchunks
        self.aid = 0
        self.buf.arena = self

    def reserve(self, nch):
        self.lo = nch
        if self.pos < nch:
            self.pos = nch
        self.aid += 1
        for c in range(nch):
            self.owner[c] = self.aid
        v = self.buf[:, 0: nch * 1024]
        v.aid = self.aid
        return v

    def release(self):
        self.lo = 0

    def alloc(self, nelem, dtype=BF16):
        nb = nelem * (4 if dtype == F32 else 2)
        nch = (nb + 2047) // 2048
        lo = getattr(self, "lo", 0)
        assert nch <= self.n - lo
        if self.pos + nch > self.n:
            self.pos = lo
        a = self.pos
        self.pos += nch
        self.aid += 1
        for c in range(a, a + nch):
            self.owner[c] = self.aid
        v = self.buf[:, a * 1024: a * 1024 + nb // 2]
        v.aid = self.aid
        if dtype == F32:
            v = v.f32()
        return v


class Buf:
    def __init__(self, P, name, shape, dtype, space="sbuf", nslots=1, slot_axis=None, kind=None, slot_size=1):
        nc = P.nc
        self.slot_size = slot_size
        self.name = name
        self.shape = list(shape)
        self.dtype = dtype
        self.space = space
        if space == "sbuf":
            self.t = nc.alloc_sbuf_tensor(name, list(shape), dtype)
        elif space == "psum":
            self.t = nc.alloc_psum_tensor(name, list(shape), dtype)
        else:
            self.t = nc.dram_tensor(name, list(shape), dtype, kind=kind or "Internal")
        self.slot_axis = slot_axis
        self.slots = [Slot() for _ in range(nslots)]
        self.dsem = {}

    def _base(self):
        return self.t.ap() if self.space == "dram" else self.t

    def _slots_of(self, idx):
        if self.slot_axis is None:
            return [0]
        if not isinstance(idx, tuple):
            idx = (idx,)
        if len(idx) <= self.slot_axis:
            return list(range(len(self.slots)))
        s = idx[self.slot_axis]
        ss = self.slot_size
        if isinstance(s, int):
            return [s // ss]
        a, b, _ = s.indices(len(self.slots) * ss)
        return list(range(a // ss, (b - 1) // ss + 1))

    def __getitem__(self, idx):
        return View(self, self._base()[idx], self._slots_of(idx))

    def all(self):
        return View(self, self._base()[:], list(range(len(self.slots))))


class EngState:
    def __init__(self, name, sem, is_pe=False):
        self.name = name
        self.sem = sem
        self.count = 0
        self.waited = {}
        self.is_pe = is_pe


class Prog:
    ENG = ("pe", "act", "dve", "pool", "sp")

    def __init__(self, nc):
        self.nc = nc
        self.nsem = 0
        self.eng = {n: EngState(n, self._newsem("s_" + n), n == "pe") for n in self.ENG}
        self.streams = {n: [] for n in self.ENG}
        self.out_tokens = []
        self.tag = "setup"
        self.ninst = {n: 0 for n in self.ENG}

    def _newsem(self, name):
        self.nsem += 1
        return self.nc.alloc_semaphore("%s_%d" % (name, self.nsem))

    def buf(self, name, shape, dtype, **kw):
        return Buf(self, name, shape, dtype, **kw)

    def _deps(self, E, reads, writes):
        need = {}

        def add(tok):
            k = id(tok[0])
            if k not in need or need[k][1] < tok[1]:
                need[k] = tok

        for v in list(reads) + list(writes):
            if v.aid is not None:
                for s in v.slots:
                    assert v.buf.arena.owner[s] == v.aid, "arena buffer reused while live: %s" % v.buf.name
        for v in reads:
            for s in v.slots:
                sl = v.buf.slots[s]
                if sl.w is not None:
                    add(sl.w)
        for v in writes:
            for s in v.slots:
                sl = v.buf.slots[s]
                if sl.w is not None:
                    add(sl.w)
                for tok in sl.r.values():
                    add(tok)
        out = []
        for k, (sem, val) in need.items():
            if sem is E.sem and (E.is_pe or not SAME_ENGINE_SYNC):
                continue
            if E.waited.get(k, 0) >= val:
                continue
            E.waited[k] = val
            out.append((sem, val))
        return out

    def _mark(self, tok, reads, writes):
        k = id(tok[0])
        for v in reads:
            for s in v.slots:
                v.buf.slots[s].r[k] = tok
        for v in writes:
            for s in v.slots:
                sl = v.buf.slots[s]
                sl.w = tok
                sl.r = {}

    def op(self, ename, fn, reads, writes):
        E = self.eng[ename]
        waits = self._deps(E, reads, writes)
        if E.count >= SEM_LIMIT:
            E.sem = self._newsem("s_" + ename)
            E.count = 0
        E.count += 1
        tok = (E.sem, E.count)
        self.streams[ename].append((waits, [fn], E.sem, 1, self.tag))
        self.ninst[ename] += 1 + len(waits)
        self._mark(tok, reads, writes)

    NDSEM = 14

    def dma(self, qname, pairs, reads, writes, fresh=False, **kw):
        E = self.eng[qname]
        waits = self._deps(E, reads, writes)
        if fresh:
            sem = self._newsem("once")
            fns = [(lambda e, o=o, i=i: e.dma_start(out=o, in_=i, **kw)) for (o, i) in pairs]
            tok = (sem, 16 * len(pairs))
            self.streams[qname].append((waits, fns, sem, 16, self.tag))
            self.ninst[qname] += len(pairs) + len(waits)
            self._mark(tok, reads, writes)
            self._need = []
            self._sim(qname, tok, writes[0], reads, dma_bytes=1000000) if hasattr(self, "_sim") else None
            return
        if not hasattr(self, "dpool"):
            self.dpool = [[self._newsem("dma"), 0] for _ in range(self.NDSEM)]
            self.dpool_i = 0
        idx = self.dpool_i % self.NDSEM
        self.dpool_i += 1
        ds = self.dpool[idx]
        if ds[1] + 16 * len(pairs) > SEM_LIMIT:
            ds = [self._newsem("dma"), 0]
            self.dpool[idx] = ds
        if ds[1] > 0 and E.waited.get(id(ds[0]), 0) < ds[1]:
            E.waited[id(ds[0])] = ds[1]
            waits.append((ds[0], ds[1]))
        fns = [(lambda e, o=o, i=i: e.dma_start(out=o, in_=i, **kw)) for (o, i) in pairs]
        ds[1] += 16 * len(pairs)
        tok = (ds[0], ds[1])
        self.streams[qname].append((waits, fns, ds[0], 16, self.tag))
        self.ninst[qname] += len(pairs) + len(waits)
        self._mark(tok, reads, writes)
        if writes[0].buf.space == "dram":
            self.out_tokens.append(tok)

    def raw(self, ename, fns):
        self.streams[ename].append(([], fns, None, None, self.tag))
        self.ninst[ename] += len(fns)

    def finish(self, ename="pool"):
        last = {}
        for sem, val in self.out_tokens:
            if id(sem) not in last or last[id(sem)][1] < val:
                last[id(sem)] = (sem, val)
        self.streams[ename].append((list(last.values()), [], None, 0, "fin"))

    def emit(self):
        nc = self.nc

        def run(e, stream):
            for waits, fns, sem, inc, tag in stream:
                for (s, v) in waits:
                    e.wait_ge(s, v)
                if inc is None:
                    for fn in fns:
                        fn(e)
                    continue
                for fn in fns:
                    ins = fn(e).then_inc(sem, inc)
                    if ANNOTATE:
                        ins.annotate(tag)

        with nc.Block() as block:
            @block.tensor
            def _(e):
                run(e, self.streams["pe"])

            @block.scalar
            def _(e):
                run(e, self.streams["act"])

            @block.vector
            def _(e):
                run(e, self.streams["dve"])

            @block.gpsimd
            def _(e):
                run(e, self.streams["pool"])

            @block.sync
            def _(e):
                run(e, self.streams["sp"])

    def mm(self, out, lhsT, rhs, start=True, stop=True, **kw):
        self.op("pe", lambda e: e.matmul(out.ap, lhsT.ap, rhs.ap, start=start, stop=stop, **kw),
                [lhsT, rhs], [out])

    def tr(self, out, in_, ident):
        self.op("pe", lambda e: e.transpose(out.ap, in_.ap, ident.ap), [in_, ident], [out])

    def act(self, out, in_, func, bias=None, scale=None, accum=None):
        reads = [in_]
        writes = [out]
        kw = {}
        if bias is not None:
            if isinstance(bias, View):
                reads.append(bias)
                kw["bias"] = bias.ap
            else:
                kw["bias"] = bias
        if scale is not None:
            if isinstance(scale, View):
                reads.append(scale)
                kw["scale"] = scale.ap
            else:
                kw["scale"] = scale
        if accum is not None:
            writes.append(accum)
            kw["accum_out"] = accum.ap
        self.op("act", lambda e: e.activation(out=out.ap, in_=in_.ap, func=func, **kw), reads, writes)

    def ts(self, eng, out, in0, s1, op0, s2=None, op1=None, accum=None):
        reads = [in0]
        writes = [out]
        a1 = s1.ap if isinstance(s1, View) else s1
        a2 = s2.ap if isinstance(s2, View) else s2
        if isinstance(s1, View):
            reads.append(s1)
        if isinstance(s2, View):
            reads.append(s2)
        kw = {}
        if op1 is not None:
            kw["op1"] = op1
        if accum is not None:
            writes.append(accum)
            kw["accum_out"] = accum.ap
        self.op(eng, lambda e: e.tensor_scalar(out.ap, in0.ap, a1, a2, op0, **kw), reads, writes)

    def tt(self, eng, out, in0, in1, op):
        self.op(eng, lambda e: e.tensor_tensor(out.ap, in0.ap, in1.ap, op), [in0, in1], [out])

    def stt(self, eng, out, in0, scalar, in1, op0, op1):
        reads = [in0, in1]
        a = scalar.ap if isinstance(scalar, View) else scalar
        if isinstance(scalar, View):
            reads.append(scalar)
        self.op(eng, lambda e: e.scalar_tensor_tensor(out.ap, in0.ap, a, in1.ap, op0, op1), reads, [out])

    def copy(self, eng, out, in_):
        if eng == "act":
            self.op(eng, lambda e: e.copy(out.ap, in_.ap), [in_], [out])
        else:
            self.op(eng, lambda e: e.tensor_copy(out.ap, in_.ap), [in_], [out])

    def memset(self, eng, out, val):
        self.op(eng, lambda e: e.memset(out.ap, val), [], [out])

    def reduce(self, eng, out, in_, op, axis=AX.X):
        self.op(eng, lambda e: e.tensor_reduce(out.ap, in_.ap, axis, op), [in_], [out])

    def recip(self, out, in_):
        self.op("dve", lambda e: e.reciprocal(out.ap, in_.ap), [in_], [out])


class K:
    pass


def bc_last(v, n):
    return v.map(lambda ap: ap.unsqueeze(2).to_broadcast([ap.shape[0], ap.shape[1], n]))


def bc_mid(v, n):
    return v.map(lambda ap: ap.unsqueeze(1).to_broadcast([ap.shape[0], n, ap.shape[1]]))


def build(ngroups=8, stages=("ffn1",), taps=(), glist=None):
    nc = bass.Bass("TRN2", target_bir_lowering=False)
    P = Prog(nc)
    k = K()
    k.P = P
    k.taps = {}
    k.tapset = set(taps)
    k.stages = stages

    def din(name, shape):
        return P.buf(name, shape, F32, space="dram", kind="ExternalInput")

    k.x = din("x", [SEQ, D_MODEL])
    k.ffn1_norm = din("ffn1_norm", [1, D_MODEL])
    k.ffn1_wg = din("ffn1_wg", [D_MODEL, D_FF])
    k.ffn1_wu = din("ffn1_wu", [D_MODEL, D_FF])
    k.ffn1_wd = din("ffn1_wd", [D_FF, D_MODEL])
    k.mix_norm = din("mix_norm", [1, D_MODEL])
    k.w_in = din("w_in", [D_MODEL, IN_WIDTH])
    k.conv_w = din("conv_w", [CONV_K, 3072])
    k.a_log = din("a_log", [1, NH])
    k.dt_bias = din("dt_bias", [1, NH])
    k.dn_out_norm = din("dn_out_norm", [1, DV])
    k.q_norm = din("q_norm", [1, Q_RANK])
    k.kv_norm = din("kv_norm", [1, KV_RANK])
    k.w_uq = din("w_uq", [Q_RANK, 1024])
    k.w_uk = din("w_uk", [KV_RANK, 1024])
    k.w_uv = din("w_uv", [KV_RANK, 1024])
    k.w_iq = din("w_iq", [Q_RANK, 512])
    k.idx_k_g = din("idx_k_g", [1, IDX_DIM])
    k.idx_k_b = din("idx_k_b", [1, IDX_DIM])
    k.rel_bias = din("rel_bias", [1, REL_BUCKETS * NH])
    k.w_o = din("w_o", [D_MODEL, D_MODEL])
    k.ffn2_norm = din("ffn2_norm", [1, D_MODEL])
    k.ffn2_wg = din("ffn2_wg", [D_MODEL, D_FF])
    k.ffn2_wu = din("ffn2_wu", [D_MODEL, D_FF])
    k.ffn2_wd = din("ffn2_wd", [D_FF, D_MODEL])
    k.final_norm = din("final_norm", [1, D_MODEL])
    k.y = P.buf("y", [SEQ, D_MODEL], F32, space="dram", kind="ExternalOutput", nslots=SEQ // 128, slot_axis=0,
                slot_size=128)

    k.xres = P.buf("xres", [128, NT, D_MODEL], F32, nslots=NT, slot_axis=1)
    k.xn = [P.buf("xn0", [128, D_MODEL], BF16)] * 2
    k.xT = P.buf("xT", [128, 8, T], BF16)
    k.psF = [P.buf("psF%d" % i, [128, 512], F32, space="psum") for i in range(4)]
    k.psH = [P.buf("psH%d" % i, [128, 512], F32, space="psum") for i in range(2)]
    k.psF_i = 0
    k.psT = [P.buf("psT%d" % i, [128, 1024], BF16, space="psum") for i in range(2)]
    k.psT_i = 0
    k.stat = [P.buf("stat%d" % i, [128, 8], F32) for i in range(2)]
    k.gcol = {n: P.buf("gcol_" + n, [128, 8], F32) for n in ("ffn1", "mix", "ffn2")}
    k.ident_f = P.buf("ident_f", [128, 128], F32)
    k.dmat = P.buf("dmat", [128, 128], F32)
    k.ident_b = P.buf("ident_b", [128, 128], BF16)
    for n in ("TRIU", "TRIL", "STRICT", "BLOCK", "SELC0", "SELC1", "ONES"):
        setattr(k, n, P.buf("m_" + n, [128, 128], F32))
    k.zg = P.buf("zg", [128, NT, 1024], BF16)
    k.o_dn = P.buf("o_dn", [128, NT, 1024], BF16)
    k.o_sa = P.buf("o_sa", [128, NT, 1024], BF16)
    k.S = P.buf("S", [128, NH, DV], F32, nslots=2, slot_axis=1, slot_size=4)
    k.Sb = P.buf("Sb", [128, NH, DV], BF16, nslots=2, slot_axis=1, slot_size=4)
    k.halo = P.buf("halo", [128, 24, 3], F32, nslots=24, slot_axis=1)
    k.convw = P.buf("convw", [128, 24, 4], F32)
    k.dtb = P.buf("dtb", [128, NH], F32)
    k.nega = P.buf("nega", [128, NH], F32)
    k.gn_dn = P.buf("gn_dn", [128, DV], F32)
    k.beta = P.buf("beta", [128, NT, NH], F32)
    k.negb = P.buf("negb", [128, NT, NH], F32)
    k.gtok = P.buf("gtok", [128, NT, NH], F32)
    k.gst = P.buf("gst", [128, NT, 32], F32)
    k.egc = P.buf("egc", [128, NT, NH], F32)
    k.elast = P.buf("elast", [128, NT, NH], F32)
    k.bge = P.buf("bge", [128, NT, NH], F32)
    k.dec = P.buf("dec", [128, NT, 16], F32)
    k.ph = P.buf("phase", [128, 16384], BF16, nslots=32, slot_axis=1, slot_size=512)

    def phv(off, n):
        return k.ph[:, off:off + n]

    k.q_qT = phv(0, 2048).r("p (h t) -> p h t", h=NQ)
    k.q_kT = phv(2048, 2048).r("p (h t) -> p h t", h=NQ)
    k.q_vtm = phv(4096, 2048).r("p (i h d) -> p i h d", i=NT, h=NQ)
    k.q_ktm = phv(6144, 2048).r("p (i h d) -> p i h d", i=NT, h=NQ)
    k.q_X = [phv(8192 + i * 512, 512).r("p (h t) -> p h t", h=NQ) for i in range(NT)]
    k.q_XT = [phv(10240 + i * 512, 512).r("p (h t) -> p h t", h=NQ) for i in range(NT)]
    k.q_AT = [phv(12288 + i * 512, 512).r("p (h t) -> p h t", h=NQ) for i in range(NT)]
    k.q_IT = [phv(14336 + i * 512, 512).r("p (h t) -> p h t", h=NQ) for i in range(NT)]
    k.s_qidx = phv(0, 2048).r("p (h t) -> p h t", h=4)
    k.s_wuq = phv(2048, 2048).r("p (c f) -> p c f", c=2)
    k.s_wukT = phv(4096, 2048).r("p (h r) -> p h r", h=NH)
    k.s_wuv = phv(6144, 2048).r("p (c f) -> p c f", c=2)
    k.s_wiq = phv(8192, 1024).r("p (c f) -> p c f", c=2)
    k.s_cqT = phv(9216, 1024).r("p (c t) -> p c t", c=2)
    k.s_maskT = phv(10240, 4096)
    k.s_ql = phv(14336, 2048).r("p (h c q) -> p h c q", h=NH, c=2)
    k.rl = [P.buf("rl%d" % i, [128, 512], F32) for i in range(2)]
    k.rl_i = 0
    k.bb_rb = phv(0, 512).f32()
    k.bb_dl = phv(512, 512).f32()
    k.bb_dd = phv(1024, 256).f32()
    k.bb_ind = phv(1536, 256).f32()
    k.bb_acc = phv(2048, 2048).f32().r("p (h q) -> p h q", h=NH)
    NTT = SEQ // 128
    k.ckv_tm = P.buf("ckv_tm", [128, NTT, KV_RANK + 1], BF16, nslots=NTT, slot_axis=1)
    k.ckvT = P.buf("ckvT", [128, 2, SEQ], BF16, nslots=NTT, slot_axis=2, slot_size=128)
    k.kidxT = P.buf("kidxT", [128, SEQ], BF16, nslots=NTT, slot_axis=1, slot_size=128)
    k.biasD = P.buf("biasD", [128, NH, 128], BF16)
    k.bias1 = P.buf("bias1", [128, NH, 128], BF16)
    k.causneg = P.buf("causneg", [128, 128], F32)
    k.qn_bc = P.buf("qn_bc", [128, Q_RANK], F32)
    k.kvn_bc = P.buf("kvn_bc", [128, KV_RANK], F32)
    k.ikg_bc = P.buf("ikg_bc", [128, IDX_DIM], F32)
    k.ikb_bc = P.buf("ikb_bc", [128, IDX_DIM], F32)
    k.widx = P.buf("widx", [128, NT, IDX_HEADS], F32)
    k.thr = P.buf("thr", [128, 8], F32)
    k.thW = P.buf("thW", [128, NIT + 1], F32)
    k.pw2 = P.buf("pw2", [128, NIT + 1], F32)
    k.arena = Arena(P, ARENA_CHUNKS)

    convert_weights(k)
    setup(k)
    for g in (glist if glist is not None else range(ngroups)):
        group(k, g, stages)
    P.finish()
    P.emit()
    k.nc = nc
    return k


class BankPool:
    def __init__(self, banks):
        self.free = list(banks)

    def get(self):
        assert self.free, "PSUM bank pool exhausted"
        return self.free.pop(0)

    def put(self, b):
        self.free.append(b)


def act_rsqrt(P, out, in_, scale, eps):
    P.act(out, in_, AF.Ln, bias=eps, scale=scale)
    P.act(out, out, AF.Exp, scale=-0.5)


def act_sigmoid(P, out, in_):
    P.act(out, in_, AF.Exp, scale=-1.0)
    P.act(out, out, AF.Ln, bias=1.0)
    P.act(out, out, AF.Exp, scale=-1.0)


def next_psF(k):
    b = k.psF[k.psF_i % len(k.psF)]
    k.psF_i += 1
    return b


def next_psT(k):
    b = k.psT[k.psT_i % len(k.psT)]
    k.psT_i += 1
    return b


def tap(k, name, view, shape, dtype=F32):
    if name not in k.tapset:
        return
    cnt = k.taps.get(name, 0)
    k.taps[name] = cnt + 1
    d = k.P.buf("tap_%s_%d" % (name, cnt), shape, dtype, space="dram", kind="ExternalOutput")
    k.P.dma("sp", [(d.all().ap, view.ap)], [view], [d.all()])


def convert_weights(k):
    P = k.P
    nc = P.nc
    P.tag = "convert"

    W = {}
    order = ("ffn1_wg", "ffn1_wu", "ffn1_wd", "w_in", "w_uq", "w_uv", "w_iq", "w_uk", "w_o", "ffn2_wg", "ffn2_wu", "ffn2_wd")
    for nm in order:
        src = getattr(k, nm)
        W[nm] = P.buf(nm + "_bf", src.shape, BF16, space="dram")
        P.dma("pool", [(W[nm].all().ap, src.all().ap)], [src.all()], [W[nm].all()], fresh=True)
    for nm, b in W.items():
        setattr(k, nm, b)


def setup(k):
    P = k.P
    dm = k.dmat
    P.op("pool", lambda e: e.iota(dm.all().ap, [[1, 128]], base=0, channel_multiplier=-1,
                                  allow_small_or_imprecise_dtypes=True), [], [dm.all()])
    P.memset("dve", k.BLOCK.all(), 0.0)
    P.memset("dve", k.BLOCK[0:64, 0:64], 1.0)
    P.memset("dve", k.BLOCK[64:128, 64:128], 1.0)
    P.memset("dve", k.ONES.all(), 1.0)
    P.memset("dve", k.SELC0.all(), 0.0)
    P.memset("dve", k.SELC0[0:64, :], 1.0)
    P.memset("dve", k.SELC1.all(), 0.0)
    P.memset("dve", k.SELC1[64:128, :], 1.0)
    P.ts("dve", k.TRIU.all(), dm.all(), 0.0, ALU.is_ge)
    P.tt("dve", k.TRIU.all(), k.TRIU.all(), k.BLOCK.all(), ALU.mult)
    P.ts("dve", k.TRIL.all(), dm.all(), 0.0, ALU.is_le)
    P.tt("dve", k.TRIL.all(), k.TRIL.all(), k.BLOCK.all(), ALU.mult)
    P.ts("dve", k.STRICT.all(), dm.all(), 0.0, ALU.is_lt)
    P.tt("dve", k.STRICT.all(), k.STRICT.all(), k.BLOCK.all(), ALU.mult)
    P.ts("dve", k.ident_f.all(), dm.all(), 0.0, ALU.is_equal)
    P.copy("dve", k.ident_b.all(), k.ident_f.all())
    for n, src in (("ffn1", k.ffn1_norm), ("mix", k.mix_norm), ("ffn2", k.ffn2_norm)):
        sv = src.all().r("a (c p) -> p (a c)", p=128)
        P.dma("sp", [(k.gcol[n].all().ap, sv.ap)], [sv], [k.gcol[n].all()], allow_slow_non_contiguous=True)

    def bcast(dst, src):
        sv = src.all().map(lambda ap: ap.rearrange("a d -> (a d)").partition_broadcast(128))
        P.dma("sp", [(dst.all().ap, sv.ap)], [sv], [dst.all()])

    bcast(k.dtb, k.dt_bias)
    bcast(k.nega, k.a_log)
    P.act(k.nega.all(), k.nega.all(), AF.Exp)
    P.ts("dve", k.nega.all(), k.nega.all(), -1.0, ALU.mult)
    bcast(k.gn_dn, k.dn_out_norm)
    for kk in range(CONV_K):
        sv = k.conv_w[kk:kk + 1, :].r("a (c p) -> p (a c)", p=128)
        P.dma("sp", [(k.convw[:, :, kk].ap, sv.ap)], [sv], [k.convw.all()], allow_slow_non_contiguous=True)
    P.memset("dve", k.halo.all(), 0.0)
    for j in range(NIT + 1):
        P.memset("dve", k.pw2[:, j:j + 1], 2.0 ** -(j + 1))
    bcast(k.qn_bc, k.q_norm)
    bcast(k.kvn_bc, k.kv_norm)
    bcast(k.ikg_bc, k.idx_k_g)
    bcast(k.ikb_bc, k.idx_k_b)
    P.memset("dve", k.ckv_tm[:, :, KV_RANK:KV_RANK + 1], 1.0)
    P.ts("dve", k.causneg.all(), dm.all(), 0.0, ALU.is_gt, -1e30, ALU.mult)
    P.memset("dve", k.S.all(), 0.0)
    P.memset("dve", k.Sb.all(), 0.0)


def build_bias_tiles(k):
    P = k.P
    dm = k.dmat
    rb = k.bb_rb
    sv = k.rel_bias.all().map(lambda ap: ap.rearrange("a d -> (a d)").partition_broadcast(128))
    P.dma("sp", [(rb.ap, sv.ap)], [sv], [rb])
    dl = k.bb_dl
    P.tt("dve", dl[:, NH:REL_BUCKETS * NH], rb[:, NH:REL_BUCKETS * NH], rb[:, 0:(REL_BUCKETS - 1) * NH], ALU.subtract)
    P.tt("dve", dl[:, 0:NH], rb[:, 0:NH], rb[:, (REL_BUCKETS - 1) * NH:REL_BUCKETS * NH], ALU.subtract)
    for which, dst in ((0, k.biasD), (1, k.bias1)):
        dd = k.bb_dd
        P.ts("dve", dd, dm.all(), 128.0 * which, ALU.add)
        acc = k.bb_acc
        for h in range(NH):
            P.ts("dve", acc[:, h, :], dd, 0.0, ALU.mult, dl[:, h:h + 1], ALU.add)
        ind = k.bb_ind
        for b in range(1, REL_BUCKETS):
            P.ts("dve", ind, dd, float(NB_THR[b - 1]), ALU.is_ge)
            for h in range(NH):
                P.stt("dve", acc[:, h, :], ind, dl[:, b * NH + h:b * NH + h + 1], acc[:, h, :], ALU.mult, ALU.add)
            yield
        P.copy("dve", dst.all(), acc)


def rms_to_T(k, i, gcol, dstT):
    P = k.P
    xt = k.xres[:, i, :]
    st = k.stat[i % 2]
    xn = k.xn[i % 2]
    P.act(xn.all(), xt, AF.Square, accum=st[:, 0:1])
    act_rsqrt(P, st[:, 2:3], st[:, 0:1], 1.0 / D_MODEL, EPS)
    P.ts("dve", xn.all(), xt, st[:, 2:3], ALU.mult)
    pt = next_psT(k)
    for c in range(8):
        P.tr(pt[:, c * 128:(c + 1) * 128], xn[:, c * 128:(c + 1) * 128], k.ident_b.all())
    P.tt("dve", dstT[:, :, i * 128:(i + 1) * 128], pt.all().r("p (c t) -> p c t", c=8),
         bc_last(gcol.all(), 128), ALU.mult)


def wdma(k, dst_view, src_view):
    k.P.dma("sp", [(dst_view.ap, src_view.ap)], [src_view], [dst_view])


def wload_cols(k, w, c0, ncols):
    wt = k.arena.alloc(8 * ncols).r("p (c f) -> p c f", c=8)
    wdma(k, wt, w[:, c0:c0 + ncols].r("(c p) f -> p c f", p=128))
    return wt


def ffn(k, gcol, wg, wu, wd, bg=None):
    P = k.P
    for i in range(NT):
        rms_to_T(k, i, gcol, k.xT)
    nblk = (D_FF + 511) // 512

    def gate_up(fb):
        f0 = fb * 512
        fw = min(512, D_FF - f0)
        nfc = fw // 128
        wg_v = wload_cols(k, wg, f0, fw)
        wu_v = wload_cols(k, wu, f0, fw)
        hT = k.arena.alloc(nfc * T).r("p (c t) -> p c t", c=nfc)
        for fc in range(nfc):
            pg = next_psF(k)
            pu = next_psF(k)
            for c in range(8):
                P.mm(pg.all(), wg_v[:, c, fc * 128:(fc + 1) * 128], k.xT[:, c, :], start=(c == 0), stop=(c == 7))
            for c in range(8):
                P.mm(pu.all(), wu_v[:, c, fc * 128:(fc + 1) * 128], k.xT[:, c, :], start=(c == 0), stop=(c == 7))
            sg = k.rl[k.rl_i % 2].all()
            k.rl_i += 1
            P.act(sg, pg.all(), AF.Silu)
            P.tt("dve", hT[:, fc, :], sg, pu.all(), ALU.mult)
        return (hT, f0, fw, nfc)

    def down(hT, f0, fw, nfc):
        wd_v = k.arena.alloc(nfc * 1024).r("p (c d) -> p c d", c=nfc)
        wdma(k, wd_v, wd[f0:f0 + fw, :].r("(c p) d -> p c d", p=128))
        for i in range(NT):
            for hh in range(2):
                po = k.psH[(2 * i + hh) % 2]
                for fc in range(nfc):
                    P.mm(po.all(), hT[:, fc, i * 128:(i + 1) * 128], wd_v[:, fc, hh * 512:(hh + 1) * 512],
                         start=(fc == 0), stop=(fc == nfc - 1))
                xs = k.xres[:, i, hh * 512:(hh + 1) * 512]
                P.stt("dve", xs, po.all(), 0.5, xs, ALU.mult, ALU.add)

    def advance(n):
        if bg is not None:
            for _ in range(n):
                if next(bg, "end") == "end":
                    break

    prev = None
    for fb in range(nblk):
        cur = gate_up(fb)
        advance(6)
        if prev is not None:
            down(*prev)
        advance(6)
        prev = cur
    down(*prev)


def mixer_proj(k, g):
    P = k.P
    for i in range(NT):
        rms_to_T(k, i, k.gcol["mix"], k.xT)
    for hh in range(2):
        wz = wload_cols(k, k.w_in, C_Z0 + hh * 512, 512)
        for i in range(NT):
            ps = next_psF(k)
            for c in range(8):
                P.mm(ps.all(), k.xT[:, c, i * 128:(i + 1) * 128], wz[:, c, :], start=(c == 0), stop=(c == 7))
            sg = k.rl[k.rl_i % 2].all()
            k.rl_i += 1
            P.act(sg, ps.all(), AF.Silu)
            P.tt("dve", k.zg[:, i, hh * 512:(hh + 1) * 512].r("p (h d) -> p h d", h=4),
                 sg.r("p (h d) -> p h d", h=4), bc_mid(k.gn_dn.all(), 4), ALU.mult)
    k.smallp = k.arena.alloc(NT * 600, F32).r("p (i c) -> p i c", i=NT)
    w1 = wload_cols(k, k.w_in, 4096, 512)
    w2 = wload_cols(k, k.w_in, 4608, 88)
    for i in range(NT):
        ps = next_psF(k)
        for c in range(8):
            P.mm(ps.all(), k.xT[:, c, i * 128:(i + 1) * 128], w1[:, c, :], start=(c == 0), stop=(c == 7))
        P.copy("act", k.smallp[:, i, 0:512], ps.all())
        ps2 = next_psF(k)
        for c in range(8):
            P.mm(ps2[:, 0:88], k.xT[:, c, i * 128:(i + 1) * 128], w2[:, c, :], start=(c == 0), stop=(c == 7))
        P.copy("act", k.smallp[:, i, 512:600], ps2[:, 0:88])
    act_sigmoid(P, k.beta.all(), k.smallp[:, :, 0:8])
    P.ts("dve", k.negb.all(), k.beta.all(), -1.0, ALU.mult)
    P.tt("dve", k.gtok.all(), k.smallp[:, :, 8:16], bc_mid(k.dtb.all(), NT), ALU.add)
    P.act(k.gtok.all(), k.gtok.all(), AF.Exp)
    P.act(k.gtok.all(), k.gtok.all(), AF.Ln, bias=1.0)
    P.tt("dve", k.gtok.all(), k.gtok.all(), bc_mid(k.nega.all(), NT), ALU.mult)
    ps = next_psF(k)
    for i in range(NT):
        for j, m in enumerate((k.TRIU, k.BLOCK, k.SELC0, k.SELC1)):
            P.mm(ps[:, i * 32 + j * 8: i * 32 + j * 8 + 8], m.all(), k.gtok[:, i, :])
    P.copy("act", k.gst.all(), ps[:, 0:NT * 32].r("p (i c) -> p i c", i=NT))
    P.act(k.egc.all(), k.gst[:, :, 0:8], AF.Exp)
    P.tt("dve", k.elast.all(), k.gst[:, :, 8:16], k.gst[:, :, 0:8], ALU.subtract)
    P.act(k.elast.all(), k.elast.all(), AF.Exp)
    P.act(k.dec.all(), k.gst[:, :, 16:32], AF.Exp)
    P.tt("dve", k.bge.all(), k.beta.all(), k.egc.all(), ALU.mult)
    tap(k, "beta", k.beta.all(), [128, NT, NH])
    tap(k, "gtok", k.gtok.all(), [128, NT, NH])
    dsa_prep(k, g)


def dn_quad(k, g, hq):
    P = k.P
    A = k.arena
    hs = hq * NQ
    P.tag = "dnA"
    qT, kT, v_tm, k_tm = k.q_qT, k.q_kT, k.q_vtm, k.q_ktm
    hsl = slice(hs, hs + NQ)
    r4 = lambda v: v.r("p (h t) -> p h t", h=NQ)
    rd = lambda v: v.r("p (h d) -> p h d", h=NQ)
    FP = BankPool(k.psF + k.psH)
    TP = BankPool(k.psT)

    wvs = {}

    def chainA(kind, hl):
        ch = kind * 8 + hs + hl
        st8 = {}

        def a1():
            if hl == 0:
                wvs[kind] = wload_cols(k, k.w_in, kind * 1024 + hs * 128, NQ * 128)
            wv = wvs[kind]
            ps = FP.get()
            st8["ps"] = ps
            for c in range(8):
                P.mm(ps.all(), wv[:, c, hl * 128:(hl + 1) * 128], k.xT[:, c, :], start=(c == 0), stop=(c == 7))

        def a2a():
            P.tag = "dnA"
            ps = st8["ps"]
            cb = A.alloc(1024, F32)
            st8["cb"] = cb
            P.copy("act", cb[:, 0:3], k.halo[:, ch, :])
            P.copy("act", cb[:, 3:3 + T], ps.all())
            FP.put(ps)
            P.copy("act", k.halo[:, ch, :], cb[:, T:T + 3])

        def a2b():
            P.tag = "dnA"
            cb = st8["cb"]
            yv = A.alloc(T, F32)
            st8["yv"] = yv
            P.ts("dve", yv, cb[:, 3:3 + T], k.convw[:, ch, 3:4], ALU.mult)
            for kk in range(3):
                P.stt("dve", yv, cb[:, kk:kk + T], k.convw[:, ch, kk:kk + 1], yv, ALU.mult, ALU.add)

        def a2c():
            P.tag = "dnA"
            sv = A.alloc(T, F32)
            st8["sv"] = sv
            act_sigmoid(P, sv, st8["yv"])

        def a2d():
            P.tag = "dnA"
            cb, yv, sv = st8["cb"], st8["yv"], st8["sv"]
            if kind == 2:
                sb = cb[:, 0:T // 2].map(lambda ap: ap.bitcast(BF16))
                st8["sb"] = sb
                P.tt("dve", sb, sv, yv, ALU.mult)
            else:
                P.tt("dve", sv, sv, yv, ALU.mult)
                sq = cb[:, 0:T]
                st8["sq"] = sq
                P.act(sq, sv, AF.Square)

        def a3():
            if kind == 2:
                pt = TP.get()
                st8["pt"] = pt
                for i in range(NT):
                    P.tr(pt[:, i * 128:(i + 1) * 128], st8["sb"][:, i * 128:(i + 1) * 128], k.ident_b.all())
            else:
                pss = FP.get()
                st8["pss"] = pss
                P.mm(pss.all(), k.ONES.all(), st8["sq"])

        def a4():
            P.tag = "dnA"
            if kind == 2:
                P.copy("act", v_tm[:, :, hl, :], st8["pt"][:, 0:NT * 128].r("p (i d) -> p i d", i=NT))
                TP.put(st8["pt"])
            else:
                act_rsqrt(P, st8["yv"], st8["pss"].all(), 1.0, EPS)
                FP.put(st8["pss"])

        def a4m():
            P.tag = "dnA"
            rn = st8["yv"]
            dst = (qT if kind == 0 else kT)[:, hl, :]
            st8["dst"] = dst
            if kind == 0:
                P.stt("dve", dst, st8["sv"], DK ** -0.5, rn, ALU.mult, ALU.mult)
            else:
                P.tt("dve", dst, st8["sv"], rn, ALU.mult)

        def a5():
            P.tag = "dnA"
            pt = TP.get()
            st8["pt"] = pt
            for i in range(NT):
                P.tr(pt[:, i * 128:(i + 1) * 128], st8["dst"][:, i * 128:(i + 1) * 128], k.ident_b.all())

        def a6():
            P.copy("act", k_tm[:, :, hl, :], st8["pt"][:, 0:NT * 128].r("p (i d) -> p i d", i=NT))
            TP.put(st8["pt"])

        stages = [a1, a2a, a2b, a2c, a2d, a3, a4]
        if kind != 2:
            stages += [a4m]
        if kind == 1:
            stages += [a5, a6]
        return stages

    diagonal([chainA(kind, hl) for kind in (1, 0, 2) for hl in range(NQ)])
    tap(k, "qT", qT, [128, NQ, T], BF16)
    tap(k, "kT", kT, [128, NQ, T], BF16)
    tap(k, "v_tm", v_tm, [128, NT, NQ, 128], BF16)

    P.tag = "dnB"
    X = k.q_X; XT = k.q_XT; AT = k.q_AT; IT = k.q_IT

    def chainBC(i):
        st8 = {}

        def b1():
            lg = r4(A.alloc(NQ * 128, F32))
            st8["lg"] = lg
            P.tt("dve", lg, bc_mid(k.TRIU.all(), NQ), bc_last(k.gtok[:, i, hsl], 128), ALU.mult)

        def b2():
            psG = FP.get()
            st8["psG"] = psG
            for hl in range(NQ):
                P.mm(psG[:, hl * 128:(hl + 1) * 128], st8["lg"][:, hl, :], k.STRICT.all())
            psK = FP.get()
            st8["psK"] = psK
            for hl in range(NQ):
                kt = kT[:, hl, i * 128:(i + 1) * 128]
                P.mm(psK[:, hl * 128:(hl + 1) * 128], kt, kt)

        def b3():
            E = st8["lg"]
            P.act(E, r4(st8["psG"].all()), AF.Exp)
            Ds = r4(A.alloc(NQ * 128, F32))
            P.tt(PTT, Ds, E, bc_mid(k.STRICT.all(), NQ), ALU.mult)
            P.tt(PTT, Ds, Ds, bc_last(k.negb[:, i, hsl], 128), ALU.mult)
            P.tt("dve", X[i], r4(st8["psK"].all()), Ds, ALU.mult)
            P.tt(PTT, E, E, bc_mid(k.TRIL.all(), NQ), ALU.mult)
            FP.put(st8["psG"])
            FP.put(st8["psK"])

        def b4():
            psQ = FP.get()
            st8["psQ"] = psQ
            for hl in range(NQ):
                P.mm(psQ[:, hl * 128:(hl + 1) * 128], qT[:, hl, i * 128:(i + 1) * 128], kT[:, hl, i * 128:(i + 1) * 128])
            pt = TP.get()
            st8["pt"] = pt
            for hl in range(NQ):
                P.tr(pt[:, hl * 128:(hl + 1) * 128], X[i][:, hl, :], k.ident_b.all())
            psA0 = FP.get()
            st8["psA0"] = psA0
            for hl in range(NQ):
                P.mm(psA0[:, hl * 128:(hl + 1) * 128], X[i][:, hl, :], k.ident_b.all(), start=True, stop=False)
                P.mm(psA0[:, hl * 128:(hl + 1) * 128], k.ident_b.all(), k.ident_b.all(), start=False, stop=True)

        def b5():
            intra = r4(A.alloc(NQ * 128))
            st8["intra"] = intra
            P.tt("dve", intra, r4(st8["psQ"].all()), st8["lg"], ALU.mult)
            P.copy("act", XT[i], r4(st8["pt"][:, 0:NQ * 128]))
            P.copy("act", AT[i], r4(st8["psA0"].all()))
            FP.put(st8["psQ"])
            FP.put(st8["psA0"])
            TP.put(st8["pt"])

        def b6():
            pt = TP.get()
            st8["pt2"] = pt
            for hl in range(NQ):
                P.tr(pt[:, hl * 128:(hl + 1) * 128], st8["intra"][:, hl, :], k.ident_b.all())

        def b7():
            P.copy("act", IT[i], r4(st8["pt2"][:, 0:NQ * 128]))
            TP.put(st8["pt2"])

        stages = [b1, b2, b3, b4, b5, b6, b7]
        for m in range(5):
            def c1(m=m):
                P.tag = "dnC"
                psX = FP.get()
                st8["psX"] = psX
                for hl in range(NQ):
                    P.mm(psX[:, hl * 128:(hl + 1) * 128], XT[i][:, hl, :], X[i][:, hl, :])
                if m < 4:
                    psXT = FP.get()
                    st8["psXT"] = psXT
                    for hl in range(NQ):
                        P.mm(psXT[:, hl * 128:(hl + 1) * 128], X[i][:, hl, :], XT[i][:, hl, :])

            def c2(m=m):
                P.copy("act", X[i], r4(st8["psX"].all()))
                FP.put(st8["psX"])
                if m < 4:
                    P.copy("dve", XT[i], r4(st8["psXT"].all()))
                    FP.put(st8["psXT"])

            def c3(m=m):
                psA = FP.get()
                st8["psA"] = psA
                for hl in range(NQ):
                    P.mm(psA[:, hl * 128:(hl + 1) * 128], k.ident_b.all(), AT[i][:, hl, :], start=True, stop=False)
                    P.mm(psA[:, hl * 128:(hl + 1) * 128], X[i][:, hl, :], AT[i][:, hl, :], start=False, stop=True)

            def c4(m=m):
                P.copy("act", AT[i], r4(st8["psA"].all()))
                FP.put(st8["psA"])

            stages += [c1, c2, c3, c4]
        return stages

    diagonal([chainBC(i) for i in range(NT)])

    P.tag = "dnD"
    prep = []
    for i in range(NT):
        vb = rd(A.alloc(NQ * 128))
        P.tt(PTT, vb, v_tm[:, i, :, :], bc_last(k.beta[:, i, hsl], 128), ALU.mult)
        kbg = rd(A.alloc(NQ * 128))
        P.tt(PTT, kbg, k_tm[:, i, :, :], bc_last(k.bge[:, i, hsl], 128), ALU.mult)
        kdec = rd(A.alloc(NQ * 128))
        P.tt(PTT, kdec, k_tm[:, i, :, :], bc_last(k.elast[:, i, hsl], 128), ALU.mult)
        prep.append([vb, kbg, kdec])
    for i in range(NT):
        vb, kbg, kdec = prep[i]
        psU = next_psF(k)
        for hl in range(NQ):
            P.mm(psU[:, hl * 128:(hl + 1) * 128], AT[i][:, hl, :], vb[:, hl, :])
        psW = next_psF(k)
        for hl in range(NQ):
            P.mm(psW[:, hl * 128:(hl + 1) * 128], kbg[:, hl, :], AT[i][:, hl, :])
        prep[i] += [psU, psW]
        if i % 2 == 1 or i == NT - 1:
            for ii in range(i - (i % 2), i + 1):
                u = rd(A.alloc(NQ * 128, F32))
                P.copy("act", u, rd(prep[ii][3].all()))
                wT = X[ii]
                P.copy("dve", wT, r4(prep[ii][4].all()))
                prep[ii] += [u, wT]
    for i in range(NT):
        vb, kbg, kdec, _, _, u, wT = prep[i]
        o = rd(A.alloc(NQ * 128, F32))
        vn = XT[i]
        tmp = rd(A.alloc(NQ * 128, F32))
        for c in range(2):
            rs = slice(c * 64, c * 64 + 64)
            Sb = k.Sb[:, hs:hs + NQ, :]
            Sf = k.S[:, hs:hs + NQ, :]
            psA = next_psF(k)
            for hl in range(NQ):
                P.mm(psA[:, hl * 128:(hl + 1) * 128], wT[:, hl, :], Sb[:, hl, :])
            psB1 = next_psF(k)
            for hl in range(NQ):
                P.mm(psB1[:, hl * 128:(hl + 1) * 128], qT[:, hl, i * 128:(i + 1) * 128], Sb[:, hl, :])
            P.tt("dve", vn[rs], u[rs], rd(psA[rs, :]), ALU.subtract)
            P.tt(PTT, Sf, Sf, bc_last(k.dec[:, i, c * 8 + hs: c * 8 + hs + NQ], 128), ALU.mult)
            psB2 = next_psF(k)
            for hl in range(NQ):
                P.mm(psB2[:, hl * 128:(hl + 1) * 128], IT[i][rs, hl, :], vn[rs, hl, :])
            psS = next_psF(k)
            for hl in range(NQ):
                P.mm(psS[:, hl * 128:(hl + 1) * 128], kdec[rs, hl, :], vn[rs, hl, :])
            P.tt("dve", Sf, Sf, rd(psS.all()), ALU.add)
            P.copy("act", Sb, Sf)
            P.tt("dve", o[rs], rd(psB1[rs, :]), bc_last(k.egc[rs, i, hsl], 128), ALU.mult)
            P.copy("act", tmp[rs], rd(psB2[rs, :]))
            P.tt("dve", o[rs], o[rs], tmp[rs], ALU.add)
        sq = tmp
        P.act(sq, o, AF.Square)
        st = k.stat[i % 2]
        P.reduce("dve", st[:, 0:NQ], sq, ALU.add)
        act_rsqrt(P, st[:, 0:NQ], st[:, 0:NQ], 1.0 / DV, EPS)
        P.tt("dve", o, o, bc_last(st[:, 0:NQ], 128), ALU.mult)
        P.tt("dve", rd(k.o_dn[:, i, hs * 128:(hs + NQ) * 128]), o,
             rd(k.zg[:, i, hs * 128:(hs + NQ) * 128]), ALU.mult)
        tap(k, "o_raw", o, [128, NQ, 128])


def dsa_prep(k, g):
    P = k.P
    A = k.arena
    sp = k.smallp
    cqn = A.alloc(NT * 256).r("p (i r) -> p i r", i=NT)
    for (c0, gn, dst) in ((16, k.qn_bc, cqn), (272, k.kvn_bc, None)):
        sq = A.alloc(NT * 256, F32).r("p (i r) -> p i r", i=NT)
        P.tt("dve", sq, sp[:, :, c0:c0 + 256], sp[:, :, c0:c0 + 256], ALU.mult)
        st = k.stat[0]
        P.reduce("dve", st[:, 0:NT], sq, ALU.add)
        act_rsqrt(P, st[:, 0:NT], st[:, 0:NT], 1.0 / 256, EPS)
        P.tt("dve", sq, sp[:, :, c0:c0 + 256], bc_last(st[:, 0:NT], 256), ALU.mult)
        if dst is None:
            dst = k.ckv_tm[:, g * NT:(g + 1) * NT, 0:KV_RANK]
        P.tt("dve", dst, sq, bc_mid(gn.all(), NT), ALU.mult)
    ik = sp[:, :, 528:592]
    st = k.stat[1]
    P.reduce("dve", st[:, 0:NT], ik, ALU.add)
    P.ts("dve", st[:, 0:NT], st[:, 0:NT], -1.0 / IDX_DIM, ALU.mult)
    xc = A.alloc(NT * 64, F32).r("p (i d) -> p i d", i=NT)
    P.tt("dve", xc, ik, bc_last(st[:, 0:NT], 64), ALU.add)
    sq = A.alloc(NT * 64, F32).r("p (i d) -> p i d", i=NT)
    P.tt("dve", sq, xc, xc, ALU.mult)
    P.reduce("dve", st[:, 4:4 + NT], sq, ALU.add)
    act_rsqrt(P, st[:, 4:4 + NT], st[:, 4:4 + NT], 1.0 / IDX_DIM, EPS)
    P.tt("dve", xc, xc, bc_last(st[:, 4:4 + NT], 64), ALU.mult)
    P.tt("dve", xc, xc, bc_mid(k.ikg_bc.all(), NT), ALU.mult)
    kd = A.alloc(NT * 128).r("p (i e d) -> p i e d", i=NT, e=2)
    for e in range(2):
        P.tt("dve", kd[:, :, e, :], xc, bc_mid(k.ikb_bc.all(), NT), ALU.add)
    P.ts("dve", k.widx.all(), sp[:, :, 592:600], (IDX_HEADS ** -0.5) * (IDX_DIM ** -0.5), ALU.mult)
    for i in range(NT):
        gi = g * NT + i
        pt = next_psT(k)
        for c in range(2):
            P.tr(pt[:, c * 128:(c + 1) * 128], cqn[:, i, c * 128:(c + 1) * 128], k.ident_b.all())
        for c in range(2):
            P.tr(pt[:, (2 + c) * 128:(3 + c) * 128], k.ckv_tm[:, gi, c * 128:(c + 1) * 128], k.ident_b.all())
        P.tr(pt[:, 512:640], kd[:, i, :, :].r("p e d -> p (e d)"), k.ident_b.all())
        P.copy("act", k.s_cqT[:, :, i * 128:(i + 1) * 128], pt[:, 0:256].r("p (c t) -> p c t", c=2))
        P.copy("act", k.ckvT[:, :, gi * 128:(gi + 1) * 128], pt[:, 256:512].r("p (c t) -> p c t", c=2))
        P.copy("act", k.kidxT[:, gi * 128:(gi + 1) * 128], pt[:, 512:640])


def dsa(k, g):
    P = k.P
    A = k.arena
    wdma(k, k.s_wuq, k.w_uq.all().r("(c p) f -> p c f", p=128))
    wdma(k, k.s_wuv, k.w_uv.all().r("(c p) f -> p c f", p=128))
    wdma(k, k.s_wiq, k.w_iq.all().r("(c p) f -> p c f", p=128))
    wuk = A.alloc(2 * 1024).r("p (c f) -> p c f", c=2)
    wdma(k, wuk, k.w_uk.all().r("(c p) f -> p c f", p=128))
    for h in range(NH):
        pt = next_psT(k)
        for c in range(2):
            P.tr(pt[:, c * 128:(c + 1) * 128], wuk[:, c, h * 128:(h + 1) * 128], k.ident_b.all())
        P.copy("act", k.s_wukT[:, h, :], pt[:, 0:256])
    for hp in range(4):
        ps = next_psF(k)
        for c in range(2):
            P.mm(ps.all(), k.s_wiq[:, c, hp * 128:(hp + 1) * 128], k.s_cqT[:, c, :], start=(c == 0), stop=(c == 1))
        P.copy("act", k.s_qidx[:, hp, :], ps.all())
    tap(k, "ckv", k.ckv_tm[:, g * NT:(g + 1) * NT, :], [128, NT, KV_RANK + 1], BF16)
    tap(k, "cqT", k.s_cqT, [128, 2, T], BF16)
    tap(k, "kidxT", k.kidxT[:, g * T:(g + 1) * T], [128, T], BF16)
    tap(k, "qidx", k.s_qidx, [128, 4, T], BF16)
    tap(k, "widx", k.widx.all(), [128, NT, 8])
    fx = k.arena.reserve(12)
    score_buf = fx[:, 0:8192].f32()
    junk_buf = fx[:, 8192:12288]
    FP = BankPool(k.psF)
    TP = BankPool(k.psT)
    diagonal(dsa_score_chains(k, g, 0, score_buf, FP))
    diagonal(dsa_thr_chains(k, g, 0, score_buf, junk_buf))
    dsa_mask(k, g, 0, score_buf, junk_buf)
    for i in range(NT):
        att = dsa_att_chains(k, g, i, FP, TP)
        if i + 1 < NT:
            aux = dsa_score_chains(k, g, i + 1, score_buf, FP) + [[lambda: None], [lambda: None]] \
                + dsa_thr_chains(k, g, i + 1, score_buf, junk_buf)
            nsc = len(aux) - (NIT + 1 if (g * NT + i + 2) * 128 > TOPK else 1)
            caux = [0.7 if c < nsc else 5.0 for c in range(len(aux))]
            tot_aux = sum(caux)
            merged = []
            ia = 0
            acc = 0.0
            for c in range(len(att)):
                merged.append(att[c])
                target = tot_aux * (c + 1) / len(att)
                while ia < len(aux) and acc + 0.5 * caux[ia] <= target:
                    merged.append(aux[ia])
                    acc += caux[ia]
                    ia += 1
            merged += aux[ia:]
        else:
            merged = att
        diagonal(merged)
        if i + 1 < NT:
            dsa_mask(k, g, i + 1, score_buf, junk_buf)
    k.arena.release()


def dsa_score_chains(k, g, i, score_buf, FP):
    P = k.P
    gi = g * NT + i
    nk = (gi + 1) * 128
    tq = slice(i * 128, (i + 1) * 128)
    score = score_buf[:, 0:nk]
    chains = []
    for kb in range(0, nk, 512):
        kw = min(512, nk - kb)
        for h in range(IDX_HEADS):
            hp, e = h // 2, h % 2
            st8 = {}

            def s1(st8=st8, hp=hp, e=e, kb=kb, kw=kw):
                P.tag = "dsa_score"
                ps = FP.get()
                st8["ps"] = ps
                P.mm(ps[:, 0:kw], k.s_qidx[e * 64:(e + 1) * 64, hp, tq], k.kidxT[e * 64:(e + 1) * 64, kb:kb + kw])

            def s2(st8=st8, kw=kw):
                P.tag = "dsa_score"
                rl = k.rl[k.rl_i % 2].all()
                k.rl_i += 1
                st8["rl"] = rl
                P.act(rl[:, 0:kw], st8["ps"][:, 0:kw], AF.Relu)
                FP.put(st8["ps"])

            def s3(st8=st8, h=h, kb=kb, kw=kw):
                P.tag = "dsa_score"
                rl = st8["rl"]
                if h == 0:
                    P.ts("dve", score[:, kb:kb + kw], rl[:, 0:kw], k.widx[:, i, h:h + 1], ALU.mult)
                else:
                    P.stt("dve", score[:, kb:kb + kw], rl[:, 0:kw], k.widx[:, i, h:h + 1], score[:, kb:kb + kw],
                          ALU.mult, ALU.add)

            chains.append([s1, s2, s3])

    def causal():
        P.tag = "dsa_score"
        P.tt("dve", score[:, gi * 128:nk], score[:, gi * 128:nk], k.causneg.all(), ALU.add)

    chains.append([lambda: None, lambda: None, causal])
    return chains


def dsa_thr_chains(k, g, i, score_buf, junk_buf):
    P = k.P
    gi = g * NT + i
    nk = (gi + 1) * 128
    score = score_buf[:, 0:nk]
    th = k.thr
    chains = []
    if nk > TOPK:
        W = k.thW

        def init():
            P.tag = "dsa_thr"
            P.reduce("dve", th[:, 0:1], score[:, 0:gi * 128], ALU.min)
            P.reduce("dve", th[:, 1:2], score, ALU.max)
            P.tt("dve", th[:, 1:2], th[:, 1:2], th[:, 0:1], ALU.subtract)
            P.ts("dve", W.all(), k.pw2.all(), th[:, 1:2], ALU.mult)
            P.tt("dve", th[:, 2:3], th[:, 0:1], W[:, 0:1], ALU.add)

        chains.append([init])
        junk = junk_buf[:, 0:nk]

        def make_it(j):
            def it():
                P.tag = "dsa_thr"
                P.ts("dve", junk, score, th[:, 2:3], ALU.is_ge, 0.0, ALU.add, accum=th[:, 3:4])
                P.stt("dve", th[:, 4:5], th[:, 3:4], float(TOPK), W[:, j:j + 1], ALU.is_ge, ALU.mult)
                if j < NIT - 1:
                    P.stt("dve", th[:, 2:3], th[:, 4:5], W[:, j + 1:j + 2], th[:, 2:3], ALU.subtract, ALU.add)
                else:
                    P.stt("dve", th[:, 0:1], th[:, 4:5], W[:, j:j + 1], th[:, 2:3], ALU.subtract, ALU.add)
            return it

        for j in range(NIT):
            chains.append([make_it(j)])
    else:
        def init0():
            P.tag = "dsa_thr"
            P.memset("dve", th[:, 0:1], -1e29)

        chains.append([init0])
    return chains


def dsa_mask(k, g, i, score_buf, junk_buf):
    P = k.P
    P.tag = "dsa_thr"
    gi = g * NT + i
    nkt = gi + 1
    nk = nkt * 128
    score = score_buf[:, 0:nk]
    mk = junk_buf[:, 0:nk]
    P.ts("dve", mk, score, k.thr[:, 0:1], ALU.is_lt, NEG, ALU.mult)
    maskT = k.s_maskT[:, 0:nk].r("p (j q) -> p j q", j=nkt)
    for j0 in range(0, nkt, 8):
        nj = min(8, nkt - j0)
        pt = next_psT(k)
        for jj in range(nj):
            P.tr(pt[:, jj * 128:(jj + 1) * 128], mk[:, (j0 + jj) * 128:(j0 + jj + 1) * 128], k.ident_b.all())
        P.copy("act", maskT[:, j0:j0 + nj, :], pt[:, 0:nj * 128].r("p (j q) -> p j q", j=nj))


def dsa_att_chains(k, g, i, FP, TP):
    P = k.P
    A = k.arena
    gi = g * NT + i
    nkt = gi + 1
    nk = nkt * 128
    tq = slice(i * 128, (i + 1) * 128)
    maskT = k.s_maskT[:, 0:nk].r("p (j q) -> p j q", j=nkt)
    P.tag = "dsa_att"
    qhall = A.alloc(NH * 128).r("p (h q) -> p h q", h=NH)
    for h in range(NH):
        ps = FP.get()
        for c in range(2):
            P.mm(ps[:, 0:128], k.s_wuq[:, c, h * 128:(h + 1) * 128], k.s_cqT[:, c, tq], start=(c == 0), stop=(c == 1))
        P.copy("act", qhall[:, h, :], ps[:, 0:128])
        FP.put(ps)
    qlall = k.s_ql
    for h in range(NH):
        ps2 = FP.get()
        for rc in range(2):
            P.mm(ps2[:, rc * 128:(rc + 1) * 128], k.s_wukT[:, h, rc * 128:(rc + 1) * 128], qhall[:, h, :])
        P.act(qlall[:, h, :, :], ps2[:, 0:256].r("p (c q) -> p c q", c=2), AF.Copy, scale=float(DK ** -0.5))
        FP.put(ps2)

    def make_chain(h, j0):
        nj = min(4, nkt - j0)
        ql = qlall[:, h, :, :]
        oa = k.psH[h % 2]
        st8 = {}

        def s1():
            P.tag = "dsa_att"
            pl = FP.get()
            st8["pl"] = pl
            for jj in range(nj):
                j = j0 + jj
                dst = pl[:, jj * 128:(jj + 1) * 128]
                P.mm(dst, k.ckvT[:, 0, j * 128:(j + 1) * 128], ql[:, 0, :], start=True, stop=False)
                P.mm(dst, k.ckvT[:, 1, j * 128:(j + 1) * 128], ql[:, 1, :], start=False, stop=False)
                near = k.biasD if j == gi else (k.bias1 if j == gi - 1 else None)
                P.mm(dst, k.ident_b.all(), maskT[:, j, :], start=False, stop=(near is None))
                if near is not None:
                    P.mm(dst, k.ident_b.all(), near[:, h, :], start=False, stop=True)

        def s3():
            P.tag = "dsa_att"
            pl = st8["pl"]
            pT = A.alloc(512).r("p (j q) -> p j q", j=4)
            st8["pT"] = pT
            P.act(pT[:, 0:nj, :], pl[:, 0:nj * 128].r("p (j q) -> p j q", j=nj), AF.Exp)
            FP.put(pl)

        def s4():
            P.tag = "dsa_att"
            pT = st8["pT"]
            for jj in range(nj):
                j = j0 + jj
                P.mm(oa[:, 0:KV_RANK + 1], pT[:, jj, :], k.ckv_tm[:, j, :], start=(j == 0), stop=(j == nkt - 1))

        stages = [s1, s3, s4]
        if j0 + nj >= nkt:
            def e1():
                P.tag = "dsa_att"
                stt = k.stat[h % 2]
                P.recip(stt[:, 0:1], oa[:, KV_RANK:KV_RANK + 1])
                ol = A.alloc(256)
                st8["ol"] = ol
                P.ts("dve", ol, oa[:, 0:KV_RANK], stt[:, 0:1], ALU.mult)

            def e2():
                P.tag = "dsa_att"
                pt = TP.get()
                st8["pt"] = pt
                for rc in range(2):
                    P.tr(pt[:, rc * 128:(rc + 1) * 128], st8["ol"][:, rc * 128:(rc + 1) * 128], k.ident_b.all())

            def e3():
                P.tag = "dsa_att"
                olT = A.alloc(256).r("p (c q) -> p c q", c=2)
                st8["olT"] = olT
                P.copy("act", olT, st8["pt"][:, 0:256].r("p (c q) -> p c q", c=2))
                TP.put(st8["pt"])

            def e4():
                P.tag = "dsa_att"
                pb = TP.get()
                st8["pb"] = pb
                po = pb[:, 0:256].map(lambda ap: ap.bitcast(F32))
                st8["po"] = po
                for rc in range(2):
                    P.mm(po, st8["olT"][:, rc, :], k.s_wuv[:, rc, h * 128:(h + 1) * 128],
                         start=(rc == 0), stop=(rc == 1))

            def e5():
                P.tag = "dsa_att"
                P.copy("act", k.o_sa[:, i, h * 128:(h + 1) * 128], st8["po"])
                TP.put(st8["pb"])

            stages += [e1, e2, e3, e4, e5]
        return stages

    return [make_chain(h, j0) for h in range(NH) for j0 in range(0, nkt, 4)]


def diagonal(chains):
    active = []
    nxt = 0
    while active or nxt < len(chains):
        for ent in list(active):
            ent[0][ent[1]]()
            ent[1] += 1
            if ent[1] >= len(ent[0]):
                active.remove(ent)
        if nxt < len(chains):
            ch = chains[nxt]
            nxt += 1
            ch[0]()
            if len(ch) > 1:
                active.append([ch, 1])


def merge_wo(k, g):
    P = k.P
    A = k.arena
    for which, src in ((0, k.o_dn), (1, k.o_sa)):
        for hh in range(2):
            wv = wload_cols(k, k.w_in, (C_GA0 if which == 0 else C_GB0) + hh * 512, 512)
            for i in range(NT):
                ps = next_psF(k)
                for c in range(8):
                    P.mm(ps.all(), k.xT[:, c, i * 128:(i + 1) * 128], wv[:, c, :], start=(c == 0), stop=(c == 7))
                sg = A.alloc(512, F32)
                P.act(sg, ps.all(), AF.Sigmoid)
                od = k.o_dn[:, i, hh * 512:(hh + 1) * 512]
                if which == 0:
                    P.tt("dve", od, sg, od, ALU.mult)
                else:
                    P.tt("dve", sg, sg, k.o_sa[:, i, hh * 512:(hh + 1) * 512], ALU.mult)
                    P.tt("dve", od, sg, od, ALU.add)
    tap(k, "merged", k.o_dn.all(), [128, NT, 1024], BF16)
    for i in range(NT):
        pt = next_psT(k)
        for c in range(8):
            P.tr(pt[:, c * 128:(c + 1) * 128], k.o_dn[:, i, c * 128:(c + 1) * 128], k.ident_b.all())
        P.copy("act", k.xT[:, :, i * 128:(i + 1) * 128], pt.all().r("p (c t) -> p c t", c=8))
    for hh in range(2):
        wv = wload_cols(k, k.w_o, hh * 512, 512)
        for i in range(NT):
            ps = next_psF(k)
            for c in range(8):
                P.mm(ps.all(), k.xT[:, c, i * 128:(i + 1) * 128], wv[:, c, :], start=(c == 0), stop=(c == 7))
            xs = k.xres[:, i, hh * 512:(hh + 1) * 512]
            P.tt("dve", xs, xs, ps.all(), ALU.add)


def final_norm_store(k, g, do_norm):
    P = k.P
    A = k.arena
    t0 = g * T
    if do_norm:
        gf = A.alloc(D_MODEL, F32)
        sv = k.final_norm.all().map(lambda ap: ap.rearrange("a d -> (a d)").partition_broadcast(128))
        P.dma("pool", [(gf.ap, sv.ap)], [sv], [gf])
    for i in range(NT):
        dst = k.y[t0 + i * 128: t0 + (i + 1) * 128, :]
        if do_norm:
            xt = k.xres[:, i, :]
            st = k.stat[i % 2]
            yo = A.alloc(D_MODEL, F32)
            P.act(yo, xt, AF.Square, accum=st[:, 0:1])
            act_rsqrt(P, st[:, 2:3], st[:, 0:1], 1.0 / D_MODEL, EPS)
            P.stt("dve", yo, xt, st[:, 2:3], gf, ALU.mult, ALU.mult)
            P.dma("pool", [(dst.ap, yo.ap)], [yo], [dst])
        else:
            P.dma("pool", [(dst.ap, k.xres[:, i, :].ap)], [k.xres[:, i, :]], [dst])


def group(k, g, stages):
    P = k.P
    t0 = g * T
    for i in range(NT):
        src = k.x[t0 + i * 128: t0 + (i + 1) * 128, :]
        P.dma("pool", [(k.xres[:, i, :].ap, src.ap)], [src], [k.xres[:, i, :]])
    P.tag = "ffn1"
    if "ffn1" in stages:
        if "mix" in stages and not getattr(k, "bias_built", False):
            k.bias_gen = build_bias_tiles(k)
            ffn(k, k.gcol["ffn1"], k.ffn1_wg, k.ffn1_wu, k.ffn1_wd, bg=k.bias_gen)
            for _ in k.bias_gen:
                pass
            k.bias_built = True
        else:
            ffn(k, k.gcol["ffn1"], k.ffn1_wg, k.ffn1_wu, k.ffn1_wd)
    if "mix" in stages:
        if not getattr(k, "bias_built", False):
            P.tag = "setup"
            for _ in build_bias_tiles(k):
                pass
            k.bias_built = True
        P.tag = "mixproj"
        mixer_proj(k, g)
        if "dsa" in stages:
            P.tag = "dsa"
            dsa(k, g)
            tap(k, "o_sa", k.o_sa.all(), [128, NT, 1024], BF16)
        P.tag = "dn"
        for hq in range(NH // NQ):
            if "dn0" not in stages:
                dn_quad(k, g, hq)
        tap(k, "o_dn", k.o_dn.all(), [128, NT, 1024], BF16)
        P.tag = "wo"
        if "wo" in stages:
            merge_wo(k, g)
    P.tag = "ffn2"
    if "ffn2" in stages:
        ffn(k, k.gcol["ffn2"], k.ffn2_wg, k.ffn2_wu, k.ffn2_wd)
    P.tag = "final"
    final_norm_store(k, g, "final" in stages)


ALL_STAGES = ("ffn1", "mix", "dsa", "wo", "ffn2", "final")

INPUT_ORDER = ["x", "ffn1_norm", "ffn1_wg", "ffn1_wu", "ffn1_wd", "mix_norm", "w_in", "conv_w", "a_log",
               "dt_bias", "dn_out_norm", "q_norm", "kv_norm", "w_uq", "w_uk", "w_uv", "w_iq", "idx_k_g",
               "idx_k_b", "rel_bias", "w_o", "ffn2_norm", "ffn2_wg", "ffn2_wu", "ffn2_wd", "final_norm"]


def make_in_maps(inputs, ncores=8):
    shared = {}
    for n in INPUT_ORDER:
        if n == "x":
            continue
        a = np.asarray(inputs[n], dtype=np.float32)
        if n == "final_norm":
            a = a.reshape(1, D_MODEL)
        elif n == "rel_bias":
            a = a.reshape(1, REL_BUCKETS * NH)
        elif n in ("w_uk", "w_uv"):
            a = a.reshape(KV_RANK, 1024)
        elif n == "conv_w":
            a = a.reshape(CONV_K, 3072)
        else:
            a = a.reshape(a.shape[1:]) if a.shape[0] == 1 and a.ndim == 3 else a
        shared[n] = np.ascontiguousarray(a)
    x = np.asarray(inputs["x"], dtype=np.float32)
    maps = []
    for c in range(ncores):
        m = dict(shared)
        m["x"] = np.ascontiguousarray(x[c])
        maps.append(m)
    return maps


_CACHE = {}


def kernel(**inputs):
    if "k" not in _CACHE:
        _CACHE["k"] = build(ngroups=8, stages=ALL_STAGES)
    k = _CACHE["k"]
    in_maps = make_in_maps(inputs, 8)
    res = run_bass_kernel_spmd(k.nc, in_maps, core_ids=list(range(8)))
    return np.stack([np.asarray(r["y"]) for r in res.results], axis=0).astype(np.float32)
```
